# Optimizing a Trainium2 kernel written in Bass

```python
import math
import jax, jax.numpy as jnp
from jax import lax
import numpy as np

D_MODEL = 1024
BATCH = 1
SEQ = 16384
DEPTH = 2

ATT_HEADS = 8
ATT_KV_HEADS = 2
ATT_HEAD_DIM = 64
ATT_BLOCK = 128
ATT_WINDOW = 128
CONV_WIDTH = 512
CONV_KSIZE = 3
GLA_HEADS = 4
GLA_KEY_DIM = 256
GLA_VALUE_DIM = 512
GLA_GATE_RANK = 16
GLA_TAU = 16.0
GLA_CHUNK = 16
N_BRANCHES = 3
N_EXPERTS = 16
EXPERT_FF = 2048
CAPACITY_FACTOR = 2
DN_ALPHA = (2 * DEPTH) ** 0.25
DN_BETA = (8 * DEPTH) ** -0.25
LN_EPS = 1e-5
RMS_EPS = 1e-6

SPLIT_SIZES = (
    ATT_HEADS * ATT_HEAD_DIM,
    ATT_KV_HEADS * ATT_HEAD_DIM,
    ATT_KV_HEADS * ATT_HEAD_DIM,
    CONV_WIDTH,
    CONV_WIDTH,
    CONV_WIDTH,
    GLA_KEY_DIM,
    GLA_KEY_DIM,
    GLA_VALUE_DIM,
    GLA_VALUE_DIM,
    2 * GLA_GATE_RANK,
    N_BRANCHES * D_MODEL,
)
N_IN = int(sum(SPLIT_SIZES))
SPLIT_POINTS = [int(v) for v in np.cumsum(SPLIT_SIZES)[:-1]]

kernel_name = "hybrid_swa_conv_gla_ec_moe_deepnorm"


def layer_norm(x, g, b):
    xf = x.astype(jnp.float32)
    mu = jnp.mean(xf, axis=-1, keepdims=True)
    var = jnp.mean(jnp.square(xf - mu), axis=-1, keepdims=True)
    return ((xf - mu) * lax.rsqrt(var + LN_EPS) * g.astype(jnp.float32) + b.astype(jnp.float32)).astype(x.dtype)


def window_attention(q, k, v, sink):
    B, S = q.shape[0], q.shape[1]
    nb = S // ATT_BLOCK
    G = ATT_HEADS // ATT_KV_HEADS
    qb = q.reshape(B, nb, ATT_BLOCK, ATT_KV_HEADS, G, ATT_HEAD_DIM)

    def neighbourhood(t):
        tp = jnp.pad(t, ((0, 0), (ATT_BLOCK, ATT_BLOCK), (0, 0), (0, 0)))
        tp = tp.reshape(B, nb + 2, ATT_BLOCK, ATT_KV_HEADS, ATT_HEAD_DIM)
        return jnp.concatenate([tp[:, :-2], tp[:, 1:-1], tp[:, 2:]], axis=2)

    kw, vw = neighbourhood(k), neighbourhood(v)
    s = jnp.einsum('bnqhgd,bnkhd->bnhgqk', qb, kw).astype(jnp.float32) * (ATT_HEAD_DIM ** -0.5)
    q_off = jnp.arange(ATT_BLOCK)
    k_off = jnp.arange(3 * ATT_BLOCK) - ATT_BLOCK
    dist = jnp.abs(q_off[:, None] - k_off[None, :]).astype(jnp.float32)
    k_abs = jnp.arange(nb)[:, None] * ATT_BLOCK + k_off[None, :]
    valid = (dist <= ATT_WINDOW)[None] & ((k_abs >= 0) & (k_abs < S))[:, None, :]
    slopes = (2.0 ** (-8.0 * jnp.arange(1, ATT_HEADS + 1, dtype=jnp.float32) / ATT_HEADS)).reshape(ATT_KV_HEADS, G)
    s = s - slopes[:, :, None, None] * dist
    s = jnp.where(valid[None, :, None, None], s, -jnp.inf)
    sink_l = sink.astype(jnp.float32).reshape(ATT_KV_HEADS, G)[:, :, None, None]
    m = jnp.maximum(jnp.max(s, axis=-1, keepdims=True), sink_l)
    p = jnp.exp(s - m)
    p = p / (jnp.sum(p, axis=-1, keepdims=True) + jnp.exp(sink_l - m))
    o = jnp.einsum('bnhgqk,bnkhd->bnqhgd', p.astype(v.dtype), vw)
    return o.reshape(B, S, ATT_HEADS * ATT_HEAD_DIM)


def short_conv(h, b_gate, c_gate, w):
    u = c_gate * h
    y = lax.conv_general_dilated(
        u, w[:, None, :].astype(u.dtype), window_strides=(1,), padding=[(CONV_KSIZE // 2, CONV_KSIZE // 2)],
        dimension_numbers=('NWC', 'WIO', 'NWC'), feature_group_count=CONV_WIDTH)
    return b_gate * y


def gla_direction(q, k, v, log_a):
    f32 = jnp.float32
    B, S, H, dk = q.shape
    dv = v.shape[-1]
    C = GLA_CHUNK
    nc = S // C
    qc = q.astype(f32).reshape(B, nc, C, H, dk)
    kc = k.astype(f32).reshape(B, nc, C, H, dk)
    vc = v.astype(f32).reshape(B, nc, C, H, dv)
    b = jnp.cumsum(log_a.astype(f32).reshape(B, nc, C, H, dk), axis=2)
    lower = jnp.tril(jnp.ones((C, C), dtype=bool))[:, :, None, None]
    diff = b[:, :, :, None] - b[:, :, None, :]
    decay = jnp.exp(jnp.where(lower, diff, -jnp.inf))
    scores = jnp.einsum('bnthd,bnshd,bntshd->bnhts', qc, kc, decay)
    o_intra = jnp.einsum('bnhts,bnshv->bnthv', scores, vc)
    b_last = b[:, :, -1]
    kv = jnp.einsum('bnshd,bnshv->bnhdv', kc * jnp.exp(b_last[:, :, None] - b), vc)
    a_chunk = jnp.exp(b_last)

    def step(state, inp):
        a_n, kv_n = inp
        return a_n[..., None] * state + kv_n, state

    _, states = lax.scan(step, jnp.zeros((B, H, dk, dv), f32),
                         (jnp.moveaxis(a_chunk, 1, 0), jnp.moveaxis(kv, 1, 0)))
    states = jnp.moveaxis(states, 0, 1)
    o_inter = jnp.einsum('bnthd,bnhdv->bnthv', qc * jnp.exp(b), states)
    return (o_intra + o_inter).reshape(B, S, H, dv)


def gla_mixer(q, k, v, r, lr, w2, bias, norm_g):
    B, S = q.shape[0], q.shape[1]
    dk = GLA_KEY_DIM // GLA_HEADS
    dv = GLA_VALUE_DIM // GLA_HEADS
    qh = q.reshape(B, S, GLA_HEADS, dk) * (dk ** -0.5)
    kh = k.reshape(B, S, GLA_HEADS, dk)
    vh = v.reshape(B, S, GLA_HEADS, dv)
    z = jnp.einsum('bsir,irk->bsik', lr.reshape(B, S, 2, GLA_GATE_RANK), w2) + bias
    log_a = (jax.nn.log_sigmoid(z.astype(jnp.float32)) / GLA_TAU).reshape(B, S, 2, GLA_HEADS, dk)
    flip = lambda t: jnp.flip(t, axis=1)
    o_f = gla_direction(qh, kh, vh, log_a[:, :, 0])
    o_b = flip(gla_direction(flip(qh), flip(kh), flip(vh), flip(log_a[:, :, 1])))
    o = o_f + o_b
    o = o * lax.rsqrt(jnp.mean(o * o, axis=-1, keepdims=True) + RMS_EPS)
    o = o.reshape(B, S, GLA_VALUE_DIM) * norm_g.astype(jnp.float32)
    return (o * jax.nn.silu(r.astype(jnp.float32))).astype(r.dtype)


def expert_choice_ffn(x, router_w, w_gate, w_up, w_down):
    B, S, D = x.shape
    cap = CAPACITY_FACTOR * S // N_EXPERTS
    aff = jax.nn.softmax(jnp.einsum('bsd,de->bse', x, router_w).astype(jnp.float32), axis=-1)
    gate, idx = lax.top_k(jnp.swapaxes(aff, 1, 2), cap)
    xs = jax.vmap(lambda xb, ib: xb[ib])(x, idx)
    h = jax.nn.silu(jnp.einsum('becd,edf->becf', xs, w_gate)) * jnp.einsum('becd,edf->becf', xs, w_up)
    y = jnp.einsum('becf,efd->becd', h, w_down) * gate[..., None].astype(x.dtype)
    return jax.vmap(lambda ib, yb: jnp.zeros((S, D), yb.dtype).at[ib.reshape(-1)].add(yb.reshape(-1, D)))(idx, y)


def setup_inputs(seed: int = 0) -> dict:
    key = jax.random.key(seed)
    ks = jax.random.split(key, 20)
    L, D = DEPTH, D_MODEL
    nrm = lambda k, shape, scale: jax.random.normal(k, shape, jnp.float32) * scale
    return {
        "x": nrm(ks[0], (BATCH, SEQ, D), 1.0),
        "w_in": nrm(ks[1], (L, D, N_IN), D ** -0.5),
        "attn_sink": nrm(ks[2], (L, ATT_HEADS), 0.5),
        "conv_w": nrm(ks[3], (L, CONV_KSIZE, CONV_WIDTH), CONV_KSIZE ** -0.5),
        "gla_gate_w2": nrm(ks[4], (L, 2, GLA_GATE_RANK, GLA_KEY_DIM), GLA_GATE_RANK ** -0.5),
        "gla_gate_b": nrm(ks[5], (L, 2, GLA_KEY_DIM), 0.1),
        "gla_norm_g": 1.0 + nrm(ks[6], (L, GLA_VALUE_DIM), 0.02),
        "w_branch_attn": nrm(ks[7], (L, ATT_HEADS * ATT_HEAD_DIM, D), DN_BETA * (ATT_HEADS * ATT_HEAD_DIM) ** -0.5),
        "w_branch_conv": nrm(ks[8], (L, CONV_WIDTH, D), DN_BETA * CONV_WIDTH ** -0.5),
        "w_branch_gla": nrm(ks[9], (L, GLA_VALUE_DIM, D), DN_BETA * GLA_VALUE_DIM ** -0.5),
        "w_out": nrm(ks[10], (L, D, D), DN_BETA * D ** -0.5),
        "ln_mix_g": 1.0 + nrm(ks[11], (L, D), 0.02),
        "ln_mix_b": nrm(ks[12], (L, D), 0.02),
        "router_w": nrm(ks[13], (L, D, N_EXPERTS), D ** -0.5),
        "expert_w_gate": nrm(ks[14], (L, N_EXPERTS, D, EXPERT_FF), D ** -0.5),
        "expert_w_up": nrm(ks[15], (L, N_EXPERTS, D, EXPERT_FF), D ** -0.5),
        "expert_w_down": nrm(ks[16], (L, N_EXPERTS, EXPERT_FF, D), DN_BETA * EXPERT_FF ** -0.5),
        "ln_ffn_g": 1.0 + nrm(ks[17], (L, D), 0.02),
        "ln_ffn_b": nrm(ks[18], (L, D), 0.02),
    }


def reference(x, w_in, attn_sink, conv_w, gla_gate_w2, gla_gate_b, gla_norm_g,
              w_branch_attn, w_branch_conv, w_branch_gla, w_out, ln_mix_g, ln_mix_b,
              router_w, expert_w_gate, expert_w_up, expert_w_down, ln_ffn_g, ln_ffn_b):
    B, S, _ = x.shape
    for l in range(DEPTH):
        proj = jnp.einsum('bsd,dn->bsn', x, w_in[l])
        (aq, ak, av, ch, cb, cc, gq, gk, gv, gr, glr, mg) = jnp.split(proj, SPLIT_POINTS, axis=-1)
        y_attn = window_attention(aq.reshape(B, S, ATT_HEADS, ATT_HEAD_DIM),
                                  ak.reshape(B, S, ATT_KV_HEADS, ATT_HEAD_DIM),
                                  av.reshape(B, S, ATT_KV_HEADS, ATT_HEAD_DIM), attn_sink[l])
        y_conv = short_conv(ch, cb, cc, conv_w[l])
        y_gla = gla_mixer(gq, gk, gv, gr, glr, gla_gate_w2[l], gla_gate_b[l], gla_norm_g[l])
        g = jax.nn.sigmoid(mg.reshape(B, S, N_BRANCHES, D_MODEL))
        merged = (g[:, :, 0] * (y_attn @ w_branch_attn[l])
                  + g[:, :, 1] * (y_conv @ w_branch_conv[l])
                  + g[:, :, 2] * (y_gla @ w_branch_gla[l]))
        x = layer_norm(DN_ALPHA * x + merged @ w_out[l], ln_mix_g[l], ln_mix_b[l])
        ffn = expert_choice_ffn(x, router_w[l], expert_w_gate[l], expert_w_up[l], expert_w_down[l])
        x = layer_norm(DN_ALPHA * x + ffn, ln_ffn_g[l], ln_ffn_b[l])
    return x
```

```python
import numpy as np
import concourse.bass as bass
import concourse.mybir as mybir
from concourse.bass_utils import run_bass_kernel_spmd

F32 = mybir.dt.float32
BF16 = mybir.dt.bfloat16
I32 = mybir.dt.int32
ALU = mybir.AluOpType
AF = mybir.ActivationFunctionType
AX = mybir.AxisListType

ENGS = ("pe", "act", "dve", "pool", "sp")
NDSEM = 48
NHW = 32


class Prog:
    def __init__(self, nc):
        self.nc = nc
        self.ops = {e: [] for e in ENGS}
        self.sem = {e: nc.alloc_semaphore("prog_" + e) for e in ("pe", "act", "dve", "pool")}
        self.dsems = [nc.alloc_semaphore("dsem%d" % i) for i in range(NDSEM)]
        self.dcount = [0] * NDSEM
        self.dnext = {"hw": 0, "sw": 0}
        self.last_w = {}
        self.readers = {}
        self.allops = []
        self.out_dma_ops = []

    def _add(self, eng, fn, reads, writes, dma=False):
        op = dict(eng=eng, fn=fn, deps=[], signal=False, dma=dma, idx=len(self.allops))
        writes = list(writes) + [k for k in reads if k.startswith("ps") and k not in writes]
        reads = [k for k in reads if not k.startswith("ps")]
        deps = []
        for k in reads:
            w = self.last_w.get(k)
            if w is not None:
                deps.append(w)
        for k in writes:
            w = self.last_w.get(k)
            if w is not None:
                deps.append(w)
            for r in self.readers.get(k, ()):
                deps.append(r)
        if dma:
            if eng == "pool":
                i = NHW + self.dnext["sw"]
                self.dnext["sw"] = (self.dnext["sw"] + 1) % (NDSEM - NHW)
            else:
                i = self.dnext["hw"]
                self.dnext["hw"] = (self.dnext["hw"] + 1) % NHW
            prev = getattr(self, "_dprev", {}).get(i)
            if prev is not None:
                deps.append(prev)
            if not hasattr(self, "_dprev"):
                self._dprev = {}
            self._dprev[i] = op
            self.dcount[i] += 1
            op["dsem"] = i
            op["dval"] = 16 * self.dcount[i]
            op["signal"] = True
        seen = set()
        for d in deps:
            if d is op or id(d) in seen:
                continue
            seen.add(id(d))
            if d["eng"] == "pe" and eng == "pe" and not d["dma"] and not dma:
                continue
            op["deps"].append(d)
            d["signal"] = True
        for k in reads:
            self.readers.setdefault(k, []).append(op)
        for k in writes:
            self.last_w[k] = op
            self.readers[k] = []
        self.ops[eng].append(op)
        self.allops.append(op)
        return op

    def op(self, eng, fn, reads=(), writes=()):
        return self._add(eng, fn, list(reads), list(writes))

    def dma(self, out, in_, reads=(), writes=(), q="sp", **kw):
        return self._add(q, lambda e: e.dma_start(out=out, in_=in_, **kw), list(reads), list(writes), dma=True)

    def dma_fn(self, fn, reads=(), writes=(), q="pool"):
        return self._add(q, fn, list(reads), list(writes), dma=True)

    def mm(self, out, lhsT, rhs, start=True, stop=True, reads=(), writes=()):
        return self.op("pe", lambda t: t.matmul(out, lhsT, rhs, start=start, stop=stop), reads, writes)

    def tr(self, out, in_, ident, reads=(), writes=()):
        return self.op("pe", lambda t: t.transpose(out, in_, ident), reads, writes)

    def act(self, out, in_, func, reads=(), writes=(), **kw):
        return self.op("act", lambda a: a.activation(out=out, in_=in_, func=func, **kw), reads, writes)

    def copy(self, eng, out, in_, reads=(), writes=()):
        if eng == "act":
            return self.op("act", lambda a: a.copy(out, in_), reads, writes)
        return self.op(eng, lambda v: v.tensor_copy(out, in_), reads, writes)

    def tt(self, eng, out, in0, in1, op, reads=(), writes=()):
        return self.op(eng, lambda v: v.tensor_tensor(out, in0, in1, op), reads, writes)

    def stt(self, eng, out, in0, scalar, in1, op0, op1, reads=(), writes=()):
        return self.op(eng, lambda v: v.scalar_tensor_tensor(out, in0, scalar, in1, op0, op1), reads, writes)

    def ts(self, eng, out, in0, s1, s2, op0, op1=None, reads=(), writes=(), **kw):
        if op1 is None:
            return self.op(eng, lambda v: v.tensor_scalar(out, in0, s1, s2, op0, **kw), reads, writes)
        return self.op(eng, lambda v: v.tensor_scalar(out, in0, s1, s2, op0, op1, **kw), reads, writes)

    def memset(self, eng, ap, val, reads=(), writes=()):
        return self.op(eng, lambda v: v.memset(ap, val), reads, writes)

    def emit(self, final_wait_ops=()):
        cnt = {e: 0 for e in self.sem}
        for op in self.allops:
            if op["dma"]:
                op["ticket"] = (self.dsems[op["dsem"]], op["dval"])
            elif op["signal"]:
                cnt[op["eng"]] += 1
                op["ticket"] = (self.sem[op["eng"]], cnt[op["eng"]])
        nc = self.nc
        final_tickets = [op["ticket"] for op in final_wait_ops]

        def run(engname, engobj):
            waited = {}
            for op in self.ops[engname]:
                need = {}
                for d in op["deps"]:
                    s, v = d["ticket"]
                    if need.get(s, (None, 0))[1] < v:
                        need[s] = (s, v)
                for s, v in need.values():
                    if waited.get(s, 0) >= v:
                        continue
                    engobj.wait_ge(s, v)
                    waited[s] = v
                ins = op["fn"](engobj)
                if op["signal"]:
                    s, v = op["ticket"]
                    ins.then_inc(s, 16 if op["dma"] else 1)
            if engname == "sp":
                for s, v in final_tickets:
                    if waited.get(s, 0) < v:
                        engobj.wait_ge(s, v)
                        waited[s] = v

        with nc.Block() as block:
            @block.sync
            def _(e):
                run("sp", e)

            @block.tensor
            def _(e):
                run("pe", e)

            @block.scalar
            def _(e):
                run("act", e)

            @block.vector
            def _(e):
                run("dve", e)

            @block.gpsimd
            def _(e):
                run("pool", e)


S_ = 16384
BLK = 1024
NBLK = S_ // BLK


def build_G(nblk=NBLK, ssalloc=None):
    nc = bass.Bass("TRN2", target_bir_lowering=False)
    SS = ssalloc or nblk * BLK
    xT = nc.dram_tensor("xT", [1024, SS], F32, kind="ExternalInput").ap()
    W = nc.dram_tensor("W", [1024, 272], F32, kind="ExternalInput").ap()
    W2 = nc.dram_tensor("W2", [32, 64], F32, kind="ExternalInput").ap()
    U16d = nc.dram_tensor("U16", [128, 128], F32, kind="ExternalInput").ap()
    UL16d = nc.dram_tensor("UL16", [128, 128], F32, kind="ExternalInput").ap()
    UMd = nc.dram_tensor("UM", [128, 128], F32, kind="ExternalInput").ap()
    oT = nc.dram_tensor("oT", [128, SS], F32, kind="ExternalOutput").ap()
    P = Prog(nc)
    sb = lambda n, s, d: nc.alloc_sbuf_tensor(n, s, d)
    Wb = sb("Wb", [128, 8, 272], BF16)
    W2s = sb("W2s", [32, 64], F32)
    U16 = sb("U16s", [128, 128], F32)
    UL16 = sb("UL16s", [128, 128], F32)
    UM = sb("UMs", [128, 128], F32)
    xb = [sb("xb%d" % i, [128, 8, BLK], BF16) for i in range(2)]
    xf = [sb("xf%d" % i, [128, 8, BLK], F32) for i in range(2)]
    qT = [sb("qT%d" % i, [64, BLK], F32) for i in range(2)]
    kT = [sb("kT%d" % i, [64, BLK], F32) for i in range(2)]
    lrT = [sb("lrT%d" % i, [32, BLK], F32) for i in range(2)]
    ob = [sb("ob%d" % i, [128, BLK], F32) for i in range(2)]
    e1 = [sb("e1_%d" % i, [128, 64], F32) for i in range(2)]
    sp = [sb("sp_%d" % i, [128, 64], F32) for i in range(2)]
    eq = [sb("eq_%d" % i, [64, 128], F32) for i in range(2)]
    ek = [sb("ek_%d" % i, [64, 128], F32) for i in range(2)]
    ekh = [sb("ekh_%d" % i, [128, 64], F32) for i in range(2)]
    QT = [sb("QT_%d" % i, [64, 128], BF16) for i in range(2)]
    KT = [sb("KT_%d" % i, [64, 128], BF16) for i in range(2)]
    Kh = [sb("Kh_%d" % i, [128, 64], BF16) for i in range(2)]
    V = [sb("V_%d" % i, [128, 128], BF16) for i in range(2)]
    ATm = [sb("ATm_%d" % i, [128, 128], BF16) for i in range(2)]
    Sf = sb("Sf", [64, 128], F32)
    Sb = sb("Sb", [64, 128], BF16)
    ps = [nc.alloc_psum_tensor("ps%d" % i, [128, 512], F32) for i in range(8)]

    P.dma(Wb[:], W.rearrange("(kc p) n -> p kc n", p=128), writes=["Wb"], q="pool")
    P.dma(W2s[:], W2, writes=["W2s"])
    P.dma(U16[:], U16d, writes=["U16"])
    P.dma(UL16[:], UL16d, writes=["UL16"])
    P.dma(UM[:], UMd, writes=["UM"])
    P.memset("dve", Sf[:], 0.0, writes=["Sf"])
    P.memset("dve", Sb[:], 0.0, writes=["Sb"])
    for i in range(2):
        P.memset("pool", lrT[i][:], 1.0, writes=["lrT%d" % i])
    xTv = xT.rearrange("(kc p) t -> p kc t", p=128)
    outs = []

    def load_blk(b):
        i = b % 2
        P.dma(xf[i][:], xTv[:, :, b * BLK:(b + 1) * BLK], writes=["xf%d" % i])
        for kc in range(8):
            P.copy("pool" if kc % 2 else "act", xb[i][:, kc, :], xf[i][:, kc, :], reads=["xf%d" % i], writes=["xb%d" % i])

    load_blk(0)
    for b in range(nblk):
        i = b % 2
        if b + 1 < nblk:
            load_blk(b + 1)
        xk = "xb%d" % i
        for sbk in range(BLK // 512):
            cs = slice(sbk * 512, (sbk + 1) * 512)
            for (name, c0, c1, pst, dst, m) in (("q", 0, 64, ps[0], qT[i], 64), ("k", 64, 128, ps[1], kT[i], 64),
                                                 ("lr", 256, 272, ps[2], lrT[i], 16)):
                for kc in range(8):
                    P.mm(pst[0:m, :], Wb[:, kc, c0:c1], xb[i][:, kc, cs], start=(kc == 0), stop=(kc == 7),
                         reads=["Wb", xk], writes=["ps_" + name])
                P.copy("act", dst[0:m, cs], pst[0:m, :], reads=["ps_" + name], writes=["%sT%d" % (name, i)])
        for c in range(BLK // 128):
            n = b * (BLK // 128) + c
            j = n % 2
            cols = slice(c * 128, (c + 1) * 128)
            pA, pC = ps[3 + j], ps[5 + j]
            kA, kB = "psA%d" % j, "psB%d" % j
            J = str(j)
            for kc in range(8):
                P.mm(pA[:, 0:192], xb[i][:, kc, cols], Wb[:, kc, 64:256], start=(kc == 0), stop=(kc == 7),
                     reads=["Wb", xk], writes=[kA])
            P.mm(pA[:, 192:256], lrT[i][:, cols], W2s[:], reads=["lrT%d" % i, "W2s"], writes=[kA])
            P.act(e1[j][:], pA[:, 192:256], AF.Exp, scale=-1.0, reads=[kA], writes=["e1" + J])
            P.act(sp[j][:], e1[j][:], AF.Ln, bias=1.0, reads=["e1" + J], writes=["sp" + J])
            P.mm(pA[:, 320:384], UL16[:], sp[j][:], reads=["UL16", "sp" + J], writes=[kA])
            P.mm(pA[0:64, 384:512], sp[j][:], U16[:], reads=["U16", "sp" + J], writes=[kA])
            P.act(eq[j][:], pA[0:64, 384:512], AF.Exp, scale=-1.0, reads=[kA], writes=["eq" + J])
            P.act(ek[j][:], pA[0:64, 384:512], AF.Exp, scale=1.0, reads=[kA], writes=["ek" + J])
            P.act(ekh[j][:], pA[:, 320:384], AF.Exp, scale=1.0, reads=[kA], writes=["ekh" + J])
            P.stt("dve", QT[j][:], qT[i][:, cols], 0.125, eq[j][:], ALU.mult, ALU.mult,
                  reads=["qT%d" % i, "eq" + J], writes=["QT" + J])
            P.tt("dve", KT[j][:], kT[i][:, cols], ek[j][:], ALU.mult, reads=["kT%d" % i, "ek" + J], writes=["KT" + J])
            P.tt("dve", Kh[j][:], pA[:, 0:64], ekh[j][:], ALU.mult, reads=[kA, "ekh" + J], writes=["Kh" + J])
            P.copy("dve", V[j][:], pA[:, 64:192], reads=[kA], writes=["V" + J])
            P.mm(pC[:, 0:128], KT[j][:], QT[j][:], reads=["KT" + J, "QT" + J], writes=[kB])
            P.tt("dve", ATm[j][:], pC[:, 0:128], UM[:], ALU.mult, reads=[kB, "UM"], writes=["ATm" + J])
            P.mm(pC[0:64, 128:256], Kh[j][:], V[j][:], reads=["Kh" + J, "V" + J], writes=[kB])
            P.mm(pC[:, 256:384], V[j][:], ATm[j][:], start=True, stop=False, reads=["V" + J, "ATm" + J], writes=[kB])
            P.mm(pC[:, 256:384], Sb[:], QT[j][:], start=False, stop=True, reads=["Sb", "QT" + J], writes=[kB])
            P.copy("act", ob[i][:, cols], pC[:, 256:384], reads=[kB], writes=["ob%d" % i])
            P.stt("dve", Sf[:], Sf[:], eq[j][:, 127:128], pC[0:64, 128:256], ALU.mult, ALU.add,
                  reads=["Sf", "eq" + J, kB], writes=["Sf"])
            P.copy("pool", Sb[:], Sf[:], reads=["Sf"], writes=["Sb"])
        outs.append(P.dma(oT[:, b * BLK:(b + 1) * BLK], ob[i][:], reads=["ob%d" % i], writes=["oT"]))
    P.emit(final_wait_ops=outs)
    return nc


def g_consts():
    s = np.arange(128)[:, None]
    t = np.arange(128)[None, :]
    U = (s <= t).astype(np.float32)
    return dict(U16=U / 16.0, UL16=-(s > t).astype(np.float32) / 16.0, UM=U)


def g_inputs(xT_full, w_in_l, w2_l, b_l, h, d, flipped=False):
    xT = xT_full[:, ::-1] if (d == 1 and not flipped) else xT_full
    cq = 2304 + 64 * h
    ck = 2560 + 64 * h
    cv = 2816 + 128 * h
    clr = 3840 + 16 * d
    W = np.concatenate([w_in_l[:, cq:cq + 64], w_in_l[:, ck:ck + 64], w_in_l[:, cv:cv + 128], w_in_l[:, clr:clr + 16]], axis=1)
    W2 = np.zeros((32, 64), np.float32)
    W2[0:16] = w2_l[d][:, 64 * h:64 * h + 64]
    W2[16] = b_l[d][64 * h:64 * h + 64]
    r = dict(xT=np.ascontiguousarray(xT), W=np.ascontiguousarray(W), W2=W2)
    r.update(g_consts())
    return r


NT = 2048
TB = 256
XW = TB + 256
TPB = TB // 128
NB_A = NT // TB
ALPHA = 4.0 ** 0.25
SLOPES = [2.0 ** (-(h + 1)) for h in range(8)]
C_AQ, C_AK, C_AV, C_CH, C_CB, C_CC, C_GR = 0, 512, 640, 768, 1280, 1792, 2304
NWA = 2816


def layer_norm(P, t, tk, bst, mv, rstd, lng, lnb, gk, bk, epsln):
    for hf in range(2):
        P.op("dve", lambda v, o=bst[:, hf, :], a=t[:, 512 * hf:512 * hf + 512]: v.bn_stats(o, a), reads=[tk], writes=["bst"])
    P.op("dve", lambda v, o=mv[:], a=bst[:]: v.bn_aggr(o, a), reads=["bst"], writes=["mv"])
    P.act(rstd[:], mv[:, 1:2], AF.Sqrt, bias=epsln[:, 0:1], reads=["mv", "epsln"], writes=["rstd"])
    P.op("dve", lambda v, o=rstd[:]: v.reciprocal(o, o), reads=["rstd"], writes=["rstd"])
    P.ts("dve", t[:], t[:], mv[:, 0:1], rstd[:, 0:1], ALU.subtract, ALU.mult, reads=[tk, "mv", "rstd"], writes=[tk])
    P.tt("dve", t[:], t[:], lng[:], ALU.mult, reads=[tk, gk], writes=[tk])
    P.tt("pool", t[:], t[:], lnb[:], ALU.add, reads=[tk, bk], writes=[tk])


def build_A(nblocks=NB_A):
    nc = bass.Bass("TRN2", target_bir_lowering=False)
    din = lambda n, s, d=F32: nc.dram_tensor(n, s, d, kind="ExternalInput").ap()
    xTh = din("xTh", [1024, NT + 256])
    xtok = din("xtok", [NT, 1024])
    oTf = din("oTf", [512, NT])
    oTb = din("oTb", [512, NT])
    Wa_d = din("Wa", [1024, NWA])
    Wmg_d = din("Wmg", [1024, 3072])
    Wba_d = din("Wba", [512, 1024])
    Wbc_d = din("Wbc", [512, 1024])
    Wbg_d = din("Wbg", [512, 1024])
    Wo_d = din("Wo", [1024, 1024])
    Wr_d = din("Wr", [1024, 16])
    maskb_d = din("maskb", [3, 128, 384])
    dist_d = din("dist", [128, 384])
    sink_d = din("sink", [128, 8])
    convw_d = din("convw", [128, 4, 3])
    normg_d = din("normg", [128, 4])
    lng_d = din("lng", [128, 1024])
    lnb_d = din("lnb", [128, 1024])
    ident_d = din("ident", [128, 128])
    x1_d = nc.dram_tensor("x1", [NT, 1024], F32, kind="ExternalOutput").ap()
    aff_d = nc.dram_tensor("aff", [NT, 16], F32, kind="ExternalOutput").ap()

    P = Prog(nc)
    sb = lambda n, s, d: nc.alloc_sbuf_tensor(n, s, d)
    stage = sb("stage", [128, 8, 512], F32)
    Wa = sb("Wa_s", [128, 8, NWA], BF16)
    Wmg = [sb("Wmg%d" % i, [128, 8, 512], BF16) for i in range(2)]
    Wba = sb("Wba_s", [128, 4, 1024], BF16)
    Wbc = sb("Wbc_s", [128, 4, 1024], BF16)
    Wbg = sb("Wbg_s", [128, 4, 1024], BF16)
    Wo = sb("Wo_s", [128, 8, 1024], BF16)
    Wr = sb("Wr_s", [128, 8, 16], F32)
    maskb = sb("maskb_s", [128, 3, 384], F32)
    dist = sb("dist_s", [128, 384], F32)
    sink = sb("sink_s", [128, 8], F32)
    convw = sb("convw_s", [128, 4, 3], F32)
    normg = sb("normg_s", [128, 4], F32)
    lng = sb("lng_s", [128, 1024], F32)
    lnb = sb("lnb_s", [128, 1024], F32)
    ident = sb("ident_s", [128, 128], F32)
    ones = sb("ones_s", [128, 128], F32)
    epst = sb("eps_s", [128, 1], F32)
    xb = sb("xb_s", [128, 8, XW], BF16)
    kT = sb("kT_s", [64, 2, XW], BF16)
    V = sb("V_s", [128, (TPB + 2) * 128], BF16)
    qT = [sb("qT%d" % i, [64, TB], BF16) for i in range(2)]
    s1 = [sb("s1_%d" % i, [128, 384], F32) for i in range(2)]
    Pn = [sb("Pn_%d" % i, [128, 384], BF16) for i in range(2)]
    Dg = [sb("Dg_%d" % i, [128, 128], BF16) for i in range(2)]
    PT = [sb("PT_%d" % i, [128, 384], BF16) for i in range(2)]
    st = [sb("st_%d" % i, [128, 8], F32) for i in range(2)]
    yaT = sb("yaT", [128, 4, TB], BF16)
    ycT = sb("ycT", [128, 4, TB], BF16)
    ygT = sb("ygT", [128, 4, TB], BF16)
    mT = sb("mT", [128, 8, TB], BF16)
    cc_s = sb("cc_s", [128, TB + 2], F32)
    u_s = sb("u_s", [128, TB + 2], F32)
    t_s = sb("t_s", [128, TB], F32)
    of_s = [sb("of_%d" % i, [128, TB], F32) for i in range(2)]
    ob_s = [sb("ob_%d" % i, [128, TB], F32) for i in range(2)]
    sq_s = sb("sq_s", [128, TB], F32)
    rinv = sb("rinv_s", [128, TB], F32)
    silr = sb("silr_s", [128, TB], F32)
    g_s = [sb("g_%d" % i, [128, TB], F32) for i in range(2)]
    accm = [sb("accm%d" % i, [128, TB], F32) for i in range(4)]
    tmp = sb("tmp_s", [128, TB], F32)
    xt = [sb("xt_%d" % i, [128, 1024], F32) for i in range(2)]
    bst = sb("bst", [128, 2, 6], F32)
    mv = sb("mv", [128, 2], F32)
    rstd = sb("rstd", [128, 1], F32)
    x1T = sb("x1T", [128, 1024], F32)
    lg = sb("lg", [128, 16], F32)
    lst = sb("lst", [128, 4], F32)
    affs = [sb("affs%d" % i, [128, 16], F32) for i in range(2)]
    psb = [nc.alloc_psum_tensor("psb%d" % i, [128, 512], F32) for i in range(8)]
    bank = [0]
    ceng = [0]

    def nb():
        i = bank[0]
        bank[0] = (i + 1) % 8
        return psb[i], "psb%d" % i

    def cast_eng():
        ceng[0] += 1
        return "pool" if ceng[0] % 2 else "act"

    def load_w(dst, dkey, src, nk, ncols):
        sv = src.rearrange("(kc p) n -> p kc n", p=128)
        for c0 in range(0, ncols, 512):
            w = min(512, ncols - c0)
            P.dma(stage[:, 0:nk, 0:w], sv[:, :, c0:c0 + w], writes=["stage"])
            for kc in range(nk):
                P.copy(cast_eng(), dst[:, kc, c0:c0 + w], stage[:, kc, 0:w], reads=["stage"], writes=[dkey])

    P.dma(Wr[:], Wr_d.rearrange("(kc p) n -> p kc n", p=128), writes=["Wr"])
    P.dma(maskb[:], maskb_d.rearrange("m p k -> p m k"), writes=["maskb"])
    for (t, d, k) in ((dist, dist_d, "dist"), (sink, sink_d, "sink"), (convw, convw_d, "convw"), (normg, normg_d, "normg"),
                      (lng, lng_d, "lng"), (lnb, lnb_d, "lnb"), (ident, ident_d, "ident")):
        P.dma(t[:], d, writes=[k])
    P.memset("dve", ones[:], 1.0, writes=["ones"])
    P.memset("dve", epst[:], 1e-6, writes=["eps"])
    epsln = sb("epsln_s", [128, 1], F32)
    P.memset("dve", epsln[:], 1e-5, writes=["epsln"])
    load_w(Wa, "Wa", Wa_d, 8, NWA)
    load_w(Wba, "Wba", Wba_d, 4, 1024)
    load_w(Wbc, "Wbc", Wbc_d, 4, 1024)
    load_w(Wbg, "Wbg", Wbg_d, 4, 1024)
    load_w(Wo, "Wo", Wo_d, 8, 1024)
    outs = []
    WmgV = Wmg_d.rearrange("(kc p) n -> p kc n", p=128)
    xTv = xTh.rearrange("(kc p) t -> p kc t", p=128)
    mgcnt = [0]
    acnt = [0]

    for b in range(nblocks):
        P.dma(stage[:, :, 0:XW], xTv[:, :, TB * b:TB * b + XW], writes=["stage"])
        for kc in range(8):
            P.copy(cast_eng(), xb[:, kc, :], stage[:, kc, 0:XW], reads=["stage"], writes=["xb"])
        XM = lambda kc: xb[:, kc, 128:128 + TB]
        for g in range(2):
            pb, pk = nb()
            for kc in range(8):
                P.mm(pb[0:64, 0:XW], Wa[:, kc, C_AK + 64 * g:C_AK + 64 * g + 64], xb[:, kc, :],
                     start=(kc == 0), stop=(kc == 7), reads=["Wa", "xb"], writes=[pk])
            P.copy("act", kT[:, g, :], pb[0:64, 0:XW], reads=[pk], writes=["kT"])
        pb, pk = nb()
        for ti in range(TPB + 2):
            for kc in range(8):
                P.mm(pb[:, 128 * ti:128 * ti + 128], xb[:, kc, 128 * ti:128 * ti + 128], Wa[:, kc, C_AV:C_AV + 128],
                     start=(kc == 0), stop=(kc == 7), reads=["Wa", "xb"], writes=[pk])
        P.copy("act", V[:], pb[:, 0:(TPB + 2) * 128], reads=[pk], writes=["V"])
        for h in range(8):
            g = h // 4
            qi = h % 2
            pb, pk = nb()
            for kc in range(8):
                P.mm(pb[0:64, 0:TB], Wa[:, kc, C_AQ + 64 * h:C_AQ + 64 * h + 64], XM(kc), start=(kc == 0), stop=(kc == 7),
                     reads=["Wa", "xb"], writes=[pk])
            P.copy("act", qT[qi][:], pb[0:64, 0:TB], reads=[pk], writes=["qT%d" % qi])
            for i in range(TPB):
                j = acnt[0] % 2
                acnt[0] += 1
                J = str(j)
                gt = TPB * b + i
                mi = 0 if gt == 0 else (2 if gt == NT // 128 - 1 else 1)
                pb, pk = nb()
                P.mm(pb[:, 0:384], qT[qi][:, 128 * i:128 * i + 128], kT[:, g, 128 * i:128 * i + 384], reads=["qT%d" % qi, "kT"], writes=[pk])
                P.stt("dve", s1[j][:], pb[:, 0:384], 0.125, maskb[:, mi, :], ALU.mult, ALU.add, reads=[pk, "maskb"], writes=["s1" + J])
                P.stt("dve", s1[j][:], dist[:], -SLOPES[h], s1[j][:], ALU.mult, ALU.add, reads=["dist", "s1" + J], writes=["s1" + J])
                P.op("dve", lambda v, o=st[j][:, 0:1], a=s1[j][:]: v.reduce_max(o, a, AX.X), reads=["s1" + J], writes=["st" + J])
                P.ts("dve", st[j][:, 1:2], st[j][:, 0:1], sink[:, h:h + 1], -1.0, ALU.max, ALU.mult, reads=["st" + J, "sink"], writes=["st" + J])
                P.act(Pn[j][:], s1[j][:], AF.Exp, bias=st[j][:, 1:2], accum_out=st[j][:, 2:3], reads=["s1" + J, "st" + J], writes=["Pn" + J, "st" + J])
                P.act(st[j][:, 3:4], sink[:, h:h + 1], AF.Exp, bias=st[j][:, 1:2], reads=["sink", "st" + J], writes=["st" + J])
                P.tt("dve", st[j][:, 4:5], st[j][:, 2:3], st[j][:, 3:4], ALU.add, reads=["st" + J], writes=["st" + J])
                P.op("dve", lambda v, o=st[j][:, 5:6], a=st[j][:, 4:5]: v.reciprocal(o, a), reads=["st" + J], writes=["st" + J])
                P.ts("dve", Dg[j][:], ident[:], st[j][:, 5:6], None, ALU.mult, reads=["ident", "st" + J], writes=["Dg" + J])
                pt_, ptk = nb()
                for k in range(3):
                    P.mm(pt_[:, 128 * k:128 * k + 128], Pn[j][:, 128 * k:128 * k + 128], Dg[j][:], reads=["Pn" + J, "Dg" + J], writes=[ptk])
                P.copy("act", PT[j][:], pt_[:, 0:384], reads=[ptk], writes=["PT" + J])
                pb2, pk2 = nb()
                po = 64 * (h % 2)
                for k in range(3):
                    P.mm(pb2[po:po + 64, 0:128], V[:, 128 * (i + k) + 64 * g:128 * (i + k) + 64 * g + 64], PT[j][:, 128 * k:128 * k + 128],
                         start=(k == 0), stop=(k == 2), reads=["V", "PT" + J], writes=[pk2])
                P.copy("act", yaT[po:po + 64, h // 2, 128 * i:128 * i + 128], pb2[po:po + 64, 0:128], reads=[pk2], writes=["yaT"])
        for fc in range(4):
            pch, kch = nb()
            pcc, kcc = nb()
            pcb, kcb = nb()
            for (pbk, kk, c0) in ((pch, kch, C_CH), (pcc, kcc, C_CC)):
                for kc in range(8):
                    P.mm(pbk[:, 0:TB + 2], Wa[:, kc, c0 + 128 * fc:c0 + 128 * fc + 128], xb[:, kc, 127:129 + TB],
                         start=(kc == 0), stop=(kc == 7), reads=["Wa", "xb"], writes=[kk])
            for kc in range(8):
                P.mm(pcb[:, 0:TB], Wa[:, kc, C_CB + 128 * fc:C_CB + 128 * fc + 128], XM(kc),
                     start=(kc == 0), stop=(kc == 7), reads=["Wa", "xb"], writes=[kcb])
            P.copy("act", cc_s[:], pcc[:, 0:TB + 2], reads=[kcc], writes=["cc_s"])
            P.tt("dve", u_s[:], pch[:, 0:TB + 2], cc_s[:], ALU.mult, reads=[kch, "cc_s"], writes=["u_s"])
            P.ts("dve", t_s[:], u_s[:, 0:TB], convw[:, fc, 0:1], None, ALU.mult, reads=["u_s", "convw"], writes=["t_s"])
            P.stt("dve", t_s[:], u_s[:, 1:TB + 1], convw[:, fc, 1:2], t_s[:], ALU.mult, ALU.add, reads=["u_s", "convw", "t_s"], writes=["t_s"])
            P.stt("dve", t_s[:], u_s[:, 2:TB + 2], convw[:, fc, 2:3], t_s[:], ALU.mult, ALU.add, reads=["u_s", "convw", "t_s"], writes=["t_s"])
            P.tt("dve", ycT[:, fc, :], pcb[:, 0:TB], t_s[:], ALU.mult, reads=[kcb, "t_s"], writes=["ycT"])
        for h in range(4):
            j = h % 2
            J = str(j)
            P.dma(of_s[j][:], oTf[128 * h:128 * h + 128, TB * b:TB * b + TB], writes=["of" + J])
            P.dma(ob_s[j][:], oTb[128 * h:128 * h + 128, TB * b:TB * b + TB], writes=["ob" + J])
            P.tt("pool", of_s[j][:], of_s[j][:], ob_s[j][:], ALU.add, reads=["of" + J, "ob" + J], writes=["of" + J])
            P.act(sq_s[:], of_s[j][:], AF.Square, reads=["of" + J], writes=["sq_s"])
            pb, pk = nb()
            P.mm(pb[:, 0:TB], ones[:], sq_s[:], reads=["ones", "sq_s"], writes=[pk])
            P.act(rinv[:], pb[:, 0:TB], AF.Sqrt, scale=1.0 / 128.0, bias=epst[:, 0:1], reads=[pk, "eps"], writes=["rinv"])
            P.op("dve", lambda v, o=rinv[:]: v.reciprocal(o, o), reads=["rinv"], writes=["rinv"])
            pr, pkr = nb()
            for kc in range(8):
                P.mm(pr[:, 0:TB], Wa[:, kc, C_GR + 128 * h:C_GR + 128 * h + 128], XM(kc), start=(kc == 0), stop=(kc == 7),
                     reads=["Wa", "xb"], writes=[pkr])
            P.act(silr[:], pr[:, 0:TB], AF.Silu, reads=[pkr], writes=["silr"])
            P.stt("dve", rinv[:], of_s[j][:], normg[:, h:h + 1], rinv[:], ALU.mult, ALU.mult, reads=["of" + J, "normg", "rinv"], writes=["rinv"])
            P.tt("dve", ygT[:, h, :], rinv[:], silr[:], ALU.mult, reads=["rinv", "silr"], writes=["ygT"])
        for half in range(2):
            for br in range(3):
                wi = mgcnt[0] % 2
                mgcnt[0] += 1
                c0 = 1024 * br + 512 * half
                P.dma(stage[:, :, :], WmgV[:, :, c0:c0 + 512], writes=["stage"])
                for kc in range(8):
                    P.copy(cast_eng(), Wmg[wi][:, kc, :], stage[:, kc, :], reads=["stage"], writes=["Wmg%d" % wi])
                for cl in range(4):
                    c = 4 * half + cl
                    gi = (br * 4 + cl) % 2
                    pg, kg = nb()
                    for kc in range(8):
                        P.mm(pg[:, 0:TB], Wmg[wi][:, kc, 128 * cl:128 * cl + 128], XM(kc), start=(kc == 0), stop=(kc == 7),
                             reads=["Wmg%d" % wi, "xb"], writes=[kg])
                    P.act(g_s[gi][:], pg[:, 0:TB], AF.Sigmoid, reads=[kg], writes=["g_s%d" % gi])
                    pbr, kbr = nb()
                    Wb_, yT_, wk, yk = ((Wba, yaT, "Wba", "yaT"), (Wbc, ycT, "Wbc", "ycT"), (Wbg, ygT, "Wbg", "ygT"))[br]
                    for fk in range(4):
                        P.mm(pbr[:, 0:TB], Wb_[:, fk, 128 * c:128 * c + 128], yT_[:, fk, :], start=(fk == 0), stop=(fk == 3),
                             reads=[wk, yk], writes=[kbr])
                    ak = "accm%d" % cl
                    if br == 0:
                        P.tt("dve", accm[cl][:], g_s[gi][:], pbr[:, 0:TB], ALU.mult, reads=["g_s%d" % gi, kbr], writes=[ak])
                    else:
                        P.tt("dve", tmp[:], g_s[gi][:], pbr[:, 0:TB], ALU.mult, reads=["g_s%d" % gi, kbr], writes=["tmp"])
                        if br == 1:
                            P.tt("pool", accm[cl][:], accm[cl][:], tmp[:], ALU.add, reads=[ak, "tmp"], writes=[ak])
                        else:
                            P.tt("pool", mT[:, c, :], accm[cl][:], tmp[:], ALU.add, reads=[ak, "tmp"], writes=["mT"])
        for i in range(TPB):
            gt = TPB * b + i
            j = gt % 2
            J = str(j)
            P.dma(xt[j][:], xtok[128 * gt:128 * gt + 128, :], writes=["xt" + J])
            for hf in range(2):
                pb, pk = nb()
                for c in range(8):
                    P.mm(pb[:, :], mT[:, c, 128 * i:128 * i + 128], Wo[:, c, 512 * hf:512 * hf + 512], start=(c == 0), stop=(c == 7),
                         reads=["mT", "Wo"], writes=[pk])
                P.stt("dve", xt[j][:, 512 * hf:512 * hf + 512], xt[j][:, 512 * hf:512 * hf + 512], ALPHA, pb[:, :], ALU.mult, ALU.add,
                      reads=["xt" + J, pk], writes=["xt" + J])
            layer_norm(P, xt[j], "xt" + J, bst, mv, rstd, lng, lnb, "lng", "lnb", epsln)
            outs.append(P.dma(x1_d[128 * gt:128 * gt + 128, :], xt[j][:], reads=["xt" + J], writes=["x1_d"]))
            for hf in range(2):
                pb, pk = nb()
                for kq in range(4):
                    kc = 4 * hf + kq
                    P.mm(pb[:, 128 * kq:128 * kq + 128], xt[j][:, 128 * kc:128 * kc + 128], ident[:], reads=["xt" + J, "ident"], writes=[pk])
                P.copy("act", x1T[:, 512 * hf:512 * hf + 512], pb[:, :], reads=[pk], writes=["x1T"])
            pb, pk = nb()
            for kc in range(8):
                P.mm(pb[:, 0:16], x1T[:, 128 * kc:128 * kc + 128], Wr[:, kc, :], start=(kc == 0), stop=(kc == 7), reads=["x1T", "Wr"], writes=[pk])
            P.copy("act", lg[:], pb[:, 0:16], reads=[pk], writes=["lg"])
            P.op("dve", lambda v, o=lst[:, 0:1], a=lg[:]: v.reduce_max(o, a, AX.X), reads=["lg"], writes=["lst"])
            P.ts("dve", lst[:, 1:2], lst[:, 0:1], -1.0, None, ALU.mult, reads=["lst"], writes=["lst"])
            P.act(lg[:], lg[:], AF.Exp, bias=lst[:, 1:2], accum_out=lst[:, 2:3], reads=["lg", "lst"], writes=["lg", "lst"])
            P.op("dve", lambda v, o=lst[:, 3:4], a=lst[:, 2:3]: v.reciprocal(o, a), reads=["lst"], writes=["lst"])
            P.ts("dve", affs[j][:], lg[:], lst[:, 3:4], None, ALU.mult, reads=["lg", "lst"], writes=["affs" + J])
            outs.append(P.dma(aff_d[128 * gt:128 * gt + 128, :], affs[j][:], reads=["affs" + J], writes=["aff_d"]))
    P.emit(final_wait_ops=outs)
    return nc


def a_consts(core):
    q = np.arange(128)[:, None]
    ko = np.arange(384)[None, :] - 128
    dist = np.abs(q - ko).astype(np.float32)
    base = np.where(dist <= 128, 0.0, -30000.0).astype(np.float32)
    mk = np.stack([base, base, base]).copy()
    if core == 0:
        mk[0][:, 0:128] = -30000.0
    if core == 7:
        mk[2][:, 256:384] = -30000.0
    return dict(maskb=mk, dist=dist, ident=np.eye(128, dtype=np.float32))


def a_inputs(core, x_l, xT_l, oTf_full, oTb_full, w):
    S = x_l.shape[0]
    t0 = core * NT
    xTh = np.zeros((1024, NT + 256), np.float32)
    lo, hi = max(t0 - 128, 0), min(t0 + NT + 128, S)
    xTh[:, lo - (t0 - 128):hi - (t0 - 128)] = xT_l[:, lo:hi]
    w_in = w["w_in"]
    Wa = np.concatenate([w_in[:, 0:2304], w_in[:, 3328:3840]], axis=1)
    r = dict(xTh=xTh, xtok=np.ascontiguousarray(x_l[t0:t0 + NT]),
             oTf=np.ascontiguousarray(oTf_full[:, t0:t0 + NT]), oTb=np.ascontiguousarray(oTb_full[:, t0:t0 + NT]),
             Wa=np.ascontiguousarray(Wa), Wmg=np.ascontiguousarray(w_in[:, 3872:6944]),
             Wba=w["w_branch_attn"], Wbc=w["w_branch_conv"], Wbg=w["w_branch_gla"], Wo=w["w_out"], Wr=w["router_w"],
             sink=np.ascontiguousarray(np.broadcast_to(w["attn_sink"][None, :], (128, 8))),
             convw=np.ascontiguousarray(w["conv_w"].reshape(3, 4, 128).transpose(2, 1, 0)),
             normg=np.ascontiguousarray(w["gla_norm_g"].reshape(4, 128).T),
             lng=np.ascontiguousarray(np.broadcast_to(w["ln_mix_g"][None, :], (128, 1024))),
             lnb=np.ascontiguousarray(np.broadcast_to(w["ln_mix_b"][None, :], (128, 1024))))
    r.update(a_consts(core))
    return r


S_E = 16384
CAP = 2048
NITER = 36


def build_E(stop_after=None):
    nc = bass.Bass("TRN2", target_bir_lowering=False)
    din = lambda n, s, d=F32: nc.dram_tensor(n, s, d, kind="ExternalInput").ap()
    aff_d = din("affT", [2, 128, 128])
    x1_d = din("x1", [S_E, 1024])
    Wg_d = din("Wg", [2, 1024, 2048])
    Wu_d = din("Wu", [2, 1024, 2048])
    Wd_d = din("Wd", [2, 2048, 1024])
    iota_d = din("iota", [128, 512])
    Lc_d = din("Lc", [128, 128, 4])
    U_d = din("U", [128, 128])
    SU_d = din("SU", [128, 128])
    ident_d = din("ident", [128, 128])
    dense = nc.dram_tensor("dense", [S_E, 1024], F32, kind="ExternalOutput").ap()
    dbg = nc.dram_tensor("dbg", [2, 128, 32], F32, kind="ExternalOutput").ap()

    P = Prog(nc)
    sb = lambda n, s, d: nc.alloc_sbuf_tensor(n, s, d)
    stage = sb("stage", [128, 8, 512], F32)
    Wg = sb("Wg_s", [128, 8, 2048], BF16)
    Wu = sb("Wu_s", [128, 8, 2048], BF16)
    Wd = sb("Wd_s", [128, 16, 1024], BF16)
    xsT = sb("xsT", [128, 8, CAP], BF16)
    hT = sb("hT", [128, 16, 256], BF16)
    iota = sb("iota_s", [128, 512], F32)
    Lf = sb("Lf", [128, 128, 4], F32)
    L = sb("L", [128, 128, 4], BF16)
    U = sb("U_s", [128, 128], BF16)
    SU = sb("SU_s", [128, 128], F32)
    ident = sb("ident_s", [128, 128], F32)
    identb = sb("identb_s", [128, 128], BF16)
    ones = sb("ones_s", [128, 128], F32)
    A = sb("A_s", [128, 128], F32)
    cmp_ = sb("cmp_s", [128, 128], F32)
    sm = sb("sm_s", [128, 16], F32)
    mk = sb("mk_s", [128, 128], F32)
    mkb = sb("mkb_s", [128, 128], BF16)
    mTb = sb("mTb_s", [128, 128], BF16)
    posm = sb("posm_s", [128, 128], F32)
    poss = sb("poss_s", [128, 128], F32)
    hib = sb("hib_s", [128, 128], BF16)
    hif = sb("hif_s", [128, 128], F32)
    OH = [sb("OH%d" % i, [128, 512], BF16) for i in range(4)]
    R = sb("R_s", [4, 512], F32)
    RT = sb("RT_s", [128, 16, 4], F32)
    tokf = sb("tokf", [128, 16], F32)
    idx = sb("idx_s", [128, 16], I32)
    gate = sb("gate_s", [128, 16], F32)
    xg = [sb("xg%d" % i, [128, 1024], F32) for i in range(2)]
    xgb = sb("xgb", [128, 1024], BF16)
    sg = [sb("sg%d" % i, [128, 256], F32) for i in range(2)]
    ysb = [sb("ysb%d" % i, [128, 1024], F32) for i in range(2)]
    cur = [sb("cur%d" % i, [128, 1024], F32) for i in range(2)]
    psb = [nc.alloc_psum_tensor("psb%d" % i, [128, 512], F32) for i in range(8)]
    bank = [0]
    ceng = [0]

    def nb():
        i = bank[0]
        bank[0] = (i + 1) % 8
        return psb[i], "psb%d" % i

    def cast_eng():
        ceng[0] += 1
        return "dve" if ceng[0] % 2 else "act"

    def load_w(dst, dkey, sv, nk, kc0, ncols):
        for c0 in range(0, ncols, 512):
            P.dma(stage[:, 0:nk, :], sv[:, kc0:kc0 + nk, c0:c0 + 512], writes=["stage"])
            for kc in range(nk):
                P.copy(cast_eng(), dst[:, kc0 + kc, c0:c0 + 512], stage[:, kc, :], reads=["stage"], writes=[dkey])

    P.dma(iota[:], iota_d, writes=["iota"])
    P.dma(Lf[:], Lc_d, writes=["Lf"])
    P.dma(SU[:], SU_d, writes=["SU"])
    P.dma(ident[:], ident_d, writes=["ident"])
    P.dma(cmp_[:], U_d, writes=["cmp"])
    P.copy("dve", U[:], cmp_[:], reads=["cmp"], writes=["U"])
    P.copy("dve", identb[:], ident[:], reads=["ident"], writes=["identb"])
    P.memset("dve", ones[:], 1.0, writes=["ones"])
    for kc in range(8):
        P.memset("dve", stage[:, kc, :], 0.0, writes=["stage"])
    dv = dense.rearrange("(n p) d -> n p d", p=128)
    zkeys = []
    zops = []
    for z in range(32):
        zk = "dz%d" % z
        zops.append(P.dma(dense[512 * z:512 * z + 512, :].rearrange("(p r) d -> p r d", p=128), stage[:, :, :], reads=["stage"], writes=[zk]))
        zkeys.append(zk)
    outs = list(zops)
    sckeys = []
    for e in range(2):
        E = str(e)
        P.dma(A[:], aff_d[e], writes=["A"])
        P.memset("dve", sm[:, 0:1], 0.0, writes=["sm"])
        P.memset("dve", sm[:, 1:2], 1.0, writes=["sm"])
        for it in range(NITER):
            P.ts("dve", sm[:, 2:3], sm[:, 0:1], sm[:, 1:2], 0.5, ALU.add, ALU.mult, reads=["sm"], writes=["sm"])
            P.ts("dve", cmp_[:], A[:], sm[:, 2:3], None, ALU.is_ge, reads=["A", "sm"], writes=["cmp"])
            P.op("dve", lambda v, o=sm[:, 3:4], a=cmp_[:]: v.reduce_sum(o, a, AX.X), reads=["cmp"], writes=["sm"])
            pb, pk = nb()
            P.mm(pb[:, 0:1], ones[:], sm[:, 3:4], reads=["ones", "sm"], writes=[pk])
            P.ts("dve", sm[:, 4:5], pb[:, 0:1], float(CAP), None, ALU.is_ge, reads=[pk], writes=["sm"])
            P.stt("dve", sm[:, 0:1], sm[:, 4:5], sm[:, 2:3], sm[:, 0:1], ALU.mult, ALU.max, reads=["sm"], writes=["sm"])
            P.stt("dve", sm[:, 5:6], sm[:, 4:5], 2.0, sm[:, 2:3], ALU.mult, ALU.add, reads=["sm"], writes=["sm"])
            P.tt("dve", sm[:, 1:2], sm[:, 1:2], sm[:, 5:6], ALU.min, reads=["sm"], writes=["sm"])
        P.ts("dve", mk[:], A[:], sm[:, 0:1], None, ALU.is_ge, reads=["A", "sm"], writes=["mk"])
        P.copy("dve", mkb[:], mk[:], reads=["mk"], writes=["mkb"])
        pb, pk = nb()
        P.mm(pb[:, 0:128], mkb[:], identb[:], reads=["mkb", "identb"], writes=[pk])
        P.copy("act", mTb[:], pb[:, 0:128], reads=[pk], writes=["mTb"])
        pc, pck = nb()
        P.mm(pc[:, 0:128], mTb[:], U[:], reads=["mTb", "U"], writes=[pck])
        P.copy("act", posm[:], pc[:, 0:128], reads=[pck], writes=["posm"])
        pb, pk = nb()
        P.mm(pb[:, 0:1], SU[:], posm[:, 127:128], reads=["SU", "posm"], writes=[pk])
        P.copy("act", sm[:, 7:8], pb[:, 0:1], reads=[pk], writes=["sm"])
        P.ts("dve", posm[:], posm[:], sm[:, 7:8], None, ALU.add, reads=["posm", "sm"], writes=["posm"])
        P.tt("dve", posm[:], posm[:], mk[:], ALU.mult, reads=["posm", "mk"], writes=["posm"])
        P.ts("dve", posm[:], posm[:], -1.0, None, ALU.add, reads=["posm"], writes=["posm"])
        P.copy("dve", hib[:], A[:], reads=["A"], writes=["hib"])
        P.copy("dve", hif[:], hib[:], reads=["hib"], writes=["hif"])
        P.tt("dve", hif[:], A[:], hif[:], ALU.subtract, reads=["A", "hif"], writes=["hif"])
        P.copy("dve", L[:, :, 0:2], Lf[:, :, 0:2], reads=["Lf"], writes=["L"])
        P.copy("dve", L[:, :, 2], hib[:], reads=["hib"], writes=["L"])
        P.copy("dve", L[:, :, 3], hif[:], reads=["hif"], writes=["L"])
        ohc = 0
        for sblk in range(4):
            P.ts("dve", poss[:], posm[:], -512.0 * sblk, None, ALU.add, reads=["posm"], writes=["poss"])
            pr, prk = nb()
            for j in range(128):
                o = ohc % 4
                ohc += 1
                P.ts("dve", OH[o][:], iota[:], poss[:, j:j + 1], None, ALU.is_equal, reads=["iota", "poss"], writes=["OH%d" % o])
                P.mm(pr[0:4, :], L[:, j, :], OH[o][:], start=(j == 0), stop=(j == 127), reads=["L", "OH%d" % o], writes=[prk])
            P.copy("act", R[:, :], pr[0:4, :], reads=[prk], writes=["R"])
            pb, pk = nb()
            for k in range(4):
                P.mm(pb[:, 4 * k:4 * k + 4], R[:, 128 * k:128 * k + 128], ident[0:4, 0:4], reads=["R", "ident"], writes=[pk])
            for k in range(4):
                P.copy("act", RT[:, 4 * sblk + k, :], pb[:, 4 * k:4 * k + 4], reads=[pk], writes=["RT"])
        P.stt("dve", tokf[:], RT[:, :, 0], 128.0, RT[:, :, 1], ALU.mult, ALU.add, reads=["RT"], writes=["tokf"])
        P.copy("dve", idx[:], tokf[:], reads=["tokf"], writes=["idx"])
        P.tt("dve", gate[:], RT[:, :, 2], RT[:, :, 3], ALU.add, reads=["RT"], writes=["gate"])
        outs.append(P.dma(dbg[e, :, 0:16], tokf[:], reads=["tokf"], writes=["dbg%d" % e]))
        outs.append(P.dma(dbg[e, :, 16:32], gate[:], reads=["gate"], writes=["dbg%db" % e]))
        if stop_after == "idx":
            continue
        load_w(Wg, "Wg", Wg_d[e].rearrange("(kc p) n -> p kc n", p=128), 8, 0, 2048)
        load_w(Wu, "Wu", Wu_d[e].rearrange("(kc p) n -> p kc n", p=128), 8, 0, 2048)
        WdV = Wd_d[e].rearrange("(kc p) n -> p kc n", p=128)
        load_w(Wd, "Wd", WdV, 8, 0, 1024)
        load_w(Wd, "Wd", WdV, 8, 8, 1024)
        for blk in range(16):
            j = blk % 2
            J = str(j)
            P.dma_fn(lambda g, o=xg[j][:], off=idx[:, blk:blk + 1]: g.indirect_dma_start(
                out=o, out_offset=None, in_=x1_d, in_offset=bass.IndirectOffsetOnAxis(ap=off, axis=0)),
                reads=["idx"], writes=["xg" + J])
            P.copy("act", xgb[:, 0:512], xg[j][:, 0:512], reads=["xg" + J], writes=["xgb"])
            P.copy("dve", xgb[:, 512:1024], xg[j][:, 512:1024], reads=["xg" + J], writes=["xgb"])
            for hf in range(2):
                pb, pk = nb()
                for kq in range(4):
                    kc = 4 * hf + kq
                    P.mm(pb[:, 128 * kq:128 * kq + 128], xgb[:, 128 * kc:128 * kc + 128], identb[:], reads=["xgb", "identb"], writes=[pk])
                for kq in range(4):
                    P.copy("act" if kq % 2 else "dve", xsT[:, 4 * hf + kq, 128 * blk:128 * blk + 128],
                           pb[:, 128 * kq:128 * kq + 128], reads=[pk], writes=["xsT"])
        for jb in range(8):
            js = slice(256 * jb, 256 * jb + 256)
            for fb in range(16):
                gi = fb % 2
                pg, kg = nb()
                pu, ku = nb()
                for kc in range(8):
                    P.mm(pg[:, 0:256], Wg[:, kc, 128 * fb:128 * fb + 128], xsT[:, kc, js], start=(kc == 0), stop=(kc == 7),
                         reads=["Wg", "xsT"], writes=[kg])
                for kc in range(8):
                    P.mm(pu[:, 0:256], Wu[:, kc, 128 * fb:128 * fb + 128], xsT[:, kc, js], start=(kc == 0), stop=(kc == 7),
                         reads=["Wu", "xsT"], writes=[ku])
                P.act(sg[gi][:], pg[:, 0:256], AF.Silu, reads=[kg], writes=["sg%d" % gi])
                P.tt("dve", hT[:, fb, :], sg[gi][:], pu[:, 0:256], ALU.mult, reads=["sg%d" % gi, ku], writes=["hT"])
            for sub in range(2):
                blk = 2 * jb + sub
                yi = blk % 2
                Y = str(yi)
                if e == 1:
                    P.dma_fn(lambda g, o=cur[yi][:], off=idx[:, blk:blk + 1]: g.indirect_dma_start(
                        out=o, out_offset=None, in_=dense, in_offset=bass.IndirectOffsetOnAxis(ap=off, axis=0)),
                        reads=["idx"] + sckeys + zkeys, writes=["cur" + Y])
                for hf in range(2):
                    pb, pk = nb()
                    for fb in range(16):
                        P.mm(pb[:, :], hT[:, fb, 128 * sub:128 * sub + 128], Wd[:, fb, 512 * hf:512 * hf + 512], start=(fb == 0), stop=(fb == 15),
                             reads=["hT", "Wd"], writes=[pk])
                    if e == 0:
                        P.ts("dve", ysb[yi][:, 512 * hf:512 * hf + 512], pb[:, :], gate[:, blk:blk + 1], None, ALU.mult,
                             reads=[pk, "gate"], writes=["ysb" + Y])
                    else:
                        P.stt("dve", ysb[yi][:, 512 * hf:512 * hf + 512], pb[:, :], gate[:, blk:blk + 1], cur[yi][:, 512 * hf:512 * hf + 512],
                              ALU.mult, ALU.add, reads=[pk, "gate", "cur" + Y], writes=["ysb" + Y])
                sk = "dsc%d_%d" % (e, blk)
                op = P.dma_fn(lambda g, i_=ysb[yi][:], off=idx[:, blk:blk + 1]: g.indirect_dma_start(
                    out=dense, out_offset=bass.IndirectOffsetOnAxis(ap=off, axis=0), in_=i_, in_offset=None),
                    reads=["idx", "ysb" + Y] + (zkeys if e == 0 else []), writes=[sk])
                outs.append(op)
                if e == 0:
                    sckeys.append(sk)
    P.emit(final_wait_ops=outs)
    return nc


def e_consts():
    p = np.arange(128)[:, None]
    j = np.arange(128)[None, :]
    Lc = np.zeros((128, 128, 4), np.float32)
    Lc[:, :, 0] = p
    Lc[:, :, 1] = j
    return dict(iota=np.ascontiguousarray(np.broadcast_to(np.arange(512, dtype=np.float32)[None, :], (128, 512))),
                Lc=Lc, U=(p <= j).astype(np.float32), SU=(p < j).astype(np.float32), ident=np.eye(128, dtype=np.float32))


def e_inputs(core, aff_full, x1_full, w):
    es = slice(2 * core, 2 * core + 2)
    r = dict(affT=np.ascontiguousarray(aff_full[:, es].T.reshape(2, 128, 128)), x1=x1_full,
             Wg=np.ascontiguousarray(w["expert_w_gate"][es]), Wu=np.ascontiguousarray(w["expert_w_up"][es]),
             Wd=np.ascontiguousarray(w["expert_w_down"][es]))
    r.update(e_consts())
    return r


def build_C():
    nc = bass.Bass("TRN2", target_bir_lowering=False)
    din = lambda n, s, d=F32: nc.dram_tensor(n, s, d, kind="ExternalInput").ap()
    part = din("part", [8, NT, 1024])
    x1 = din("x1", [NT, 1024])
    lng_d = din("lng", [128, 1024])
    lnb_d = din("lnb", [128, 1024])
    out = nc.dram_tensor("x2", [NT, 1024], F32, kind="ExternalOutput").ap()
    P = Prog(nc)
    sb = lambda n, s, d: nc.alloc_sbuf_tensor(n, s, d)
    lng = sb("lng_s", [128, 1024], F32)
    lnb = sb("lnb_s", [128, 1024], F32)
    epsln = sb("epsln_s", [128, 1], F32)
    xt = [sb("xt%d" % i, [128, 1024], F32) for i in range(2)]
    pt = [[sb("pt%d_%d" % (i, k), [128, 1024], F32) for k in range(8)] for i in range(2)]
    bst = sb("bst", [128, 2, 6], F32)
    mv = sb("mv", [128, 2], F32)
    rstd = sb("rstd", [128, 1], F32)
    P.dma(lng[:], lng_d, writes=["lng"])
    P.dma(lnb[:], lnb_d, writes=["lnb"])
    P.memset("dve", epsln[:], 1e-5, writes=["epsln"])
    outs = []
    for t in range(NT // 128):
        i = t % 2
        I = str(i)
        rows = slice(128 * t, 128 * t + 128)
        P.dma(xt[i][:], x1[rows, :], writes=["xt" + I])
        for k in range(8):
            P.dma(pt[i][k][:], part[k, rows, :], writes=["pt%s_%d" % (I, k)])
        P.tt("pool", pt[i][0][:], pt[i][0][:], pt[i][1][:], ALU.add, reads=["pt%s_0" % I, "pt%s_1" % I], writes=["pt%s_0" % I])
        P.tt("dve", pt[i][2][:], pt[i][2][:], pt[i][3][:], ALU.add, reads=["pt%s_2" % I, "pt%s_3" % I], writes=["pt%s_2" % I])
        P.tt("pool", pt[i][4][:], pt[i][4][:], pt[i][5][:], ALU.add, reads=["pt%s_4" % I, "pt%s_5" % I], writes=["pt%s_4" % I])
        P.tt("dve", pt[i][6][:], pt[i][6][:], pt[i][7][:], ALU.add, reads=["pt%s_6" % I, "pt%s_7" % I], writes=["pt%s_6" % I])
        P.tt("pool", pt[i][0][:], pt[i][0][:], pt[i][2][:], ALU.add, reads=["pt%s_0" % I, "pt%s_2" % I], writes=["pt%s_0" % I])
        P.tt("dve", pt[i][4][:], pt[i][4][:], pt[i][6][:], ALU.add, reads=["pt%s_4" % I, "pt%s_6" % I], writes=["pt%s_4" % I])
        P.tt("dve", pt[i][0][:], pt[i][0][:], pt[i][4][:], ALU.add, reads=["pt%s_0" % I, "pt%s_4" % I], writes=["pt%s_0" % I])
        P.stt("dve", xt[i][:], xt[i][:], ALPHA, pt[i][0][:], ALU.mult, ALU.add, reads=["xt" + I, "pt%s_0" % I], writes=["xt" + I])
        layer_norm(P, xt[i], "xt" + I, bst, mv, rstd, lng, lnb, "lng", "lnb", epsln)
        outs.append(P.dma(out[rows, :], xt[i][:], reads=["xt" + I], writes=["x2"]))
    P.emit(final_wait_ops=outs)
    return nc


def c_inputs(core, dense_list, x1_full, w):
    s = slice(core * NT, core * NT + NT)
    return dict(part=np.ascontiguousarray(np.stack([d[s] for d in dense_list])), x1=np.ascontiguousarray(x1_full[s]),
                lng=np.ascontiguousarray(np.broadcast_to(w["ln_ffn_g"][None, :], (128, 1024))),
                lnb=np.ascontiguousarray(np.broadcast_to(w["ln_ffn_b"][None, :], (128, 1024))))


_NC = {}


def _get(name, fn):
    if name not in _NC:
        _NC[name] = fn()
    return _NC[name]


def _run(nc, maps):
    res = run_bass_kernel_spmd(nc, maps, core_ids=list(range(8)))
    return res.results


def kernel(**inputs):
    inputs = {k: np.asarray(v) for k, v in inputs.items()}
    x = np.ascontiguousarray(inputs["x"][0], dtype=np.float32)
    depth = inputs["w_in"].shape[0]
    for l in range(depth):
        w = {k: np.ascontiguousarray(v[l]) for k, v in inputs.items() if k != "x"}
        xT = np.ascontiguousarray(x.T)
        xTr = np.ascontiguousarray(xT[:, ::-1])
        maps = []
        for c in range(8):
            m = g_inputs(xT, w["w_in"], w["gla_gate_w2"], w["gla_gate_b"], c % 4, 0)
            if c // 4 == 1:
                m = g_inputs(xTr, w["w_in"], w["gla_gate_w2"], w["gla_gate_b"], c % 4, 1, flipped=True)
            maps.append(m)
        res = _run(_get("G", build_G), maps)
        oTf = np.concatenate([res[h]["oT"] for h in range(4)], axis=0)
        oTb = np.concatenate([res[4 + h]["oT"][:, ::-1] for h in range(4)], axis=0)
        del maps, res
        maps = [a_inputs(c, x, xT, oTf, oTb, w) for c in range(8)]
        res = _run(_get("A", build_A), maps)
        x1 = np.concatenate([res[c]["x1"] for c in range(8)], axis=0)
        aff = np.concatenate([res[c]["aff"] for c in range(8)], axis=0)
        del maps, res
        maps = [e_inputs(c, aff, x1, w) for c in range(8)]
        res = _run(_get("E", build_E), maps)
        dense = [res[c]["dense"] for c in range(8)]
        del maps, res
        maps = [c_inputs(c, dense, x1, w) for c in range(8)]
        res = _run(_get("C", build_C), maps)
        x = np.concatenate([res[c]["x2"] for c in range(8)], axis=0)
        del maps, res, dense
    return x[None].astype(np.float32)
```

```python
import numpy as np
import concourse.bass as bass
import concourse.mybir as mybir
from concourse.bass_utils import run_bass_kernel_spmd

F32 = mybir.dt.float32
BF16 = mybir.dt.bfloat16
I32 = mybir.dt.int32
ALU = mybir.AluOpType
AF = mybir.ActivationFunctionType
AX = mybir.AxisListType

ENGS = ("pe", "act", "dve", "pool", "sp")
import os
SKIP_SAME_ENGINE = bool(int(os.environ.get('SKIP_SAME', '0')))
NDSEM = 48
NHW = 32


class Prog:
    def __init__(self, nc):
        self.nc = nc
        self.ops = {e: [] for e in ENGS}
        self.sem = {e: nc.alloc_semaphore("prog_" + e) for e in ("pe", "act", "dve", "pool")}
        self.dsems = [nc.alloc_semaphore("dsem%d" % i) for i in range(NDSEM)]
        self.dcount = [0] * NDSEM
        self.dnext = {"hw": 0, "sw": 0}
        self.last_w = {}
        self.readers = {}
        self.allops = []
        self.out_dma_ops = []

    def _add(self, eng, fn, reads, writes, dma=False):
        op = dict(eng=eng, fn=fn, deps=[], signal=False, dma=dma, idx=len(self.allops))
        writes = list(writes) + [k for k in reads if k.startswith("ps") and k not in writes]
        reads = [k for k in reads if not k.startswith("ps")]
        deps = []
        for k in reads:
            w = self.last_w.get(k)
            if w is not None:
                deps.append(w)
        for k in writes:
            w = self.last_w.get(k)
            if w is not None:
                deps.append(w)
            for r in self.readers.get(k, ()):
                deps.append(r)
        if dma:
            if eng == "pool":
                i = NHW + self.dnext["sw"]
                self.dnext["sw"] = (self.dnext["sw"] + 1) % (NDSEM - NHW)
            else:
                i = self.dnext["hw"]
                self.dnext["hw"] = (self.dnext["hw"] + 1) % NHW
            prev = getattr(self, "_dprev", {}).get(i)
            if prev is not None:
                deps.append(prev)
            if not hasattr(self, "_dprev"):
                self._dprev = {}
            self._dprev[i] = op
            self.dcount[i] += 1
            op["dsem"] = i
            op["dval"] = 16 * self.dcount[i]
            op["signal"] = True
        seen = set()
        for d in deps:
            if d is op or id(d) in seen:
                continue
            seen.add(id(d))
            if d["eng"] == "pe" and eng == "pe" and not d["dma"] and not dma:
                continue
            if SKIP_SAME_ENGINE and d["eng"] == eng and not d["dma"] and not dma:
                continue
            op["deps"].append(d)
            d["signal"] = True
        for k in reads:
            self.readers.setdefault(k, []).append(op)
        for k in writes:
            self.last_w[k] = op
            self.readers[k] = []
        self.ops[eng].append(op)
        self.allops.append(op)
        return op

    def op(self, eng, fn, reads=(), writes=()):
        return self._add(eng, fn, list(reads), list(writes))

    def dma(self, out, in_, reads=(), writes=(), q="sp", **kw):
        return self._add(q, lambda e: e.dma_start(out=out, in_=in_, **kw), list(reads), list(writes), dma=True)

    def dma_fn(self, fn, reads=(), writes=(), q="pool"):
        return self._add(q, fn, list(reads), list(writes), dma=True)

    def mm(self, out, lhsT, rhs, start=True, stop=True, reads=(), writes=()):
        return self.op("pe", lambda t: t.matmul(out, lhsT, rhs, start=start, stop=stop), reads, writes)

    def tr(self, out, in_, ident, reads=(), writes=()):
        return self.op("pe", lambda t: t.transpose(out, in_, ident), reads, writes)

    def act(self, out, in_, func, reads=(), writes=(), **kw):
        return self.op("act", lambda a: a.activation(out=out, in_=in_, func=func, **kw), reads, writes)

    def copy(self, eng, out, in_, reads=(), writes=()):
        if eng == "act":
            return self.op("act", lambda a: a.copy(out, in_), reads, writes)
        return self.op(eng, lambda v: v.tensor_copy(out, in_), reads, writes)

    def tt(self, eng, out, in0, in1, op, reads=(), writes=()):
        return self.op(eng, lambda v: v.tensor_tensor(out, in0, in1, op), reads, writes)

    def stt(self, eng, out, in0, scalar, in1, op0, op1, reads=(), writes=()):
        return self.op(eng, lambda v: v.scalar_tensor_tensor(out, in0, scalar, in1, op0, op1), reads, writes)

    def ts(self, eng, out, in0, s1, s2, op0, op1=None, reads=(), writes=(), **kw):
        if op1 is None:
            return self.op(eng, lambda v: v.tensor_scalar(out, in0, s1, s2, op0, **kw), reads, writes)
        return self.op(eng, lambda v: v.tensor_scalar(out, in0, s1, s2, op0, op1, **kw), reads, writes)

    def memset(self, eng, ap, val, reads=(), writes=()):
        return self.op(eng, lambda v: v.memset(ap, val), reads, writes)

    def emit(self, final_wait_ops=()):
        cnt = {e: 0 for e in self.sem}
        for op in self.allops:
            if op["dma"]:
                op["ticket"] = (self.dsems[op["dsem"]], op["dval"])
            elif op["signal"]:
                cnt[op["eng"]] += 1
                op["ticket"] = (self.sem[op["eng"]], cnt[op["eng"]])
        nc = self.nc
        final_tickets = [op["ticket"] for op in final_wait_ops]

        def run(engname, engobj):
            waited = {}
            for op in self.ops[engname]:
                need = {}
                for d in op["deps"]:
                    s, v = d["ticket"]
                    if need.get(s, (None, 0))[1] < v:
                        need[s] = (s, v)
                for s, v in need.values():
                    if waited.get(s, 0) >= v:
                        continue
                    engobj.wait_ge(s, v)
                    waited[s] = v
                ins = op["fn"](engobj)
                if op["signal"]:
                    s, v = op["ticket"]
                    ins.then_inc(s, 16 if op["dma"] else 1)
            if engname == "sp":
                for s, v in final_tickets:
                    if waited.get(s, 0) < v:
                        engobj.wait_ge(s, v)
                        waited[s] = v

        with nc.Block() as block:
            @block.sync
            def _(e):
                run("sp", e)

            @block.tensor
            def _(e):
                run("pe", e)

            @block.scalar
            def _(e):
                run("act", e)

            @block.vector
            def _(e):
                run("dve", e)

            @block.gpsimd
            def _(e):
                run("pool", e)


S_ = 16384
BLK = 1024
NBLK = S_ // BLK


def build_G(nblk=NBLK, ssalloc=None):
    nc = bass.Bass("TRN2", target_bir_lowering=False)
    SS = ssalloc or nblk * BLK
    xT = nc.dram_tensor("xT", [1024, SS], F32, kind="ExternalInput").ap()
    W = nc.dram_tensor("W", [1024, 272], F32, kind="ExternalInput").ap()
    W2 = nc.dram_tensor("W2", [32, 64], F32, kind="ExternalInput").ap()
    U16d = nc.dram_tensor("U16", [128, 128], F32, kind="ExternalInput").ap()
    UL16d = nc.dram_tensor("UL16", [128, 128], F32, kind="ExternalInput").ap()
    UMd = nc.dram_tensor("UM", [128, 128], F32, kind="ExternalInput").ap()
    oT = nc.dram_tensor("oT", [128, SS], F32, kind="ExternalOutput").ap()
    P = Prog(nc)
    sb = lambda n, s, d: nc.alloc_sbuf_tensor(n, s, d)
    Wb = sb("Wb", [128, 8, 272], BF16)
    W2s = sb("W2s", [32, 64], F32)
    U16 = sb("U16s", [128, 128], F32)
    UL16 = sb("UL16s", [128, 128], F32)
    UM = sb("UMs", [128, 128], F32)
    xb = [sb("xb%d" % i, [128, 8, BLK], BF16) for i in range(2)]
    xf = [sb("xf%d" % i, [128, 8, BLK], F32) for i in range(2)]
    qT = [sb("qT%d" % i, [64, BLK], F32) for i in range(2)]
    kT = [sb("kT%d" % i, [64, BLK], F32) for i in range(2)]
    lrT = [sb("lrT%d" % i, [32, BLK], F32) for i in range(2)]
    ob = [sb("ob%d" % i, [128, BLK], F32) for i in range(2)]
    e1 = [sb("e1_%d" % i, [128, 64], F32) for i in range(2)]
    sp = [sb("sp_%d" % i, [128, 64], F32) for i in range(2)]
    eq = [sb("eq_%d" % i, [64, 128], F32) for i in range(2)]
    ek = [sb("ek_%d" % i, [64, 128], F32) for i in range(2)]
    ekh = [sb("ekh_%d" % i, [128, 64], F32) for i in range(2)]
    QT = [sb("QT_%d" % i, [64, 128], BF16) for i in range(2)]
    KT = [sb("KT_%d" % i, [64, 128], BF16) for i in range(2)]
    Kh = [sb("Kh_%d" % i, [128, 64], BF16) for i in range(2)]
    V = [sb("V_%d" % i, [128, 128], BF16) for i in range(2)]
    ATm = [sb("ATm_%d" % i, [128, 128], BF16) for i in range(2)]
    Sf = sb("Sf", [64, 128], F32)
    Sb = sb("Sb", [64, 128], BF16)
    ps = [nc.alloc_psum_tensor("ps%d" % i, [128, 512], F32) for i in range(8)]

    P.dma(Wb[:], W.rearrange("(kc p) n -> p kc n", p=128), writes=["Wb"], q="pool")
    P.dma(W2s[:], W2, writes=["W2s"])
    P.dma(U16[:], U16d, writes=["U16"])
    P.dma(UL16[:], UL16d, writes=["UL16"])
    P.dma(UM[:], UMd, writes=["UM"])
    P.memset("dve", Sf[:], 0.0, writes=["Sf"])
    P.memset("dve", Sb[:], 0.0, writes=["Sb"])
    for i in range(2):
        P.memset("pool", lrT[i][:], 1.0, writes=["lrT%d" % i])
    xTv = xT.rearrange("(kc p) t -> p kc t", p=128)
    outs = []

    def load_blk(b):
        i = b % 2
        P.dma(xf[i][:], xTv[:, :, b * BLK:(b + 1) * BLK], writes=["xf%d" % i])
        for kc in range(8):
            P.copy("pool" if kc % 2 else "act", xb[i][:, kc, :], xf[i][:, kc, :], reads=["xf%d" % i], writes=["xb%d" % i])

    load_blk(0)
    for b in range(nblk):
        i = b % 2
        if b + 1 < nblk:
            load_blk(b + 1)
        xk = "xb%d" % i
        for sbk in range(BLK // 512):
            cs = slice(sbk * 512, (sbk + 1) * 512)
            for (name, c0, c1, pst, dst, m) in (("q", 0, 64, ps[0], qT[i], 64), ("k", 64, 128, ps[1], kT[i], 64),
                                                 ("lr", 256, 272, ps[2], lrT[i], 16)):
                for kc in range(8):
                    P.mm(pst[0:m, :], Wb[:, kc, c0:c1], xb[i][:, kc, cs], start=(kc == 0), stop=(kc == 7),
                         reads=["Wb", xk], writes=["ps_" + name])
                P.copy("act", dst[0:m, cs], pst[0:m, :], reads=["ps_" + name], writes=["%sT%d" % (name, i)])
        def chunk_vars(b, c):
            n = b * (BLK // 128) + c
            j = n % 2
            return n, j, slice(c * 128, (c + 1) * 128), ps[3 + j], ps[5 + j], "psA%d" % j, "psB%d" % j, str(j)

        def stage1(b, c):
            i = b % 2
            xk = "xb%d" % i
            n, j, cols, pA, pC, kA, kB, J = chunk_vars(b, c)
            for kc in range(8):
                P.mm(pA[:, 0:192], xb[i][:, kc, cols], Wb[:, kc, 64:256], start=(kc == 0), stop=(kc == 7),
                     reads=["Wb", xk], writes=[kA])
            P.mm(pA[:, 192:256], lrT[i][:, cols], W2s[:], reads=["lrT%d" % i, "W2s"], writes=[kA])
            P.act(e1[j][:], pA[:, 192:256], AF.Exp, scale=-1.0, reads=[kA], writes=["e1" + J])
            P.act(sp[j][:], e1[j][:], AF.Ln, bias=1.0, reads=["e1" + J], writes=["sp" + J])
            P.mm(pA[:, 320:384], UL16[:], sp[j][:], reads=["UL16", "sp" + J], writes=[kA])
            P.mm(pA[0:64, 384:512], sp[j][:], U16[:], reads=["U16", "sp" + J], writes=[kA])
            P.act(eq[j][:], pA[0:64, 384:512], AF.Exp, scale=-1.0, reads=[kA], writes=["eq" + J])
            P.act(ek[j][:], pA[0:64, 384:512], AF.Exp, scale=1.0, reads=[kA], writes=["ek" + J])
            P.act(ekh[j][:], pA[:, 320:384], AF.Exp, scale=1.0, reads=[kA], writes=["ekh" + J])
            P.stt("dve", QT[j][:], qT[i][:, cols], 0.125, eq[j][:], ALU.mult, ALU.mult,
                  reads=["qT%d" % i, "eq" + J], writes=["QT" + J])
            P.tt("dve", KT[j][:], kT[i][:, cols], ek[j][:], ALU.mult, reads=["kT%d" % i, "ek" + J], writes=["KT" + J])
            P.tt("dve", Kh[j][:], pA[:, 0:64], ekh[j][:], ALU.mult, reads=[kA, "ekh" + J], writes=["Kh" + J])
            P.copy("dve", V[j][:], pA[:, 64:192], reads=[kA], writes=["V" + J])

        def stage2(b, c):
            i = b % 2
            n, j, cols, pA, pC, kA, kB, J = chunk_vars(b, c)
            P.mm(pC[:, 0:128], KT[j][:], QT[j][:], reads=["KT" + J, "QT" + J], writes=[kB])
            P.tt("dve", ATm[j][:], pC[:, 0:128], UM[:], ALU.mult, reads=[kB, "UM"], writes=["ATm" + J])
            P.mm(pC[0:64, 128:256], Kh[j][:], V[j][:], reads=["Kh" + J, "V" + J], writes=[kB])
            P.mm(pC[:, 256:384], V[j][:], ATm[j][:], start=True, stop=False, reads=["V" + J, "ATm" + J], writes=[kB])
            P.mm(pC[:, 256:384], Sb[:], QT[j][:], start=False, stop=True, reads=["Sb", "QT" + J], writes=[kB])
            P.copy("act", ob[i][:, cols], pC[:, 256:384], reads=[kB], writes=["ob%d" % i])
            P.stt("dve", Sf[:], Sf[:], eq[j][:, 127:128], pC[0:64, 128:256], ALU.mult, ALU.add,
                  reads=["Sf", "eq" + J, kB], writes=["Sf"])
            P.copy("pool", Sb[:], Sf[:], reads=["Sf"], writes=["Sb"])

        NCH = BLK // 128
        stage1(b, 0)
        for c in range(NCH):
            if c + 1 < NCH:
                stage1(b, c + 1)
            stage2(b, c)
        outs.append(P.dma(oT[:, b * BLK:(b + 1) * BLK], ob[i][:], reads=["ob%d" % i], writes=["oT"]))
    P.emit(final_wait_ops=outs)
    return nc


def g_consts():
    s = np.arange(128)[:, None]
    t = np.arange(128)[None, :]
    U = (s <= t).astype(np.float32)
    return dict(U16=U / 16.0, UL16=-(s > t).astype(np.float32) / 16.0, UM=U)


def g_inputs(xT_full, w_in_l, w2_l, b_l, h, d, flipped=False):
    xT = xT_full[:, ::-1] if (d == 1 and not flipped) else xT_full
    cq = 2304 + 64 * h
    ck = 2560 + 64 * h
    cv = 2816 + 128 * h
    clr = 3840 + 16 * d
    W = np.concatenate([w_in_l[:, cq:cq + 64], w_in_l[:, ck:ck + 64], w_in_l[:, cv:cv + 128], w_in_l[:, clr:clr + 16]], axis=1)
    W2 = np.zeros((32, 64), np.float32)
    W2[0:16] = w2_l[d][:, 64 * h:64 * h + 64]
    W2[16] = b_l[d][64 * h:64 * h + 64]
    r = dict(xT=np.ascontiguousarray(xT), W=np.ascontiguousarray(W), W2=W2)
    r.update(g_consts())
    return r


NT = 2048
TB = 512
XW = TB + 256
TPB = TB // 128
NB_A = NT // TB
ALPHA = 4.0 ** 0.25
SLOPES = [2.0 ** (-(h + 1)) for h in range(8)]
C_AQ, C_AK, C_AV, C_CH, C_CB, C_CC, C_GR = 0, 512, 640, 768, 1280, 1792, 2304
NWA = 2816


def layer_norm(P, t, tk, bst, mv, rstd, lng, lnb, gk, bk, epsln):
    for hf in range(2):
        P.op("dve", lambda v, o=bst[:, hf, :], a=t[:, 512 * hf:512 * hf + 512]: v.bn_stats(o, a), reads=[tk], writes=["bst"])
    P.op("dve", lambda v, o=mv[:], a=bst[:]: v.bn_aggr(o, a), reads=["bst"], writes=["mv"])
    P.act(rstd[:], mv[:, 1:2], AF.Sqrt, bias=epsln[:, 0:1], reads=["mv", "epsln"], writes=["rstd"])
    P.op("dve", lambda v, o=rstd[:]: v.reciprocal(o, o), reads=["rstd"], writes=["rstd"])
    P.ts("dve", t[:], t[:], mv[:, 0:1], rstd[:, 0:1], ALU.subtract, ALU.mult, reads=[tk, "mv", "rstd"], writes=[tk])
    P.tt("dve", t[:], t[:], lng[:], ALU.mult, reads=[tk, gk], writes=[tk])
    P.tt("pool", t[:], t[:], lnb[:], ALU.add, reads=[tk, bk], writes=[tk])


def build_A(nblocks=NB_A):
    nc = bass.Bass("TRN2", target_bir_lowering=False)
    din = lambda n, s, d=F32: nc.dram_tensor(n, s, d, kind="ExternalInput").ap()
    xTh = din("xTh", [1024, NT + 256])
    xtok = din("xtok", [NT, 1024])
    oTf = din("oTf", [512, NT])
    oTb = din("oTb", [512, NT])
    Wa_d = din("Wa", [1024, NWA])
    Wmg_d = din("Wmg", [1024, 3072])
    Wba_d = din("Wba", [512, 1024])
    Wbc_d = din("Wbc", [512, 1024])
    Wbg_d = din("Wbg", [512, 1024])
    Wo_d = din("Wo", [1024, 1024])
    Wr_d = din("Wr", [1024, 16])
    maskb_d = din("maskb", [3, 128, 384])
    dist_d = din("dist", [128, 384])
    sink_d = din("sink", [128, 8])
    convw_d = din("convw", [128, 4, 3])
    normg_d = din("normg", [128, 4])
    lng_d = din("lng", [128, 1024])
    lnb_d = din("lnb", [128, 1024])
    ident_d = din("ident", [128, 128])
    x1_d = nc.dram_tensor("x1", [NT, 1024], F32, kind="ExternalOutput").ap()
    aff_d = nc.dram_tensor("aff", [NT, 16], F32, kind="ExternalOutput").ap()

    P = Prog(nc)
    sb = lambda n, s, d: nc.alloc_sbuf_tensor(n, s, d)
    stage2 = sb("stage", [128, 4096], F32)
    stage = stage2[:].rearrange("p (a b) -> p a b", b=512)
    Wa = sb("Wa_s", [128, 8, NWA], BF16)
    Wmg = [sb("Wmg%d" % i, [128, 8, 384], BF16) for i in range(2)]
    Wba = sb("Wba_s", [128, 4, 1024], BF16)
    Wbc = sb("Wbc_s", [128, 4, 1024], BF16)
    Wbg = sb("Wbg_s", [128, 4, 1024], BF16)
    Wo = sb("Wo_s", [128, 8, 1024], BF16)
    Wr = sb("Wr_s", [128, 8, 16], F32)
    maskb = sb("maskb_s", [128, 3, 384], F32)
    dist = sb("dist_s", [128, 384], F32)
    sink = sb("sink_s", [128, 8], F32)
    convw = sb("convw_s", [128, 4, 3], F32)
    normg = sb("normg_s", [128, 4], F32)
    lng = sb("lng_s", [128, 1024], F32)
    lnb = sb("lnb_s", [128, 1024], F32)
    ident = sb("ident_s", [128, 128], F32)
    ones = sb("ones_s", [128, 128], F32)
    epst = sb("eps_s", [128, 1], F32)
    xb = sb("xb_s", [128, 8, XW], BF16)
    kT = sb("kT_s", [64, 2, XW], BF16)
    V = sb("V_s", [128, (TPB + 2) * 128], BF16)
    qT = [sb("qT%d" % i, [64, TB], BF16) for i in range(2)]
    s1 = [sb("s1_%d" % i, [128, 384], F32) for i in range(2)]
    Pn = [sb("Pn_%d" % i, [128, 384], BF16) for i in range(2)]
    Dg = [sb("Dg_%d" % i, [128, 128], BF16) for i in range(2)]
    PT = [sb("PT_%d" % i, [128, 384], BF16) for i in range(2)]
    st = [sb("st_%d" % i, [128, 8], F32) for i in range(2)]
    yaT = sb("yaT", [128, 4, TB], BF16)
    ycT = sb("ycT", [128, 4, TB], BF16)
    ygT = sb("ygT", [128, 4, TB], BF16)
    mT = sb("mT", [128, 8, TB], BF16)
    cc_s = sb("cc_s", [128, TB + 2], F32)
    u_s = sb("u_s", [128, TB + 2], F32)
    t_s = sb("t_s", [128, TB], F32)
    of_s = [sb("of_0", [128, TB], F32)] * 2
    ob_s = [sb("ob_0", [128, TB], F32)] * 2
    sq_s, rinv, silr = cc_s, u_s, t_s
    g_s = [sb("g_%d" % i, [128, TB], F32) for i in range(2)]
    accm = [sb("accm0", [128, TB], F32)] * 4
    tmp = sb("tmp_s", [128, TB], F32)
    xt = [sb("xt_%d" % i, [128, 1024], F32) for i in range(2)]
    bst = sb("bst", [128, 2, 6], F32)
    mv = sb("mv", [128, 2], F32)
    rstd = sb("rstd", [128, 1], F32)
    x1T = sb("x1T", [128, 1024], F32)
    lg = sb("lg", [128, 16], F32)
    lst = sb("lst", [128, 4], F32)
    affs = [sb("affs%d" % i, [128, 16], F32) for i in range(2)]
    psb = [nc.alloc_psum_tensor("psb%d" % i, [128, 512], F32) for i in range(8)]
    bank = [0]
    ceng = [0]

    def nb():
        i = bank[0]
        bank[0] = (i + 1) % 8
        return psb[i], "psb%d" % i

    def cast_eng():
        ceng[0] += 1
        return "pool" if ceng[0] % 2 else "act"

    deferred = []

    def load_w(dst, dkey, src, nk, ncols, c_from=0, defer=False):
        sv = src.rearrange("(kc p) n -> p kc n", p=128)
        for c0 in range(c_from, ncols, 512):
            w = min(512, ncols - c0)

            def piece(c0=c0, w=w, pool_only=defer):
                P.dma(stage[:, 0:nk, 0:w], sv[:, :, c0:c0 + w], writes=["stage"])
                for kc in range(nk):
                    P.copy("pool" if pool_only else cast_eng(), dst[:, kc, c0:c0 + w], stage[:, kc, 0:w], reads=["stage"], writes=[dkey])
            if defer:
                deferred.append(piece)
            else:
                piece()

    P.dma(Wr[:], Wr_d.rearrange("(kc p) n -> p kc n", p=128), writes=["Wr"])
    P.dma(maskb[:], maskb_d.rearrange("m p k -> p m k"), writes=["maskb"])
    for (t, d, k) in ((dist, dist_d, "dist"), (sink, sink_d, "sink"), (convw, convw_d, "convw"), (normg, normg_d, "normg"),
                      (lng, lng_d, "lng"), (lnb, lnb_d, "lnb"), (ident, ident_d, "ident")):
        P.dma(t[:], d, writes=[k])
    P.memset("dve", ones[:], 1.0, writes=["ones"])
    P.memset("dve", epst[:], 1e-6, writes=["eps"])
    epsln = sb("epsln_s", [128, 1], F32)
    P.memset("dve", epsln[:], 1e-5, writes=["epsln"])
    load_w(Wa, "Wa", Wa_d, 8, 1024)
    load_w(Wa, "Wa2", Wa_d, 8, NWA, c_from=1024, defer=True)
    load_w(Wba, "Wba", Wba_d, 4, 1024, defer=True)
    load_w(Wbc, "Wbc", Wbc_d, 4, 1024, defer=True)
    load_w(Wbg, "Wbg", Wbg_d, 4, 1024, defer=True)
    load_w(Wo, "Wo", Wo_d, 8, 1024, defer=True)
    outs = []
    WmgV = Wmg_d.rearrange("(kc p) n -> p kc n", p=128)
    xTv = xTh.rearrange("(kc p) t -> p kc t", p=128)
    mgcnt = [0]
    acnt = [0]

    for b in range(nblocks):
        for ps_ in range(2):
            for kk in range(4):
                P.dma(stage2[:, XW * kk:XW * kk + XW], xTv[:, 4 * ps_ + kk, TB * b:TB * b + XW], writes=["stage"])
            for kk in range(4):
                P.copy(cast_eng(), xb[:, 4 * ps_ + kk, :], stage2[:, XW * kk:XW * kk + XW], reads=["stage"], writes=["xb"])
        XM = lambda kc: xb[:, kc, 128:128 + TB]
        for g in range(2):
            for (c0, n) in ((0, 512), (512, XW - 512)):
                pb, pk = nb()
                for kc in range(8):
                    P.mm(pb[0:64, 0:n], Wa[:, kc, C_AK + 64 * g:C_AK + 64 * g + 64], xb[:, kc, c0:c0 + n],
                         start=(kc == 0), stop=(kc == 7), reads=["Wa", "xb"], writes=[pk])
                P.copy("act", kT[:, g, c0:c0 + n], pb[0:64, 0:n], reads=[pk], writes=["kT"])
        for half in range(2):
            pb, pk = nb()
            for tl in range(3):
                ti = 3 * half + tl
                for kc in range(8):
                    P.mm(pb[:, 128 * tl:128 * tl + 128], xb[:, kc, 128 * ti:128 * ti + 128], Wa[:, kc, C_AV:C_AV + 128],
                         start=(kc == 0), stop=(kc == 7), reads=["Wa", "xb"], writes=[pk])
            P.copy("act", V[:, 384 * half:384 * half + 384], pb[:, 0:384], reads=[pk], writes=["V"])
        def qproj(h):
            qi = h % 2
            pb, pk = nb()
            for kc in range(8):
                P.mm(pb[0:64, 0:TB], Wa[:, kc, C_AQ + 64 * h:C_AQ + 64 * h + 64], XM(kc), start=(kc == 0), stop=(kc == 7),
                     reads=["Wa", "xb"], writes=[pk])
            P.copy("act", qT[qi][:], pb[0:64, 0:TB], reads=[pk], writes=["qT%d" % qi])

        def stage_a(h, i, j):
            g = h // 4
            qi = h % 2
            J = str(j)
            gt = TPB * b + i
            mi = 0 if gt == 0 else (2 if gt == NT // 128 - 1 else 1)
            pb, pk = nb()
            P.mm(pb[:, 0:384], qT[qi][:, 128 * i:128 * i + 128], kT[:, g, 128 * i:128 * i + 384], reads=["qT%d" % qi, "kT"], writes=[pk])
            P.stt("dve", s1[j][:], pb[:, 0:384], 0.125, maskb[:, mi, :], ALU.mult, ALU.add, reads=[pk, "maskb"], writes=["s1" + J])
            P.stt("dve", s1[j][:], dist[:], -SLOPES[h], s1[j][:], ALU.mult, ALU.add, reads=["dist", "s1" + J], writes=["s1" + J])
            P.op("dve", lambda v, o=st[j][:, 0:1], a=s1[j][:]: v.reduce_max(o, a, AX.X), reads=["s1" + J], writes=["st" + J])
            P.ts("dve", st[j][:, 1:2], st[j][:, 0:1], sink[:, h:h + 1], -1.0, ALU.max, ALU.mult, reads=["st" + J, "sink"], writes=["st" + J])
            P.act(Pn[j][:], s1[j][:], AF.Exp, bias=st[j][:, 1:2], accum_out=st[j][:, 2:3], reads=["s1" + J, "st" + J], writes=["Pn" + J, "st" + J])
            P.act(st[j][:, 3:4], sink[:, h:h + 1], AF.Exp, bias=st[j][:, 1:2], reads=["sink", "st" + J], writes=["st" + J])
            P.tt("dve", st[j][:, 4:5], st[j][:, 2:3], st[j][:, 3:4], ALU.add, reads=["st" + J], writes=["st" + J])
            P.op("dve", lambda v, o=st[j][:, 5:6], a=st[j][:, 4:5]: v.reciprocal(o, a), reads=["st" + J], writes=["st" + J])
            P.ts("dve", Dg[j][:], ident[:], st[j][:, 5:6], None, ALU.mult, reads=["ident", "st" + J], writes=["Dg" + J])

        def stage_b(h, i, j):
            g = h // 4
            J = str(j)
            pt_, ptk = nb()
            for k in range(3):
                P.mm(pt_[:, 128 * k:128 * k + 128], Pn[j][:, 128 * k:128 * k + 128], Dg[j][:], reads=["Pn" + J, "Dg" + J], writes=[ptk])
            P.copy("act", PT[j][:], pt_[:, 0:384], reads=[ptk], writes=["PT" + J])
            pb2, pk2 = nb()
            po = 64 * (h % 2)
            for k in range(3):
                P.mm(pb2[po:po + 64, 0:128], V[:, 128 * (i + k) + 64 * g:128 * (i + k) + 64 * g + 64], PT[j][:, 128 * k:128 * k + 128],
                     start=(k == 0), stop=(k == 2), reads=["V", "PT" + J], writes=[pk2])
            P.copy("act", yaT[po:po + 64, h // 2, 128 * i:128 * i + 128], pb2[po:po + 64, 0:128], reads=[pk2], writes=["yaT"])

        qproj(0)
        prev = None
        for h in range(8):
            for i in range(TPB):
                j = acnt[0] % 2
                acnt[0] += 1
                if i == 0 and h + 1 < 8:
                    qproj(h + 1)
                stage_a(h, i, j)
                if prev is not None:
                    stage_b(*prev)
                prev = (h, i, j)
                if i in (0, 2):
                    if deferred:
                        deferred.pop(0)()
        stage_b(*prev)
        while deferred:
            deferred.pop(0)()
        for fc in range(4):
            pch, kch = nb()
            pcc, kcc = nb()
            pcb, kcb = nb()
            ph, kph = nb()
            for (pbk, kk, c0) in ((pch, kch, C_CH), (pcc, kcc, C_CC)):
                for kc in range(8):
                    P.mm(pbk[:, 0:TB], Wa[:, kc, c0 + 128 * fc:c0 + 128 * fc + 128], xb[:, kc, 127:127 + TB],
                         start=(kc == 0), stop=(kc == 7), reads=["Wa", "Wa2", "xb"], writes=[kk])
            for (c0, o0) in ((C_CH, 0), (C_CC, 2)):
                for kc in range(8):
                    P.mm(ph[:, o0:o0 + 2], Wa[:, kc, c0 + 128 * fc:c0 + 128 * fc + 128], xb[:, kc, 127 + TB:129 + TB],
                         start=(kc == 0), stop=(kc == 7), reads=["Wa", "Wa2", "xb"], writes=[kph])
            for kc in range(8):
                P.mm(pcb[:, 0:TB], Wa[:, kc, C_CB + 128 * fc:C_CB + 128 * fc + 128], XM(kc),
                     start=(kc == 0), stop=(kc == 7), reads=["Wa", "Wa2", "xb"], writes=[kcb])
            P.copy("act", cc_s[:, 0:TB], pcc[:, 0:TB], reads=[kcc], writes=["cc_s"])
            P.copy("act", cc_s[:, TB:TB + 2], ph[:, 2:4], reads=[kph], writes=["cc_s"])
            P.tt("dve", u_s[:, 0:TB], pch[:, 0:TB], cc_s[:, 0:TB], ALU.mult, reads=[kch, "cc_s"], writes=["u_s"])
            P.tt("dve", u_s[:, TB:TB + 2], ph[:, 0:2], cc_s[:, TB:TB + 2], ALU.mult, reads=[kph, "cc_s"], writes=["u_s"])
            P.ts("dve", t_s[:], u_s[:, 0:TB], convw[:, fc, 0:1], None, ALU.mult, reads=["u_s", "convw"], writes=["t_s"])
            P.stt("dve", t_s[:], u_s[:, 1:TB + 1], convw[:, fc, 1:2], t_s[:], ALU.mult, ALU.add, reads=["u_s", "convw", "t_s"], writes=["t_s"])
            P.stt("dve", t_s[:], u_s[:, 2:TB + 2], convw[:, fc, 2:3], t_s[:], ALU.mult, ALU.add, reads=["u_s", "convw", "t_s"], writes=["t_s"])
            P.tt("dve", ycT[:, fc, :], pcb[:, 0:TB], t_s[:], ALU.mult, reads=[kcb, "t_s"], writes=["ycT"])
        for h in range(4):
            P.dma(of_s[0][:], oTf[128 * h:128 * h + 128, TB * b:TB * b + TB], writes=["of"])
            P.dma(ob_s[0][:], oTb[128 * h:128 * h + 128, TB * b:TB * b + TB], writes=["ob"])
            P.tt("pool", of_s[0][:], of_s[0][:], ob_s[0][:], ALU.add, reads=["of", "ob"], writes=["of"])
            P.act(sq_s[:, 0:TB], of_s[0][:], AF.Square, reads=["of"], writes=["cc_s"])
            pb, pk = nb()
            P.mm(pb[:, 0:TB], ones[:], sq_s[:, 0:TB], reads=["ones", "cc_s"], writes=[pk])
            P.act(rinv[:, 0:TB], pb[:, 0:TB], AF.Sqrt, scale=1.0 / 128.0, bias=epst[:, 0:1], reads=[pk, "eps"], writes=["u_s"])
            P.op("dve", lambda v, o=rinv[:, 0:TB]: v.reciprocal(o, o), reads=["u_s"], writes=["u_s"])
            pr, pkr = nb()
            for kc in range(8):
                P.mm(pr[:, 0:TB], Wa[:, kc, C_GR + 128 * h:C_GR + 128 * h + 128], XM(kc), start=(kc == 0), stop=(kc == 7),
                     reads=["Wa", "Wa2", "xb"], writes=[pkr])
            P.act(silr[:], pr[:, 0:TB], AF.Silu, reads=[pkr], writes=["t_s"])
            P.stt("dve", rinv[:, 0:TB], of_s[0][:], normg[:, h:h + 1], rinv[:, 0:TB], ALU.mult, ALU.mult, reads=["of", "normg", "u_s"], writes=["u_s"])
            P.tt("dve", ygT[:, h, :], rinv[:, 0:TB], silr[:], ALU.mult, reads=["u_s", "t_s"], writes=["ygT"])
        for c in range(8):
            wi = mgcnt[0] % 2
            mgcnt[0] += 1
            for br in range(3):
                P.dma(stage[:, :, 128 * br:128 * br + 128], WmgV[:, :, 1024 * br + 128 * c:1024 * br + 128 * c + 128], writes=["stage"])
            for kc in range(8):
                P.copy(cast_eng(), Wmg[wi][:, kc, :], stage[:, kc, 0:384], reads=["stage"], writes=["Wmg%d" % wi])
            for br in range(3):
                gi = br % 2
                pg, kg = nb()
                for kc in range(8):
                    P.mm(pg[:, 0:TB], Wmg[wi][:, kc, 128 * br:128 * br + 128], XM(kc), start=(kc == 0), stop=(kc == 7),
                         reads=["Wmg%d" % wi, "xb"], writes=[kg])
                P.act(g_s[gi][:], pg[:, 0:TB], AF.Sigmoid, reads=[kg], writes=["g_s%d" % gi])
                pbr, kbr = nb()
                Wb_, yT_, wk, yk = ((Wba, yaT, "Wba", "yaT"), (Wbc, ycT, "Wbc", "ycT"), (Wbg, ygT, "Wbg", "ygT"))[br]
                for fk in range(4):
                    P.mm(pbr[:, 0:TB], Wb_[:, fk, 128 * c:128 * c + 128], yT_[:, fk, :], start=(fk == 0), stop=(fk == 3),
                         reads=[wk, yk], writes=[kbr])
                if br == 0:
                    P.tt("dve", accm[0][:], g_s[gi][:], pbr[:, 0:TB], ALU.mult, reads=["g_s%d" % gi, kbr], writes=["accm"])
                else:
                    P.tt("dve", tmp[:], g_s[gi][:], pbr[:, 0:TB], ALU.mult, reads=["g_s%d" % gi, kbr], writes=["tmp"])
                    if br == 1:
                        P.tt("pool", accm[0][:], accm[0][:], tmp[:], ALU.add, reads=["accm", "tmp"], writes=["accm"])
                    else:
                        P.tt("pool", mT[:, c, :], accm[0][:], tmp[:], ALU.add, reads=["accm", "tmp"], writes=["mT"])
        for i in range(TPB):
            gt = TPB * b + i
            j = gt % 2
            J = str(j)
            P.dma(xt[j][:], xtok[128 * gt:128 * gt + 128, :], writes=["xt" + J])
            for hf in range(2):
                pb, pk = nb()
                for c in range(8):
                    P.mm(pb[:, :], mT[:, c, 128 * i:128 * i + 128], Wo[:, c, 512 * hf:512 * hf + 512], start=(c == 0), stop=(c == 7),
                         reads=["mT", "Wo"], writes=[pk])
                P.stt("dve", xt[j][:, 512 * hf:512 * hf + 512], xt[j][:, 512 * hf:512 * hf + 512], ALPHA, pb[:, :], ALU.mult, ALU.add,
                      reads=["xt" + J, pk], writes=["xt" + J])
            layer_norm(P, xt[j], "xt" + J, bst, mv, rstd, lng, lnb, "lng", "lnb", epsln)
            outs.append(P.dma(x1_d[128 * gt:128 * gt + 128, :], xt[j][:], reads=["xt" + J], writes=["x1_d"]))
            for hf in range(2):
                pb, pk = nb()
                for kq in range(4):
                    kc = 4 * hf + kq
                    P.mm(pb[:, 128 * kq:128 * kq + 128], xt[j][:, 128 * kc:128 * kc + 128], ident[:], reads=["xt" + J, "ident"], writes=[pk])
                P.copy("act", x1T[:, 512 * hf:512 * hf + 512], pb[:, :], reads=[pk], writes=["x1T"])
            pb, pk = nb()
            for kc in range(8):
                P.mm(pb[:, 0:16], x1T[:, 128 * kc:128 * kc + 128], Wr[:, kc, :], start=(kc == 0), stop=(kc == 7), reads=["x1T", "Wr"], writes=[pk])
            P.copy("act", lg[:], pb[:, 0:16], reads=[pk], writes=["lg"])
            P.op("dve", lambda v, o=lst[:, 0:1], a=lg[:]: v.reduce_max(o, a, AX.X), reads=["lg"], writes=["lst"])
            P.ts("dve", lst[:, 1:2], lst[:, 0:1], -1.0, None, ALU.mult, reads=["lst"], writes=["lst"])
            P.act(lg[:], lg[:], AF.Exp, bias=lst[:, 1:2], accum_out=lst[:, 2:3], reads=["lg", "lst"], writes=["lg", "lst"])
            P.op("dve", lambda v, o=lst[:, 3:4], a=lst[:, 2:3]: v.reciprocal(o, a), reads=["lst"], writes=["lst"])
            P.ts("dve", affs[j][:], lg[:], lst[:, 3:4], None, ALU.mult, reads=["lg", "lst"], writes=["affs" + J])
            outs.append(P.dma(aff_d[128 * gt:128 * gt + 128, :], affs[j][:], reads=["affs" + J], writes=["aff_d"]))
    P.emit(final_wait_ops=outs)
    return nc


def a_consts(core):
    q = np.arange(128)[:, None]
    ko = np.arange(384)[None, :] - 128
    dist = np.abs(q - ko).astype(np.float32)
    base = np.where(dist <= 128, 0.0, -30000.0).astype(np.float32)
    mk = np.stack([base, base, base]).copy()
    if core == 0:
        mk[0][:, 0:128] = -30000.0
    if core == 7:
        mk[2][:, 256:384] = -30000.0
    return dict(maskb=mk, dist=dist, ident=np.eye(128, dtype=np.float32))


def a_inputs(core, x_l, xT_l, oTf_full, oTb_full, w):
    S = x_l.shape[0]
    t0 = core * NT
    xTh = np.zeros((1024, NT + 256), np.float32)
    lo, hi = max(t0 - 128, 0), min(t0 + NT + 128, S)
    xTh[:, lo - (t0 - 128):hi - (t0 - 128)] = xT_l[:, lo:hi]
    w_in = w["w_in"]
    Wa = np.concatenate([w_in[:, 0:2304], w_in[:, 3328:3840]], axis=1)
    r = dict(xTh=xTh, xtok=np.ascontiguousarray(x_l[t0:t0 + NT]),
             oTf=np.ascontiguousarray(oTf_full[:, t0:t0 + NT]), oTb=np.ascontiguousarray(oTb_full[:, t0:t0 + NT]),
             Wa=np.ascontiguousarray(Wa), Wmg=np.ascontiguousarray(w_in[:, 3872:6944]),
             Wba=w["w_branch_attn"], Wbc=w["w_branch_conv"], Wbg=w["w_branch_gla"], Wo=w["w_out"], Wr=w["router_w"],
             sink=np.ascontiguousarray(np.broadcast_to(w["attn_sink"][None, :], (128, 8))),
             convw=np.ascontiguousarray(w["conv_w"].reshape(3, 4, 128).transpose(2, 1, 0)),
             normg=np.ascontiguousarray(w["gla_norm_g"].reshape(4, 128).T),
             lng=np.ascontiguousarray(np.broadcast_to(w["ln_mix_g"][None, :], (128, 1024))),
             lnb=np.ascontiguousarray(np.broadcast_to(w["ln_mix_b"][None, :], (128, 1024))))
    r.update(a_consts(core))
    return r


S_E = 16384
CAP = 2048
NITER = 36


def build_E(stop_after=None):
    nc = bass.Bass("TRN2", target_bir_lowering=False)
    din = lambda n, s, d=F32: nc.dram_tensor(n, s, d, kind="ExternalInput").ap()
    aff_d = din("affT", [2, 128, 128])
    x1_d = din("x1", [S_E, 1024])
    Wg_d = din("Wg", [2, 1024, 2048])
    Wu_d = din("Wu", [2, 1024, 2048])
    Wd_d = din("Wd", [2, 2048, 1024])
    iota_d = din("iota", [128, 512])
    Lc_d = din("Lc", [128, 128, 4])
    U_d = din("U", [128, 128])
    SU_d = din("SU", [128, 128])
    ident_d = din("ident", [128, 128])
    dense = nc.dram_tensor("dense", [S_E, 1024], F32, kind="ExternalOutput").ap()
    dbg = nc.dram_tensor("dbg", [2, 128, 32], F32, kind="ExternalOutput").ap()

    P = Prog(nc)
    sb = lambda n, s, d: nc.alloc_sbuf_tensor(n, s, d)
    stage = sb("stage", [128, 8, 512], F32)
    Wg = sb("Wg_s", [128, 8, 2048], BF16)
    Wu = sb("Wu_s", [128, 8, 2048], BF16)
    Wd = sb("Wd_s", [128, 16, 1024], BF16)
    xsT = sb("xsT", [128, 8, CAP], BF16)
    hT = sb("hT", [128, 16, 512], BF16)
    iota = sb("iota_s", [128, 512], F32)
    Lf = sb("Lf", [128, 128, 4], F32)
    L = sb("L", [128, 128, 4], BF16)
    U = sb("U_s", [128, 128], BF16)
    SU = sb("SU_s", [128, 128], F32)
    ident = sb("ident_s", [128, 128], F32)
    identb = sb("identb_s", [128, 128], BF16)
    ones = sb("ones_s", [128, 128], F32)
    A2 = [sb("A_s%d" % i, [128, 128], F32) for i in range(2)]
    cmp_ = sb("cmp_s", [128, 128], F32)
    sm = sb("sm_s", [128, 16], F32)
    mk = sb("mk_s", [128, 128], F32)
    mkb = sb("mkb_s", [128, 128], BF16)
    mTb = sb("mTb_s", [128, 128], BF16)
    posm = sb("posm_s", [128, 128], F32)
    poss = sb("poss_s", [128, 128], F32)
    hib = sb("hib_s", [128, 128], BF16)
    hif = sb("hif_s", [128, 128], F32)
    OH = [sb("OH%d" % i, [128, 512], BF16) for i in range(4)]
    R = sb("R_s", [4, 512], F32)
    RT = sb("RT_s", [128, 16, 4], F32)
    tokf = sb("tokf", [128, 16], F32)
    idx = sb("idx_s", [128, 16], I32)
    gate = sb("gate_s", [128, 16], F32)
    xg = [sb("xg%d" % i, [128, 1024], F32) for i in range(2)]
    xgb = sb("xgb", [128, 1024], BF16)
    sg = [sb("sg%d" % i, [128, 512], F32) for i in range(2)]
    ysb = [sb("ysb%d" % i, [128, 1024], F32) for i in range(2)]
    cur = xg
    psb = [nc.alloc_psum_tensor("psb%d" % i, [128, 512], F32) for i in range(8)]
    bank = [0]
    ceng = [0]

    def nb():
        i = bank[0]
        bank[0] = (i + 1) % 8
        return psb[i], "psb%d" % i

    cast_mode = ["act"]

    def cast_eng():
        if cast_mode[0] == "act":
            return "act"
        ceng[0] += 1
        return "dve" if ceng[0] % 2 else "act"

    def load_w(dst, dkey, sv, nk, kc0, ncols):
        for c0 in range(0, ncols, 512):
            P.dma(stage[:, 0:nk, :], sv[:, kc0:kc0 + nk, c0:c0 + 512], writes=["stage"])
            for kc in range(nk):
                P.copy(cast_eng(), dst[:, kc0 + kc, c0:c0 + 512], stage[:, kc, :], reads=["stage"], writes=[dkey])

    P.dma(iota[:], iota_d, writes=["iota"])
    P.dma(Lf[:], Lc_d, writes=["Lf"])
    P.dma(SU[:], SU_d, writes=["SU"])
    P.dma(ident[:], ident_d, writes=["ident"])
    P.dma(cmp_[:], U_d, writes=["cmp"])
    P.copy("dve", U[:], cmp_[:], reads=["cmp"], writes=["U"])
    P.copy("dve", identb[:], ident[:], reads=["ident"], writes=["identb"])
    P.memset("dve", ones[:], 1.0, writes=["ones"])
    for e in range(2):
        P.dma(A2[e][:], aff_d[e], writes=["A%d" % e])
    for kc in range(8):
        P.memset("dve", stage[:, kc, :], 0.0, writes=["stage"])
    dv = dense.rearrange("(n p) d -> n p d", p=128)
    zkeys = []
    zops = []
    for z in range(32):
        zk = "dz%d" % z
        zops.append(P.dma(dense[512 * z:512 * z + 512, :].rearrange("(p r) d -> p r d", p=128), stage[:, :, :], reads=["stage"], writes=[zk]))
        zkeys.append(zk)
    outs = list(zops)

    def load_weights(e):
        load_w(Wg, "Wg", Wg_d[e].rearrange("(kc p) n -> p kc n", p=128), 8, 0, 2048)
        load_w(Wu, "Wu", Wu_d[e].rearrange("(kc p) n -> p kc n", p=128), 8, 0, 2048)
        WdV = Wd_d[e].rearrange("(kc p) n -> p kc n", p=128)
        load_w(Wd, "Wd", WdV, 8, 0, 1024)
        load_w(Wd, "Wd", WdV, 8, 8, 1024)

    load_weights(0)
    sckeys = []
    for e in range(2):
        E = str(e)
        A = A2[e]
        P.memset("dve", sm[:, 0:1], 0.0, writes=["sm"])
        P.memset("dve", sm[:, 1:2], 1.0, writes=["sm"])
        for it in range(NITER):
            P.ts("dve", sm[:, 2:3], sm[:, 0:1], sm[:, 1:2], 0.5, ALU.add, ALU.mult, reads=["sm"], writes=["sm"])
            P.ts("dve", cmp_[:], A[:], sm[:, 2:3], None, ALU.is_ge, reads=["A" + E, "sm"], writes=["cmp"])
            P.op("dve", lambda v, o=sm[:, 3:4], a=cmp_[:]: v.reduce_sum(o, a, AX.X), reads=["cmp"], writes=["sm"])
            pb, pk = nb()
            P.mm(pb[:, 0:1], ones[:], sm[:, 3:4], reads=["ones", "sm"], writes=[pk])
            P.ts("dve", sm[:, 4:5], pb[:, 0:1], float(CAP), None, ALU.is_ge, reads=[pk], writes=["sm"])
            P.stt("dve", sm[:, 0:1], sm[:, 4:5], sm[:, 2:3], sm[:, 0:1], ALU.mult, ALU.max, reads=["sm"], writes=["sm"])
            P.stt("dve", sm[:, 5:6], sm[:, 4:5], 2.0, sm[:, 2:3], ALU.mult, ALU.add, reads=["sm"], writes=["sm"])
            P.tt("dve", sm[:, 1:2], sm[:, 1:2], sm[:, 5:6], ALU.min, reads=["sm"], writes=["sm"])
        P.ts("dve", mk[:], A[:], sm[:, 0:1], None, ALU.is_ge, reads=["A" + E, "sm"], writes=["mk"])
        P.copy("dve", mkb[:], mk[:], reads=["mk"], writes=["mkb"])
        pb, pk = nb()
        P.mm(pb[:, 0:128], mkb[:], identb[:], reads=["mkb", "identb"], writes=[pk])
        P.copy("dve", mTb[:], pb[:, 0:128], reads=[pk], writes=["mTb"])
        pc, pck = nb()
        P.mm(pc[:, 0:128], mTb[:], U[:], reads=["mTb", "U"], writes=[pck])
        P.copy("dve", posm[:], pc[:, 0:128], reads=[pck], writes=["posm"])
        pb, pk = nb()
        P.mm(pb[:, 0:1], SU[:], posm[:, 127:128], reads=["SU", "posm"], writes=[pk])
        P.copy("dve", sm[:, 7:8], pb[:, 0:1], reads=[pk], writes=["sm"])
        P.ts("dve", posm[:], posm[:], sm[:, 7:8], None, ALU.add, reads=["posm", "sm"], writes=["posm"])
        P.tt("dve", posm[:], posm[:], mk[:], ALU.mult, reads=["posm", "mk"], writes=["posm"])
        P.ts("dve", posm[:], posm[:], -1.0, None, ALU.add, reads=["posm"], writes=["posm"])
        P.copy("dve", hib[:], A[:], reads=["A" + E], writes=["hib"])
        P.copy("dve", hif[:], hib[:], reads=["hib"], writes=["hif"])
        P.tt("dve", hif[:], A[:], hif[:], ALU.subtract, reads=["A" + E, "hif"], writes=["hif"])
        P.copy("dve", L[:, :, 0:2], Lf[:, :, 0:2], reads=["Lf"], writes=["L"])
        P.copy("dve", L[:, :, 2], hib[:], reads=["hib"], writes=["L"])
        P.copy("dve", L[:, :, 3], hif[:], reads=["hif"], writes=["L"])
        ohc = 0
        for sblk in range(4):
            P.ts("dve", poss[:], posm[:], -512.0 * sblk, None, ALU.add, reads=["posm"], writes=["poss"])
            pr, prk = nb()
            for j in range(128):
                o = ohc % 4
                ohc += 1
                P.ts("dve", OH[o][:], iota[:], poss[:, j:j + 1], None, ALU.is_equal, reads=["iota", "poss"], writes=["OH%d" % o])
                P.mm(pr[0:4, :], L[:, j, :], OH[o][:], start=(j == 0), stop=(j == 127), reads=["L", "OH%d" % o], writes=[prk])
            P.copy("dve", R[:, :], pr[0:4, :], reads=[prk], writes=["R"])
            pb, pk = nb()
            for k in range(4):
                P.mm(pb[:, 4 * k:4 * k + 4], R[:, 128 * k:128 * k + 128], ident[0:4, 0:4], reads=["R", "ident"], writes=[pk])
            for k in range(4):
                P.copy("dve", RT[:, 4 * sblk + k, :], pb[:, 4 * k:4 * k + 4], reads=[pk], writes=["RT"])
        P.stt("dve", tokf[:], RT[:, :, 0], 128.0, RT[:, :, 1], ALU.mult, ALU.add, reads=["RT"], writes=["tokf"])
        P.copy("dve", idx[:], tokf[:], reads=["tokf"], writes=["idx"])
        P.tt("dve", gate[:], RT[:, :, 2], RT[:, :, 3], ALU.add, reads=["RT"], writes=["gate"])
        outs.append(P.dma(dbg[e, :, 0:16], tokf[:], reads=["tokf"], writes=["dbg%d" % e]))
        outs.append(P.dma(dbg[e, :, 16:32], gate[:], reads=["gate"], writes=["dbg%db" % e]))
        if stop_after == "idx":
            continue
        if e == 1:
            cast_mode[0] = "both"
            load_weights(1)
        for blk in range(16):
            j = blk % 2
            J = str(j)
            P.dma_fn(lambda g, o=xg[j][:], off=idx[:, blk:blk + 1]: g.indirect_dma_start(
                out=o, out_offset=None, in_=x1_d, in_offset=bass.IndirectOffsetOnAxis(ap=off, axis=0)),
                reads=["idx"], writes=["xg" + J])
            P.copy("act", xgb[:, 0:512], xg[j][:, 0:512], reads=["xg" + J], writes=["xgb"])
            P.copy("dve", xgb[:, 512:1024], xg[j][:, 512:1024], reads=["xg" + J], writes=["xgb"])
            for hf in range(2):
                pb, pk = nb()
                for kq in range(4):
                    kc = 4 * hf + kq
                    P.mm(pb[:, 128 * kq:128 * kq + 128], xgb[:, 128 * kc:128 * kc + 128], identb[:], reads=["xgb", "identb"], writes=[pk])
                for kq in range(4):
                    P.copy("act" if kq % 2 else "dve", xsT[:, 4 * hf + kq, 128 * blk:128 * blk + 128],
                           pb[:, 128 * kq:128 * kq + 128], reads=[pk], writes=["xsT"])
        for jb in range(4):
            js = slice(512 * jb, 512 * jb + 512)
            for fb in range(16):
                gi = fb % 2
                pg, kg = nb()
                pu, ku = nb()
                for kc in range(8):
                    P.mm(pg[:, :], Wg[:, kc, 128 * fb:128 * fb + 128], xsT[:, kc, js], start=(kc == 0), stop=(kc == 7),
                         reads=["Wg", "xsT"], writes=[kg])
                for kc in range(8):
                    P.mm(pu[:, :], Wu[:, kc, 128 * fb:128 * fb + 128], xsT[:, kc, js], start=(kc == 0), stop=(kc == 7),
                         reads=["Wu", "xsT"], writes=[ku])
                P.act(sg[gi][:], pg[:, :], AF.Silu, reads=[kg], writes=["sg%d" % gi])
                P.tt("dve", hT[:, fb, :], sg[gi][:], pu[:, :], ALU.mult, reads=["sg%d" % gi, ku], writes=["hT"])
            for sub in range(4):
                blk = 4 * jb + sub
                yi = blk % 2
                Y = str(yi)
                if e == 1:
                    P.dma_fn(lambda g, o=cur[yi][:], off=idx[:, blk:blk + 1]: g.indirect_dma_start(
                        out=o, out_offset=None, in_=dense, in_offset=bass.IndirectOffsetOnAxis(ap=off, axis=0)),
                        reads=["idx"] + sckeys + zkeys, writes=["xg" + Y])
                for hf in range(2):
                    pb, pk = nb()
                    for fb in range(16):
                        P.mm(pb[:, :], hT[:, fb, 128 * sub:128 * sub + 128], Wd[:, fb, 512 * hf:512 * hf + 512], start=(fb == 0), stop=(fb == 15),
                             reads=["hT", "Wd"], writes=[pk])
                    if e == 0:
                        P.ts("dve", ysb[yi][:, 512 * hf:512 * hf + 512], pb[:, :], gate[:, blk:blk + 1], None, ALU.mult,
                             reads=[pk, "gate"], writes=["ysb" + Y])
                    else:
                        P.stt("dve", ysb[yi][:, 512 * hf:512 * hf + 512], pb[:, :], gate[:, blk:blk + 1], cur[yi][:, 512 * hf:512 * hf + 512],
                              ALU.mult, ALU.add, reads=[pk, "gate", "xg" + Y], writes=["ysb" + Y])
                sk = "dsc%d_%d" % (e, blk)
                op = P.dma_fn(lambda g, i_=ysb[yi][:], off=idx[:, blk:blk + 1]: g.indirect_dma_start(
                    out=dense, out_offset=bass.IndirectOffsetOnAxis(ap=off, axis=0), in_=i_, in_offset=None),
                    reads=["idx", "ysb" + Y] + (zkeys if e == 0 else []), writes=[sk])
                outs.append(op)
                if e == 0:
                    sckeys.append(sk)
    P.emit(final_wait_ops=outs)
    return nc


def e_consts():
    p = np.arange(128)[:, None]
    j = np.arange(128)[None, :]
    Lc = np.zeros((128, 128, 4), np.float32)
    Lc[:, :, 0] = p
    Lc[:, :, 1] = j
    return dict(iota=np.ascontiguousarray(np.broadcast_to(np.arange(512, dtype=np.float32)[None, :], (128, 512))),
                Lc=Lc, U=(p <= j).astype(np.float32), SU=(p < j).astype(np.float32), ident=np.eye(128, dtype=np.float32))


def e_inputs(core, aff_full, x1_full, w):
    es = slice(2 * core, 2 * core + 2)
    r = dict(affT=np.ascontiguousarray(aff_full[:, es].T.reshape(2, 128, 128)), x1=x1_full,
             Wg=np.ascontiguousarray(w["expert_w_gate"][es]), Wu=np.ascontiguousarray(w["expert_w_up"][es]),
             Wd=np.ascontiguousarray(w["expert_w_down"][es]))
    r.update(e_consts())
    return r


def build_C():
    nc = bass.Bass("TRN2", target_bir_lowering=False)
    din = lambda n, s, d=F32: nc.dram_tensor(n, s, d, kind="ExternalInput").ap()
    part = din("part", [8, NT, 1024])
    x1 = din("x1", [NT, 1024])
    lng_d = din("lng", [128, 1024])
    lnb_d = din("lnb", [128, 1024])
    out = nc.dram_tensor("x2", [NT, 1024], F32, kind="ExternalOutput").ap()
    P = Prog(nc)
    sb = lambda n, s, d: nc.alloc_sbuf_tensor(n, s, d)
    lng = sb("lng_s", [128, 1024], F32)
    lnb = sb("lnb_s", [128, 1024], F32)
    epsln = sb("epsln_s", [128, 1], F32)
    xt = [sb("xt%d" % i, [128, 1024], F32) for i in range(2)]
    pt = [[sb("pt%d_%d" % (i, k), [128, 1024], F32) for k in range(8)] for i in range(2)]
    bst = sb("bst", [128, 2, 6], F32)
    mv = sb("mv", [128, 2], F32)
    rstd = sb("rstd", [128, 1], F32)
    P.dma(lng[:], lng_d, writes=["lng"])
    P.dma(lnb[:], lnb_d, writes=["lnb"])
    P.memset("dve", epsln[:], 1e-5, writes=["epsln"])
    outs = []
    for t in range(NT // 128):
        i = t % 2
        I = str(i)
        rows = slice(128 * t, 128 * t + 128)
        P.dma(xt[i][:], x1[rows, :], writes=["xt" + I])
        for k in range(8):
            P.dma(pt[i][k][:], part[k, rows, :], writes=["pt%s_%d" % (I, k)])
        P.tt("pool", pt[i][0][:], pt[i][0][:], pt[i][1][:], ALU.add, reads=["pt%s_0" % I, "pt%s_1" % I], writes=["pt%s_0" % I])
        P.tt("dve", pt[i][2][:], pt[i][2][:], pt[i][3][:], ALU.add, reads=["pt%s_2" % I, "pt%s_3" % I], writes=["pt%s_2" % I])
        P.tt("pool", pt[i][4][:], pt[i][4][:], pt[i][5][:], ALU.add, reads=["pt%s_4" % I, "pt%s_5" % I], writes=["pt%s_4" % I])
        P.tt("dve", pt[i][6][:], pt[i][6][:], pt[i][7][:], ALU.add, reads=["pt%s_6" % I, "pt%s_7" % I], writes=["pt%s_6" % I])
        P.tt("pool", pt[i][0][:], pt[i][0][:], pt[i][2][:], ALU.add, reads=["pt%s_0" % I, "pt%s_2" % I], writes=["pt%s_0" % I])
        P.tt("dve", pt[i][4][:], pt[i][4][:], pt[i][6][:], ALU.add, reads=["pt%s_4" % I, "pt%s_6" % I], writes=["pt%s_4" % I])
        P.tt("dve", pt[i][0][:], pt[i][0][:], pt[i][4][:], ALU.add, reads=["pt%s_0" % I, "pt%s_4" % I], writes=["pt%s_0" % I])
        P.stt("dve", xt[i][:], xt[i][:], ALPHA, pt[i][0][:], ALU.mult, ALU.add, reads=["xt" + I, "pt%s_0" % I], writes=["xt" + I])
        layer_norm(P, xt[i], "xt" + I, bst, mv, rstd, lng, lnb, "lng", "lnb", epsln)
        outs.append(P.dma(out[rows, :], xt[i][:], reads=["xt" + I], writes=["x2"]))
    P.emit(final_wait_ops=outs)
    return nc


def c_inputs(core, dense_list, x1_full, w):
    s = slice(core * NT, core * NT + NT)
    return dict(part=np.ascontiguousarray(np.stack([d[s] for d in dense_list])), x1=np.ascontiguousarray(x1_full[s]),
                lng=np.ascontiguousarray(np.broadcast_to(w["ln_ffn_g"][None, :], (128, 1024))),
                lnb=np.ascontiguousarray(np.broadcast_to(w["ln_ffn_b"][None, :], (128, 1024))))


_NC = {}


def _get(name, fn):
    if name not in _NC:
        _NC[name] = fn()
    return _NC[name]


def _run(nc, maps):
    res = run_bass_kernel_spmd(nc, maps, core_ids=list(range(8)))
    return res.results


def kernel(**inputs):
    inputs = {k: np.asarray(v) for k, v in inputs.items()}
    x = np.ascontiguousarray(inputs["x"][0], dtype=np.float32)
    depth = inputs["w_in"].shape[0]
    for l in range(depth):
        w = {k: np.ascontiguousarray(v[l]) for k, v in inputs.items() if k != "x"}
        xT = np.ascontiguousarray(x.T)
        xTr = np.ascontiguousarray(xT[:, ::-1])
        maps = []
        for c in range(8):
            m = g_inputs(xT, w["w_in"], w["gla_gate_w2"], w["gla_gate_b"], c % 4, 0)
            if c // 4 == 1:
                m = g_inputs(xTr, w["w_in"], w["gla_gate_w2"], w["gla_gate_b"], c % 4, 1, flipped=True)
            maps.append(m)
        res = _run(_get("G", build_G), maps)
        oTf = np.concatenate([res[h]["oT"] for h in range(4)], axis=0)
        oTb = np.concatenate([res[4 + h]["oT"][:, ::-1] for h in range(4)], axis=0)
        del maps, res
        maps = [a_inputs(c, x, xT, oTf, oTb, w) for c in range(8)]
        res = _run(_get("A", build_A), maps)
        x1 = np.concatenate([res[c]["x1"] for c in range(8)], axis=0)
        aff = np.concatenate([res[c]["aff"] for c in range(8)], axis=0)
        del maps, res
        maps = [e_inputs(c, aff, x1, w) for c in range(8)]
        res = _run(_get("E", build_E), maps)
        dense = [res[c]["dense"] for c in range(8)]
        del maps, res
        maps = [c_inputs(c, dense, x1, w) for c in range(8)]
        res = _run(_get("C", build_C), maps)
        x = np.concatenate([res[c]["x2"] for c in range(8)], axis=0)
        del maps, res, dense
    return x[None].astype(np.float32)
```

```python
import numpy as np
import concourse.bass as bass
import concourse.mybir as mybir
from concourse.bass_utils import run_bass_kernel_spmd

F32 = mybir.dt.float32
BF16 = mybir.dt.bfloat16
I32 = mybir.dt.int32
ALU = mybir.AluOpType
AF = mybir.ActivationFunctionType
AX = mybir.AxisListType

ENGS = ("pe", "act", "dve", "pool", "sp")
import os
SKIP_SAME_ENGINE = bool(int(os.environ.get('SKIP_SAME', '0')))
NDSEM = 48
NHW = 32


class Prog:
    def __init__(self, nc):
        self.nc = nc
        self.ops = {e: [] for e in ENGS}
        self.sem = {e: nc.alloc_semaphore("prog_" + e) for e in ("pe", "act", "dve", "pool")}
        self.dsems = [nc.alloc_semaphore("dsem%d" % i) for i in range(NDSEM)]
        self.dcount = [0] * NDSEM
        self.dnext = {"hw": 0, "sw": 0}
        self.last_w = {}
        self.readers = {}
        self.allops = []
        self.out_dma_ops = []

    def _add(self, eng, fn, reads, writes, dma=False):
        op = dict(eng=eng, fn=fn, deps=[], signal=False, dma=dma, idx=len(self.allops))
        writes = list(writes) + [k for k in reads if k.startswith("ps") and k not in writes]
        reads = [k for k in reads if not k.startswith("ps")]
        deps = []
        for k in reads:
            w = self.last_w.get(k)
            if w is not None:
                deps.append(w)
        for k in writes:
            w = self.last_w.get(k)
            if w is not None:
                deps.append(w)
            for r in self.readers.get(k, ()):
                deps.append(r)
        if dma:
            if eng == "pool":
                i = NHW + self.dnext["sw"]
                self.dnext["sw"] = (self.dnext["sw"] + 1) % (NDSEM - NHW)
            else:
                i = self.dnext["hw"]
                self.dnext["hw"] = (self.dnext["hw"] + 1) % NHW
            prev = getattr(self, "_dprev", {}).get(i)
            if prev is not None:
                deps.append(prev)
            if not hasattr(self, "_dprev"):
                self._dprev = {}
            self._dprev[i] = op
            self.dcount[i] += 1
            op["dsem"] = i
            op["dval"] = 16 * self.dcount[i]
            op["signal"] = True
        seen = set()
        for d in deps:
            if d is op or id(d) in seen:
                continue
            seen.add(id(d))
            if d["eng"] == "pe" and eng == "pe" and not d["dma"] and not dma:
                continue
            if SKIP_SAME_ENGINE and d["eng"] == eng and not d["dma"] and not dma:
                continue
            op["deps"].append(d)
            d["signal"] = True
        for k in reads:
            self.readers.setdefault(k, []).append(op)
        for k in writes:
            self.last_w[k] = op
            self.readers[k] = []
        self.ops[eng].append(op)
        self.allops.append(op)
        return op

    def op(self, eng, fn, reads=(), writes=()):
        return self._add(eng, fn, list(reads), list(writes))

    def dma(self, out, in_, reads=(), writes=(), q="sp", **kw):
        return self._add(q, lambda e: e.dma_start(out=out, in_=in_, **kw), list(reads), list(writes), dma=True)

    def dma_fn(self, fn, reads=(), writes=(), q="pool"):
        return self._add(q, fn, list(reads), list(writes), dma=True)

    def mm(self, out, lhsT, rhs, start=True, stop=True, reads=(), writes=()):
        return self.op("pe", lambda t: t.matmul(out, lhsT, rhs, start=start, stop=stop), reads, writes)

    def tr(self, out, in_, ident, reads=(), writes=()):
        return self.op("pe", lambda t: t.transpose(out, in_, ident), reads, writes)

    def act(self, out, in_, func, reads=(), writes=(), **kw):
        return self.op("act", lambda a: a.activation(out=out, in_=in_, func=func, **kw), reads, writes)

    def copy(self, eng, out, in_, reads=(), writes=()):
        if eng == "act":
            return self.op("act", lambda a: a.copy(out, in_), reads, writes)
        return self.op(eng, lambda v: v.tensor_copy(out, in_), reads, writes)

    def tt(self, eng, out, in0, in1, op, reads=(), writes=()):
        return self.op(eng, lambda v: v.tensor_tensor(out, in0, in1, op), reads, writes)

    def stt(self, eng, out, in0, scalar, in1, op0, op1, reads=(), writes=()):
        return self.op(eng, lambda v: v.scalar_tensor_tensor(out, in0, scalar, in1, op0, op1), reads, writes)

    def ts(self, eng, out, in0, s1, s2, op0, op1=None, reads=(), writes=(), **kw):
        if op1 is None:
            return self.op(eng, lambda v: v.tensor_scalar(out, in0, s1, s2, op0, **kw), reads, writes)
        return self.op(eng, lambda v: v.tensor_scalar(out, in0, s1, s2, op0, op1, **kw), reads, writes)

    def memset(self, eng, ap, val, reads=(), writes=()):
        return self.op(eng, lambda v: v.memset(ap, val), reads, writes)

    def emit(self, final_wait_ops=()):
        cnt = {e: 0 for e in self.sem}
        for op in self.allops:
            if op["dma"]:
                op["ticket"] = (self.dsems[op["dsem"]], op["dval"])
            elif op["signal"]:
                cnt[op["eng"]] += 1
                op["ticket"] = (self.sem[op["eng"]], cnt[op["eng"]])
        nc = self.nc
        final_tickets = [op["ticket"] for op in final_wait_ops]

        def run(engname, engobj):
            waited = {}
            for op in self.ops[engname]:
                need = {}
                for d in op["deps"]:
                    s, v = d["ticket"]
                    if need.get(s, (None, 0))[1] < v:
                        need[s] = (s, v)
                for s, v in need.values():
                    if waited.get(s, 0) >= v:
                        continue
                    engobj.wait_ge(s, v)
                    waited[s] = v
                ins = op["fn"](engobj)
                if op["signal"]:
                    s, v = op["ticket"]
                    ins.then_inc(s, 16 if op["dma"] else 1)
            if engname == "sp":
                for s, v in final_tickets:
                    if waited.get(s, 0) < v:
                        engobj.wait_ge(s, v)
                        waited[s] = v

        with nc.Block() as block:
            @block.sync
            def _(e):
                run("sp", e)

            @block.tensor
            def _(e):
                run("pe", e)

            @block.scalar
            def _(e):
                run("act", e)

            @block.vector
            def _(e):
                run("dve", e)

            @block.gpsimd
            def _(e):
                run("pool", e)


S_ = 16384
BLK = 1024
NBLK = S_ // BLK


def build_G(nblk=NBLK, ssalloc=None):
    nc = bass.Bass("TRN2", target_bir_lowering=False)
    SS = ssalloc or nblk * BLK
    xT = nc.dram_tensor("xT", [1024, SS], F32, kind="ExternalInput").ap()
    W = nc.dram_tensor("W", [1024, 272], F32, kind="ExternalInput").ap()
    W2 = nc.dram_tensor("W2", [32, 64], F32, kind="ExternalInput").ap()
    U16d = nc.dram_tensor("U16", [128, 128], F32, kind="ExternalInput").ap()
    UL16d = nc.dram_tensor("UL16", [128, 128], F32, kind="ExternalInput").ap()
    UMd = nc.dram_tensor("UM", [128, 128], F32, kind="ExternalInput").ap()
    oT = nc.dram_tensor("oT", [128, SS], F32, kind="ExternalOutput").ap()
    P = Prog(nc)
    sb = lambda n, s, d: nc.alloc_sbuf_tensor(n, s, d)
    Wb = sb("Wb", [128, 8, 272], BF16)
    W2s = sb("W2s", [32, 64], F32)
    U16 = sb("U16s", [128, 128], F32)
    UL16 = sb("UL16s", [128, 128], F32)
    UM = sb("UMs", [128, 128], F32)
    xb = [sb("xb%d" % i, [128, 8, BLK], BF16) for i in range(2)]
    xf = [sb("xf%d" % i, [128, 8, BLK], F32) for i in range(2)]
    qT = [sb("qT%d" % i, [64, BLK], F32) for i in range(2)]
    kT = [sb("kT%d" % i, [64, BLK], F32) for i in range(2)]
    lrT = [sb("lrT%d" % i, [32, BLK], F32) for i in range(2)]
    ob = [sb("ob%d" % i, [128, BLK], F32) for i in range(2)]
    e1 = [sb("e1_%d" % i, [128, 64], F32) for i in range(2)]
    sp = [sb("sp_%d" % i, [128, 64], F32) for i in range(2)]
    eq = [sb("eq_%d" % i, [64, 128], F32) for i in range(2)]
    ek = [sb("ek_%d" % i, [64, 128], F32) for i in range(2)]
    ekh = [sb("ekh_%d" % i, [128, 64], F32) for i in range(2)]
    QT = [sb("QT_%d" % i, [64, 128], BF16) for i in range(2)]
    KT = [sb("KT_%d" % i, [64, 128], BF16) for i in range(2)]
    Kh = [sb("Kh_%d" % i, [128, 64], BF16) for i in range(2)]
    V = [sb("V_%d" % i, [128, 128], BF16) for i in range(2)]
    ATm = [sb("ATm_%d" % i, [128, 128], BF16) for i in range(2)]
    Sf = sb("Sf", [64, 128], F32)
    Sb = sb("Sb", [64, 128], BF16)
    ps = [nc.alloc_psum_tensor("ps%d" % i, [128, 512], F32) for i in range(8)]

    P.dma(Wb[:], W.rearrange("(kc p) n -> p kc n", p=128), writes=["Wb"], q="pool")
    P.dma(W2s[:], W2, writes=["W2s"])
    P.dma(U16[:], U16d, writes=["U16"])
    P.dma(UL16[:], UL16d, writes=["UL16"])
    P.dma(UM[:], UMd, writes=["UM"])
    P.memset("dve", Sf[:], 0.0, writes=["Sf"])
    P.memset("dve", Sb[:], 0.0, writes=["Sb"])
    for i in range(2):
        P.memset("pool", lrT[i][:], 1.0, writes=["lrT%d" % i])
    xTv = xT.rearrange("(kc p) t -> p kc t", p=128)
    outs = []

    def load_blk(b):
        i = b % 2
        P.dma(xf[i][:], xTv[:, :, b * BLK:(b + 1) * BLK], writes=["xf%d" % i])
        for kc in range(8):
            P.copy("pool" if kc % 2 else "act", xb[i][:, kc, :], xf[i][:, kc, :], reads=["xf%d" % i], writes=["xb%d" % i])

    load_blk(0)
    for b in range(nblk):
        i = b % 2
        if b + 1 < nblk:
            load_blk(b + 1)
        xk = "xb%d" % i
        for sbk in range(BLK // 512):
            cs = slice(sbk * 512, (sbk + 1) * 512)
            for (name, c0, c1, pst, dst, m) in (("q", 0, 64, ps[0], qT[i], 64), ("k", 64, 128, ps[1], kT[i], 64),
                                                 ("lr", 256, 272, ps[2], lrT[i], 16)):
                for kc in range(8):
                    P.mm(pst[0:m, :], Wb[:, kc, c0:c1], xb[i][:, kc, cs], start=(kc == 0), stop=(kc == 7),
                         reads=["Wb", xk], writes=["ps_" + name])
                P.copy("act", dst[0:m, cs], pst[0:m, :], reads=["ps_" + name], writes=["%sT%d" % (name, i)])
        def chunk_vars(b, c):
            n = b * (BLK // 128) + c
            j = n % 2
            return n, j, slice(c * 128, (c + 1) * 128), ps[3 + j], ps[5 + j], "psA%d" % j, "psB%d" % j, str(j)

        def stage1(b, c):
            i = b % 2
            xk = "xb%d" % i
            n, j, cols, pA, pC, kA, kB, J = chunk_vars(b, c)
            for kc in range(8):
                P.mm(pA[:, 0:192], xb[i][:, kc, cols], Wb[:, kc, 64:256], start=(kc == 0), stop=(kc == 7),
                     reads=["Wb", xk], writes=[kA])
            P.mm(pA[:, 192:256], lrT[i][:, cols], W2s[:], reads=["lrT%d" % i, "W2s"], writes=[kA])
            P.act(e1[j][:], pA[:, 192:256], AF.Exp, scale=-1.0, reads=[kA], writes=["e1" + J])
            P.act(sp[j][:], e1[j][:], AF.Ln, bias=1.0, reads=["e1" + J], writes=["sp" + J])
            P.mm(pA[:, 320:384], UL16[:], sp[j][:], reads=["UL16", "sp" + J], writes=[kA])
            P.mm(pA[0:64, 384:512], sp[j][:], U16[:], reads=["U16", "sp" + J], writes=[kA])
            P.act(eq[j][:], pA[0:64, 384:512], AF.Exp, scale=-1.0, reads=[kA], writes=["eq" + J])
            P.act(ek[j][:], pA[0:64, 384:512], AF.Exp, scale=1.0, reads=[kA], writes=["ek" + J])
            P.act(ekh[j][:], pA[:, 320:384], AF.Exp, scale=1.0, reads=[kA], writes=["ekh" + J])
            P.stt("dve", QT[j][:], qT[i][:, cols], 0.125, eq[j][:], ALU.mult, ALU.mult,
                  reads=["qT%d" % i, "eq" + J], writes=["QT" + J])
            P.tt("dve", KT[j][:], kT[i][:, cols], ek[j][:], ALU.mult, reads=["kT%d" % i, "ek" + J], writes=["KT" + J])
            P.tt("dve", Kh[j][:], pA[:, 0:64], ekh[j][:], ALU.mult, reads=[kA, "ekh" + J], writes=["Kh" + J])
            P.copy("dve", V[j][:], pA[:, 64:192], reads=[kA], writes=["V" + J])

        def stage2(b, c):
            i = b % 2
            n, j, cols, pA, pC, kA, kB, J = chunk_vars(b, c)
            P.mm(pC[:, 0:128], KT[j][:], QT[j][:], reads=["KT" + J, "QT" + J], writes=[kB])
            P.tt("dve", ATm[j][:], pC[:, 0:128], UM[:], ALU.mult, reads=[kB, "UM"], writes=["ATm" + J])
            P.mm(pC[0:64, 128:256], Kh[j][:], V[j][:], reads=["Kh" + J, "V" + J], writes=[kB])
            P.mm(pC[:, 256:384], V[j][:], ATm[j][:], start=True, stop=False, reads=["V" + J, "ATm" + J], writes=[kB])
            P.mm(pC[:, 256:384], Sb[:], QT[j][:], start=False, stop=True, reads=["Sb", "QT" + J], writes=[kB])
            P.copy("act", ob[i][:, cols], pC[:, 256:384], reads=[kB], writes=["ob%d" % i])
            P.stt("dve", Sf[:], Sf[:], eq[j][:, 127:128], pC[0:64, 128:256], ALU.mult, ALU.add,
                  reads=["Sf", "eq" + J, kB], writes=["Sf"])
            P.copy("pool", Sb[:], Sf[:], reads=["Sf"], writes=["Sb"])

        NCH = BLK // 128
        stage1(b, 0)
        for c in range(NCH):
            if c + 1 < NCH:
                stage1(b, c + 1)
            stage2(b, c)
        outs.append(P.dma(oT[:, b * BLK:(b + 1) * BLK], ob[i][:], reads=["ob%d" % i], writes=["oT"]))
    P.emit(final_wait_ops=outs)
    return nc


def g_consts():
    s = np.arange(128)[:, None]
    t = np.arange(128)[None, :]
    U = (s <= t).astype(np.float32)
    return dict(U16=U / 16.0, UL16=-(s > t).astype(np.float32) / 16.0, UM=U)


def g_inputs(xT_full, w_in_l, w2_l, b_l, h, d, flipped=False):
    xT = xT_full[:, ::-1] if (d == 1 and not flipped) else xT_full
    cq = 2304 + 64 * h
    ck = 2560 + 64 * h
    cv = 2816 + 128 * h
    clr = 3840 + 16 * d
    W = np.concatenate([w_in_l[:, cq:cq + 64], w_in_l[:, ck:ck + 64], w_in_l[:, cv:cv + 128], w_in_l[:, clr:clr + 16]], axis=1)
    W2 = np.zeros((32, 64), np.float32)
    W2[0:16] = w2_l[d][:, 64 * h:64 * h + 64]
    W2[16] = b_l[d][64 * h:64 * h + 64]
    r = dict(xT=np.ascontiguousarray(xT), W=np.ascontiguousarray(W), W2=W2)
    r.update(g_consts())
    return r


NT = 2048
TB = 512
XW = TB + 256
TPB = TB // 128
NB_A = NT // TB
ALPHA = 4.0 ** 0.25
SLOPES = [2.0 ** (-(h + 1)) for h in range(8)]
C_AQ, C_AK, C_AV, C_CH, C_CB, C_CC, C_GR = 0, 512, 640, 768, 1280, 1792, 2304
NWA = 2816


def layer_norm(P, t, tk, bst, mv, rstd, lng, lnb, gk, bk, epsln):
    for hf in range(2):
        P.op("dve", lambda v, o=bst[:, hf, :], a=t[:, 512 * hf:512 * hf + 512]: v.bn_stats(o, a), reads=[tk], writes=["bst"])
    P.op("dve", lambda v, o=mv[:], a=bst[:]: v.bn_aggr(o, a), reads=["bst"], writes=["mv"])
    P.act(rstd[:], mv[:, 1:2], AF.Sqrt, bias=epsln[:, 0:1], reads=["mv", "epsln"], writes=["rstd"])
    P.op("dve", lambda v, o=rstd[:]: v.reciprocal(o, o), reads=["rstd"], writes=["rstd"])
    P.ts("dve", t[:], t[:], mv[:, 0:1], rstd[:, 0:1], ALU.subtract, ALU.mult, reads=[tk, "mv", "rstd"], writes=[tk])
    P.tt("dve", t[:], t[:], lng[:], ALU.mult, reads=[tk, gk], writes=[tk])
    P.tt("pool", t[:], t[:], lnb[:], ALU.add, reads=[tk, bk], writes=[tk])


def build_A(nblocks=NB_A):
    nc = bass.Bass("TRN2", target_bir_lowering=False)
    din = lambda n, s, d=F32: nc.dram_tensor(n, s, d, kind="ExternalInput").ap()
    xTh = din("xTh", [1024, NT + 256])
    xtok = din("xtok", [NT, 1024])
    oTf = din("oTf", [512, NT])
    oTb = din("oTb", [512, NT])
    Wa_d = din("Wa", [1024, NWA])
    Wmg_d = din("Wmg", [1024, 3072])
    Wba_d = din("Wba", [512, 1024])
    Wbc_d = din("Wbc", [512, 1024])
    Wbg_d = din("Wbg", [512, 1024])
    Wo_d = din("Wo", [1024, 1024])
    Wr_d = din("Wr", [1024, 16])
    maskb_d = din("maskb", [3, 128, 384])
    dist_d = din("dist", [128, 384])
    sink_d = din("sink", [128, 8])
    convw_d = din("convw", [128, 4, 3])
    normg_d = din("normg", [128, 4])
    lng_d = din("lng", [128, 1024])
    lnb_d = din("lnb", [128, 1024])
    ident_d = din("ident", [128, 128])
    x1_d = nc.dram_tensor("x1", [NT, 1024], F32, kind="ExternalOutput").ap()
    aff_d = nc.dram_tensor("aff", [NT, 16], F32, kind="ExternalOutput").ap()

    P = Prog(nc)
    sb = lambda n, s, d: nc.alloc_sbuf_tensor(n, s, d)
    stage2 = sb("stage", [128, 4096], F32)
    stage = stage2[:].rearrange("p (a b) -> p a b", b=512)
    Wa = sb("Wa_s", [128, 8, NWA], BF16)
    Wmg = [sb("Wmg%d" % i, [128, 8, 384], BF16) for i in range(2)]
    Wba = sb("Wba_s", [128, 4, 1024], BF16)
    Wbc = sb("Wbc_s", [128, 4, 1024], BF16)
    Wbg = sb("Wbg_s", [128, 4, 1024], BF16)
    Wo = sb("Wo_s", [128, 8, 1024], BF16)
    Wr = sb("Wr_s", [128, 8, 16], F32)
    maskb = sb("maskb_s", [128, 3, 384], F32)
    dist = sb("dist_s", [128, 384], F32)
    sink = sb("sink_s", [128, 8], F32)
    convw = sb("convw_s", [128, 4, 3], F32)
    normg = sb("normg_s", [128, 4], F32)
    lng = sb("lng_s", [128, 1024], F32)
    lnb = sb("lnb_s", [128, 1024], F32)
    ident = sb("ident_s", [128, 128], F32)
    ones = sb("ones_s", [128, 128], F32)
    epst = sb("eps_s", [128, 1], F32)
    xb = sb("xb_s", [128, 8, XW], BF16)
    kT = sb("kT_s", [64, 2, XW], BF16)
    V = sb("V_s", [128, (TPB + 2) * 128], BF16)
    qT = [sb("qT%d" % i, [64, TB], BF16) for i in range(2)]
    s1 = [sb("s1_%d" % i, [128, 384], F32) for i in range(2)]
    Pn = [sb("Pn_%d" % i, [128, 384], BF16) for i in range(2)]
    Dg = [sb("Dg_%d" % i, [128, 128], BF16) for i in range(2)]
    PT = [sb("PT_%d" % i, [128, 384], BF16) for i in range(2)]
    st = [sb("st_%d" % i, [128, 8], F32) for i in range(2)]
    yaT = sb("yaT", [128, 4, TB], BF16)
    ycT = sb("ycT", [128, 4, TB], BF16)
    ygT = sb("ygT", [128, 4, TB], BF16)
    mT = sb("mT", [128, 8, TB], BF16)
    cc_s = sb("cc_s", [128, TB + 2], F32)
    u_s = sb("u_s", [128, TB + 2], F32)
    t_s = sb("t_s", [128, TB], F32)
    of_s = [sb("of_0", [128, TB], F32)] * 2
    ob_s = [sb("ob_0", [128, TB], F32)] * 2
    sq_s, rinv, silr = cc_s, u_s, t_s
    g_s = [sb("g_%d" % i, [128, TB], F32) for i in range(2)]
    accm = [sb("accm0", [128, TB], F32)] * 4
    tmp = sb("tmp_s", [128, TB], F32)
    xt = [sb("xt_%d" % i, [128, 1024], F32) for i in range(2)]
    bst = sb("bst", [128, 2, 6], F32)
    mv = sb("mv", [128, 2], F32)
    rstd = sb("rstd", [128, 1], F32)
    x1T = sb("x1T", [128, 1024], F32)
    lg = sb("lg", [128, 16], F32)
    lst = sb("lst", [128, 4], F32)
    affs = [sb("affs%d" % i, [128, 16], F32) for i in range(2)]
    psb = [nc.alloc_psum_tensor("psb%d" % i, [128, 512], F32) for i in range(8)]
    bank = [0]
    ceng = [0]

    def nb():
        i = bank[0]
        bank[0] = (i + 1) % 8
        return psb[i], "psb%d" % i

    def cast_eng():
        ceng[0] += 1
        return "pool" if ceng[0] % 2 else "act"

    deferred = []

    def load_w(dst, dkey, src, nk, ncols, c_from=0, defer=False):
        sv = src.rearrange("(kc p) n -> p kc n", p=128)
        for c0 in range(c_from, ncols, 512):
            w = min(512, ncols - c0)

            def piece(c0=c0, w=w, pool_only=defer):
                P.dma(stage[:, 0:nk, 0:w], sv[:, :, c0:c0 + w], writes=["stage"])
                for kc in range(nk):
                    P.copy("pool" if pool_only else cast_eng(), dst[:, kc, c0:c0 + w], stage[:, kc, 0:w], reads=["stage"], writes=[dkey])
            if defer:
                deferred.append(piece)
            else:
                piece()

    P.dma(Wr[:], Wr_d.rearrange("(kc p) n -> p kc n", p=128), writes=["Wr"])
    P.dma(maskb[:], maskb_d.rearrange("m p k -> p m k"), writes=["maskb"])
    for (t, d, k) in ((dist, dist_d, "dist"), (sink, sink_d, "sink"), (convw, convw_d, "convw"), (normg, normg_d, "normg"),
                      (lng, lng_d, "lng"), (lnb, lnb_d, "lnb"), (ident, ident_d, "ident")):
        P.dma(t[:], d, writes=[k])
    P.memset("dve", ones[:], 1.0, writes=["ones"])
    P.memset("dve", epst[:], 1e-6, writes=["eps"])
    epsln = sb("epsln_s", [128, 1], F32)
    P.memset("dve", epsln[:], 1e-5, writes=["epsln"])
    load_w(Wa, "Wa", Wa_d, 8, 1024)
    load_w(Wa, "Wa2", Wa_d, 8, NWA, c_from=1024, defer=True)
    load_w(Wba, "Wba", Wba_d, 4, 1024, defer=True)
    load_w(Wbc, "Wbc", Wbc_d, 4, 1024, defer=True)
    load_w(Wbg, "Wbg", Wbg_d, 4, 1024, defer=True)
    load_w(Wo, "Wo", Wo_d, 8, 1024, defer=True)
    outs = []
    WmgV = Wmg_d.rearrange("(kc p) n -> p kc n", p=128)
    xTv = xTh.rearrange("(kc p) t -> p kc t", p=128)
    mgcnt = [0]
    acnt = [0]

    for b in range(nblocks):
        for ps_ in range(2):
            for kk in range(4):
                P.dma(stage2[:, XW * kk:XW * kk + XW], xTv[:, 4 * ps_ + kk, TB * b:TB * b + XW], writes=["stage"])
            for kk in range(4):
                P.copy(cast_eng(), xb[:, 4 * ps_ + kk, :], stage2[:, XW * kk:XW * kk + XW], reads=["stage"], writes=["xb"])
        XM = lambda kc: xb[:, kc, 128:128 + TB]
        for g in range(2):
            for (c0, n) in ((0, 512), (512, XW - 512)):
                pb, pk = nb()
                for kc in range(8):
                    P.mm(pb[0:64, 0:n], Wa[:, kc, C_AK + 64 * g:C_AK + 64 * g + 64], xb[:, kc, c0:c0 + n],
                         start=(kc == 0), stop=(kc == 7), reads=["Wa", "xb"], writes=[pk])
                P.copy("act", kT[:, g, c0:c0 + n], pb[0:64, 0:n], reads=[pk], writes=["kT"])
        for half in range(2):
            pb, pk = nb()
            for tl in range(3):
                ti = 3 * half + tl
                for kc in range(8):
                    P.mm(pb[:, 128 * tl:128 * tl + 128], xb[:, kc, 128 * ti:128 * ti + 128], Wa[:, kc, C_AV:C_AV + 128],
                         start=(kc == 0), stop=(kc == 7), reads=["Wa", "xb"], writes=[pk])
            P.copy("act", V[:, 384 * half:384 * half + 384], pb[:, 0:384], reads=[pk], writes=["V"])
        def qproj(h):
            qi = h % 2
            pb, pk = nb()
            for kc in range(8):
                P.mm(pb[0:64, 0:TB], Wa[:, kc, C_AQ + 64 * h:C_AQ + 64 * h + 64], XM(kc), start=(kc == 0), stop=(kc == 7),
                     reads=["Wa", "xb"], writes=[pk])
            P.copy("act", qT[qi][:], pb[0:64, 0:TB], reads=[pk], writes=["qT%d" % qi])

        def stage_a(h, i, j):
            g = h // 4
            qi = h % 2
            J = str(j)
            gt = TPB * b + i
            mi = 0 if gt == 0 else (2 if gt == NT // 128 - 1 else 1)
            pb, pk = nb()
            P.mm(pb[:, 0:384], qT[qi][:, 128 * i:128 * i + 128], kT[:, g, 128 * i:128 * i + 384], reads=["qT%d" % qi, "kT"], writes=[pk])
            P.stt("dve", s1[j][:], pb[:, 0:384], 0.125, maskb[:, mi, :], ALU.mult, ALU.add, reads=[pk, "maskb"], writes=["s1" + J])
            P.stt("dve", s1[j][:], dist[:], -SLOPES[h], s1[j][:], ALU.mult, ALU.add, reads=["dist", "s1" + J], writes=["s1" + J])
            P.op("dve", lambda v, o=st[j][:, 0:1], a=s1[j][:]: v.reduce_max(o, a, AX.X), reads=["s1" + J], writes=["st" + J])
            P.ts("dve", st[j][:, 1:2], st[j][:, 0:1], sink[:, h:h + 1], -1.0, ALU.max, ALU.mult, reads=["st" + J, "sink"], writes=["st" + J])
            P.act(Pn[j][:], s1[j][:], AF.Exp, bias=st[j][:, 1:2], accum_out=st[j][:, 2:3], reads=["s1" + J, "st" + J], writes=["Pn" + J, "st" + J])
            P.act(st[j][:, 3:4], sink[:, h:h + 1], AF.Exp, bias=st[j][:, 1:2], reads=["sink", "st" + J], writes=["st" + J])
            P.tt("dve", st[j][:, 4:5], st[j][:, 2:3], st[j][:, 3:4], ALU.add, reads=["st" + J], writes=["st" + J])
            P.op("dve", lambda v, o=st[j][:, 5:6], a=st[j][:, 4:5]: v.reciprocal(o, a), reads=["st" + J], writes=["st" + J])
            P.ts("dve", Dg[j][:], ident[:], st[j][:, 5:6], None, ALU.mult, reads=["ident", "st" + J], writes=["Dg" + J])

        def stage_b(h, i, j):
            g = h // 4
            J = str(j)
            pt_, ptk = nb()
            for k in range(3):
                P.mm(pt_[:, 128 * k:128 * k + 128], Pn[j][:, 128 * k:128 * k + 128], Dg[j][:], reads=["Pn" + J, "Dg" + J], writes=[ptk])
            P.copy("act", PT[j][:], pt_[:, 0:384], reads=[ptk], writes=["PT" + J])
            pb2, pk2 = nb()
            po = 64 * (h % 2)
            for k in range(3):
                P.mm(pb2[po:po + 64, 0:128], V[:, 128 * (i + k) + 64 * g:128 * (i + k) + 64 * g + 64], PT[j][:, 128 * k:128 * k + 128],
                     start=(k == 0), stop=(k == 2), reads=["V", "PT" + J], writes=[pk2])
            P.copy("act", yaT[po:po + 64, h // 2, 128 * i:128 * i + 128], pb2[po:po + 64, 0:128], reads=[pk2], writes=["yaT"])

        qproj(0)
        prev = None
        for h in range(8):
            for i in range(TPB):
                j = acnt[0] % 2
                acnt[0] += 1
                if i == 0 and h + 1 < 8:
                    qproj(h + 1)
                stage_a(h, i, j)
                if prev is not None:
                    stage_b(*prev)
                prev = (h, i, j)
                if i in (0, 2):
                    if deferred:
                        deferred.pop(0)()
        stage_b(*prev)
        while deferred:
            deferred.pop(0)()
        for fc in range(4):
            pch, kch = nb()
            pcc, kcc = nb()
            pcb, kcb = nb()
            ph, kph = nb()
            for (pbk, kk, c0) in ((pch, kch, C_CH), (pcc, kcc, C_CC)):
                for kc in range(8):
                    P.mm(pbk[:, 0:TB], Wa[:, kc, c0 + 128 * fc:c0 + 128 * fc + 128], xb[:, kc, 127:127 + TB],
                         start=(kc == 0), stop=(kc == 7), reads=["Wa", "Wa2", "xb"], writes=[kk])
            for (c0, o0) in ((C_CH, 0), (C_CC, 2)):
                for kc in range(8):
                    P.mm(ph[:, o0:o0 + 2], Wa[:, kc, c0 + 128 * fc:c0 + 128 * fc + 128], xb[:, kc, 127 + TB:129 + TB],
                         start=(kc == 0), stop=(kc == 7), reads=["Wa", "Wa2", "xb"], writes=[kph])
            for kc in range(8):
                P.mm(pcb[:, 0:TB], Wa[:, kc, C_CB + 128 * fc:C_CB + 128 * fc + 128], XM(kc),
                     start=(kc == 0), stop=(kc == 7), reads=["Wa", "Wa2", "xb"], writes=[kcb])
            P.copy("act", cc_s[:, 0:TB], pcc[:, 0:TB], reads=[kcc], writes=["cc_s"])
            P.copy("act", cc_s[:, TB:TB + 2], ph[:, 2:4], reads=[kph], writes=["cc_s"])
            P.tt("dve", u_s[:, 0:TB], pch[:, 0:TB], cc_s[:, 0:TB], ALU.mult, reads=[kch, "cc_s"], writes=["u_s"])
            P.tt("dve", u_s[:, TB:TB + 2], ph[:, 0:2], cc_s[:, TB:TB + 2], ALU.mult, reads=[kph, "cc_s"], writes=["u_s"])
            P.ts("dve", t_s[:], u_s[:, 0:TB], convw[:, fc, 0:1], None, ALU.mult, reads=["u_s", "convw"], writes=["t_s"])
            P.stt("dve", t_s[:], u_s[:, 1:TB + 1], convw[:, fc, 1:2], t_s[:], ALU.mult, ALU.add, reads=["u_s", "convw", "t_s"], writes=["t_s"])
            P.stt("dve", t_s[:], u_s[:, 2:TB + 2], convw[:, fc, 2:3], t_s[:], ALU.mult, ALU.add, reads=["u_s", "convw", "t_s"], writes=["t_s"])
            P.tt("dve", ycT[:, fc, :], pcb[:, 0:TB], t_s[:], ALU.mult, reads=[kcb, "t_s"], writes=["ycT"])
        for h in range(4):
            P.dma(of_s[0][:], oTf[128 * h:128 * h + 128, TB * b:TB * b + TB], writes=["of"])
            P.dma(ob_s[0][:], oTb[128 * h:128 * h + 128, TB * b:TB * b + TB], writes=["ob"])
            P.tt("pool", of_s[0][:], of_s[0][:], ob_s[0][:], ALU.add, reads=["of", "ob"], writes=["of"])
            P.act(sq_s[:, 0:TB], of_s[0][:], AF.Square, reads=["of"], writes=["cc_s"])
            pb, pk = nb()
            P.mm(pb[:, 0:TB], ones[:], sq_s[:, 0:TB], reads=["ones", "cc_s"], writes=[pk])
            P.act(rinv[:, 0:TB], pb[:, 0:TB], AF.Sqrt, scale=1.0 / 128.0, bias=epst[:, 0:1], reads=[pk, "eps"], writes=["u_s"])
            P.op("dve", lambda v, o=rinv[:, 0:TB]: v.reciprocal(o, o), reads=["u_s"], writes=["u_s"])
            pr, pkr = nb()
            for kc in range(8):
                P.mm(pr[:, 0:TB], Wa[:, kc, C_GR + 128 * h:C_GR + 128 * h + 128], XM(kc), start=(kc == 0), stop=(kc == 7),
                     reads=["Wa", "Wa2", "xb"], writes=[pkr])
            P.act(silr[:], pr[:, 0:TB], AF.Silu, reads=[pkr], writes=["t_s"])
            P.stt("dve", rinv[:, 0:TB], of_s[0][:], normg[:, h:h + 1], rinv[:, 0:TB], ALU.mult, ALU.mult, reads=["of", "normg", "u_s"], writes=["u_s"])
            P.tt("dve", ygT[:, h, :], rinv[:, 0:TB], silr[:], ALU.mult, reads=["u_s", "t_s"], writes=["ygT"])
        for c in range(8):
            wi = mgcnt[0] % 2
            mgcnt[0] += 1
            for br in range(3):
                P.dma(stage[:, :, 128 * br:128 * br + 128], WmgV[:, :, 1024 * br + 128 * c:1024 * br + 128 * c + 128], writes=["stage"])
            for kc in range(8):
                P.copy(cast_eng(), Wmg[wi][:, kc, :], stage[:, kc, 0:384], reads=["stage"], writes=["Wmg%d" % wi])
            for br in range(3):
                gi = br % 2
                pg, kg = nb()
                for kc in range(8):
                    P.mm(pg[:, 0:TB], Wmg[wi][:, kc, 128 * br:128 * br + 128], XM(kc), start=(kc == 0), stop=(kc == 7),
                         reads=["Wmg%d" % wi, "xb"], writes=[kg])
                P.act(g_s[gi][:], pg[:, 0:TB], AF.Sigmoid, reads=[kg], writes=["g_s%d" % gi])
                pbr, kbr = nb()
                Wb_, yT_, wk, yk = ((Wba, yaT, "Wba", "yaT"), (Wbc, ycT, "Wbc", "ycT"), (Wbg, ygT, "Wbg", "ygT"))[br]
                for fk in range(4):
                    P.mm(pbr[:, 0:TB], Wb_[:, fk, 128 * c:128 * c + 128], yT_[:, fk, :], start=(fk == 0), stop=(fk == 3),
                         reads=[wk, yk], writes=[kbr])
                if br == 0:
                    P.tt("dve", accm[0][:], g_s[gi][:], pbr[:, 0:TB], ALU.mult, reads=["g_s%d" % gi, kbr], writes=["accm"])
                else:
                    P.tt("dve", tmp[:], g_s[gi][:], pbr[:, 0:TB], ALU.mult, reads=["g_s%d" % gi, kbr], writes=["tmp"])
                    if br == 1:
                        P.tt("pool", accm[0][:], accm[0][:], tmp[:], ALU.add, reads=["accm", "tmp"], writes=["accm"])
                    else:
                        P.tt("pool", mT[:, c, :], accm[0][:], tmp[:], ALU.add, reads=["accm", "tmp"], writes=["mT"])
        for i in range(TPB):
            gt = TPB * b + i
            j = gt % 2
            J = str(j)
            P.dma(xt[j][:], xtok[128 * gt:128 * gt + 128, :], writes=["xt" + J])
            for hf in range(2):
                pb, pk = nb()
                for c in range(8):
                    P.mm(pb[:, :], mT[:, c, 128 * i:128 * i + 128], Wo[:, c, 512 * hf:512 * hf + 512], start=(c == 0), stop=(c == 7),
                         reads=["mT", "Wo"], writes=[pk])
                P.stt("dve", xt[j][:, 512 * hf:512 * hf + 512], xt[j][:, 512 * hf:512 * hf + 512], ALPHA, pb[:, :], ALU.mult, ALU.add,
                      reads=["xt" + J, pk], writes=["xt" + J])
            layer_norm(P, xt[j], "xt" + J, bst, mv, rstd, lng, lnb, "lng", "lnb", epsln)
            outs.append(P.dma(x1_d[128 * gt:128 * gt + 128, :], xt[j][:], reads=["xt" + J], writes=["x1_d"]))
            for hf in range(2):
                pb, pk = nb()
                for kq in range(4):
                    kc = 4 * hf + kq
                    P.mm(pb[:, 128 * kq:128 * kq + 128], xt[j][:, 128 * kc:128 * kc + 128], ident[:], reads=["xt" + J, "ident"], writes=[pk])
                P.copy("act", x1T[:, 512 * hf:512 * hf + 512], pb[:, :], reads=[pk], writes=["x1T"])
            pb, pk = nb()
            for kc in range(8):
                P.mm(pb[:, 0:16], x1T[:, 128 * kc:128 * kc + 128], Wr[:, kc, :], start=(kc == 0), stop=(kc == 7), reads=["x1T", "Wr"], writes=[pk])
            P.copy("act", lg[:], pb[:, 0:16], reads=[pk], writes=["lg"])
            P.op("dve", lambda v, o=lst[:, 0:1], a=lg[:]: v.reduce_max(o, a, AX.X), reads=["lg"], writes=["lst"])
            P.ts("dve", lst[:, 1:2], lst[:, 0:1], -1.0, None, ALU.mult, reads=["lst"], writes=["lst"])
            P.act(lg[:], lg[:], AF.Exp, bias=lst[:, 1:2], accum_out=lst[:, 2:3], reads=["lg", "lst"], writes=["lg", "lst"])
            P.op("dve", lambda v, o=lst[:, 3:4], a=lst[:, 2:3]: v.reciprocal(o, a), reads=["lst"], writes=["lst"])
            P.ts("dve", affs[j][:], lg[:], lst[:, 3:4], None, ALU.mult, reads=["lg", "lst"], writes=["affs" + J])
            outs.append(P.dma(aff_d[128 * gt:128 * gt + 128, :], affs[j][:], reads=["affs" + J], writes=["aff_d"]))
    P.emit(final_wait_ops=outs)
    return nc


def a_consts(core):
    q = np.arange(128)[:, None]
    ko = np.arange(384)[None, :] - 128
    dist = np.abs(q - ko).astype(np.float32)
    base = np.where(dist <= 128, 0.0, -30000.0).astype(np.float32)
    mk = np.stack([base, base, base]).copy()
    if core == 0:
        mk[0][:, 0:128] = -30000.0
    if core == 7:
        mk[2][:, 256:384] = -30000.0
    return dict(maskb=mk, dist=dist, ident=np.eye(128, dtype=np.float32))


def a_inputs(core, x_l, xT_l, oTf_full, oTb_full, w):
    S = x_l.shape[0]
    t0 = core * NT
    xTh = np.zeros((1024, NT + 256), np.float32)
    lo, hi = max(t0 - 128, 0), min(t0 + NT + 128, S)
    xTh[:, lo - (t0 - 128):hi - (t0 - 128)] = xT_l[:, lo:hi]
    w_in = w["w_in"]
    Wa = np.concatenate([w_in[:, 0:2304], w_in[:, 3328:3840]], axis=1)
    r = dict(xTh=xTh, xtok=np.ascontiguousarray(x_l[t0:t0 + NT]),
             oTf=np.ascontiguousarray(oTf_full[:, t0:t0 + NT]), oTb=np.ascontiguousarray(oTb_full[:, t0:t0 + NT]),
             Wa=np.ascontiguousarray(Wa), Wmg=np.ascontiguousarray(w_in[:, 3872:6944]),
             Wba=w["w_branch_attn"], Wbc=w["w_branch_conv"], Wbg=w["w_branch_gla"], Wo=w["w_out"], Wr=w["router_w"],
             sink=np.ascontiguousarray(np.broadcast_to(w["attn_sink"][None, :], (128, 8))),
             convw=np.ascontiguousarray(w["conv_w"].reshape(3, 4, 128).transpose(2, 1, 0)),
             normg=np.ascontiguousarray(w["gla_norm_g"].reshape(4, 128).T),
             lng=np.ascontiguousarray(np.broadcast_to(w["ln_mix_g"][None, :], (128, 1024))),
             lnb=np.ascontiguousarray(np.broadcast_to(w["ln_mix_b"][None, :], (128, 1024))))
    r.update(a_consts(core))
    return r


S_E = 16384
CAP = 2048
NITER = 36


def build_E(stop_after=None):
    nc = bass.Bass("TRN2", target_bir_lowering=False)
    din = lambda n, s, d=F32: nc.dram_tensor(n, s, d, kind="ExternalInput").ap()
    aff_d = din("affT", [2, 128, 128])
    x1_d = din("x1", [S_E, 1024])
    Wg_d = din("Wg", [2, 1024, 2048])
    Wu_d = din("Wu", [2, 1024, 2048])
    Wd_d = din("Wd", [2, 2048, 1024])
    iota_d = din("iota", [128, 512])
    jrow_d = din("jrow", [128, 128])
    pcol_d = din("pcol", [128, 1])
    U_d = din("U", [128, 128])
    SU_d = din("SU", [128, 128])
    ident_d = din("ident", [128, 128])
    dense = nc.dram_tensor("dense", [S_E, 1024], F32, kind="ExternalOutput").ap()
    dbg = nc.dram_tensor("dbg", [2, 128, 32], F32, kind="ExternalOutput").ap()

    affall = aff_d.rearrange("e p (j o) -> (e p j) o", o=1)
    P = Prog(nc)
    sb = lambda n, s, d: nc.alloc_sbuf_tensor(n, s, d)
    stage = sb("stage", [128, 8, 512], F32)
    Wg = sb("Wg_s", [128, 8, 2048], BF16)
    Wu = sb("Wu_s", [128, 8, 2048], BF16)
    Wd = sb("Wd_s", [128, 16, 1024], BF16)
    xsT = sb("xsT", [128, 8, CAP], BF16)
    hT = sb("hT", [128, 16, 512], BF16)
    iota = sb("iota_s", [128, 512], F32)
    U = sb("U_s", [128, 128], BF16)
    SU = sb("SU_s", [128, 128], F32)
    ident = sb("ident_s", [128, 128], F32)
    identb = sb("identb_s", [128, 128], BF16)
    ones = sb("ones_s", [128, 128], F32)
    A2 = [sb("A_s%d" % i, [128, 128], F32) for i in range(2)]
    sm = sb("sm_s", [128, 16], F32)
    mk = sb("mk_s", [128, 128], F32)
    mkb = sb("mkb_s", [128, 128], BF16)
    mTb = sb("mTb_s", [128, 128], BF16)
    posm = sb("posm_s", [128, 128], F32)
    GE2 = sb("GE2", [128, 128, 17], BF16)
    OHH2 = sb("OHH2", [128, 128, 16], BF16)
    T2 = sb("T2", [128, 128, 32], BF16)
    jrow = sb("jrow_s", [128, 128], F32)
    pcol = sb("pcol_s", [128, 1], F32)
    lo_s = sb("lo_s", [128, 128], F32)
    OHlo = [sb("OHlo%d" % i, [128, 128], BF16) for i in range(2)]
    R2 = sb("R2_s", [32, 128], F32)
    RT2 = sb("RT2_s", [128, 32], F32)
    tokf = sb("tokf", [128, 16], F32)
    tokg = sb("tokg", [128, 16], F32)
    idxg = sb("idxg_s", [128, 16], I32)
    idx = sb("idx_s", [128, 16], I32)
    gate = sb("gate_s", [128, 16], F32)
    xg = [sb("xg%d" % i, [128, 1024], F32) for i in range(2)]
    xgb = sb("xgb", [128, 1024], BF16)
    sg = [sb("sg%d" % i, [128, 512], F32) for i in range(2)]
    ysb = [sb("ysb%d" % i, [128, 1024], F32) for i in range(2)]
    cur = xg
    psb = [nc.alloc_psum_tensor("psb%d" % i, [128, 512], F32) for i in range(8)]
    bank = [0]
    ceng = [0]

    def nb():
        i = bank[0]
        bank[0] = (i + 1) % 7
        return psb[i], "psb%d" % i

    cast_mode = ["act"]

    def cast_eng():
        if cast_mode[0] == "act":
            return "act"
        ceng[0] += 1
        return "dve" if ceng[0] % 2 else "act"

    def load_w(dst, dkey, sv, nk, kc0, ncols):
        for c0 in range(0, ncols, 512):
            P.dma(stage[:, 0:nk, :], sv[:, kc0:kc0 + nk, c0:c0 + 512], writes=["stage"])
            for kc in range(nk):
                P.copy(cast_eng(), dst[:, kc0 + kc, c0:c0 + 512], stage[:, kc, :], reads=["stage"], writes=[dkey])

    P.dma(iota[:], iota_d, writes=["iota"])
    P.dma(jrow[:], jrow_d, writes=["jrow"])
    P.dma(pcol[:], pcol_d, writes=["pcol"])
    P.dma(SU[:], SU_d, writes=["SU"])
    P.dma(ident[:], ident_d, writes=["ident"])
    P.dma(mk[:], U_d, writes=["mk"])
    P.copy("dve", U[:], mk[:], reads=["mk"], writes=["U"])
    P.copy("dve", identb[:], ident[:], reads=["ident"], writes=["identb"])
    P.memset("dve", ones[:], 1.0, writes=["ones"])
    for e in range(2):
        P.dma(A2[e][:], aff_d[e], writes=["A%d" % e])
    for kc in range(8):
        P.memset("dve", stage[:, kc, :], 0.0, writes=["stage"])
    dv = dense.rearrange("(n p) d -> n p d", p=128)
    zkeys = []
    zops = []
    for z in range(32):
        zk = "dz%d" % z
        zops.append(P.dma(dense[512 * z:512 * z + 512, :].rearrange("(p r) d -> p r d", p=128), stage[:, :, :], reads=["stage"], writes=[zk]))
        zkeys.append(zk)
    outs = list(zops)

    def load_weights(e):
        load_w(Wg, "Wg", Wg_d[e].rearrange("(kc p) n -> p kc n", p=128), 8, 0, 2048)
        load_w(Wu, "Wu", Wu_d[e].rearrange("(kc p) n -> p kc n", p=128), 8, 0, 2048)
        WdV = Wd_d[e].rearrange("(kc p) n -> p kc n", p=128)
        load_w(Wd, "Wd", WdV, 8, 0, 1024)
        load_w(Wd, "Wd", WdV, 8, 8, 1024)

    load_weights(0)
    sckeys = []
    cmp2 = sb("cmp2_s", [128, 2, 128], BF16)
    th = sb("th_s", [128, 8], F32)
    P.memset("dve", th[:], 0.0, writes=["th"])
    for it in range(NITER):
        step = 2.0 ** (-(it + 1))
        P.ts("dve", th[:, 2:4], th[:, 0:2], step, None, ALU.add, reads=["th"], writes=["th"])
        for e in range(2):
            P.ts("dve", cmp2[:, e, :], A2[e][:], th[:, 2 + e:3 + e], None, ALU.is_ge, reads=["A%d" % e, "th"], writes=["cmp2"])
        P.op("dve", lambda v, o=th[:, 4:6], a=cmp2[:]: v.reduce_sum(o, a, AX.X), reads=["cmp2"], writes=["th"])
        pb, pk = nb()
        P.mm(pb[:, 0:2], ones[:], th[:, 4:6], reads=["ones", "th"], writes=[pk])
        P.ts("dve", th[:, 6:8], pb[:, 0:2], float(CAP), None, ALU.is_ge, reads=[pk], writes=["th"])
        P.stt("dve", th[:, 0:2], th[:, 6:8], step, th[:, 0:2], ALU.mult, ALU.add, reads=["th"], writes=["th"])
    for e in range(2):
        E = str(e)
        A = A2[e]
        P.ts("dve", mk[:], A[:], th[:, e:e + 1], None, ALU.is_ge, reads=["A" + E, "th"], writes=["mk"])
        P.copy("dve", mkb[:], mk[:], reads=["mk"], writes=["mkb"])
        pb, pk = nb()
        P.mm(pb[:, 0:128], mkb[:], identb[:], reads=["mkb", "identb"], writes=[pk])
        P.copy("dve", mTb[:], pb[:, 0:128], reads=[pk], writes=["mTb"])
        pc, pck = nb()
        P.mm(pc[:, 0:128], mTb[:], U[:], reads=["mTb", "U"], writes=[pck])
        P.copy("dve", posm[:], pc[:, 0:128], reads=[pck], writes=["posm"])
        pb, pk = nb()
        P.mm(pb[:, 0:1], SU[:], posm[:, 127:128], reads=["SU", "posm"], writes=[pk])
        P.copy("dve", sm[:, 7:8], pb[:, 0:1], reads=[pk], writes=["sm"])
        P.ts("dve", posm[:], posm[:], sm[:, 7:8], None, ALU.add, reads=["posm", "sm"], writes=["posm"])
        P.tt("dve", posm[:], posm[:], mk[:], ALU.mult, reads=["posm", "mk"], writes=["posm"])
        P.ts("dve", posm[:], posm[:], -1.0, None, ALU.add, reads=["posm"], writes=["posm"])
        P.stt("dve", posm[:], mk[:], -4097.0, posm[:], ALU.mult, ALU.add, reads=["mk", "posm"], writes=["posm"])
        P.ts("dve", posm[:], posm[:], 4097.0, None, ALU.add, reads=["posm"], writes=["posm"])
        for a in range(17):
            P.ts("dve", GE2[:, :, a], posm[:], 128.0 * a, None, ALU.is_ge, reads=["posm"], writes=["GE2"])
        P.tt("dve", OHH2[:], GE2[:, :, 0:16], GE2[:, :, 1:17], ALU.subtract, reads=["GE2"], writes=["OHH2"])
        P.copy("dve", lo_s[:], posm[:], reads=["posm"], writes=["lo"])
        for a in range(1, 17):
            P.stt("dve", lo_s[:], GE2[:, :, a], -128.0, lo_s[:], ALU.mult, ALU.add, reads=["GE2", "lo"], writes=["lo"])
        for a in range(16):
            P.tt("dve", T2[:, :, a], OHH2[:, :, a], jrow[:], ALU.mult, reads=["OHH2", "jrow"], writes=["T2"])
        P.ts("dve", T2[:, :, 16:32], OHH2[:], pcol[:, 0:1], None, ALU.mult, reads=["OHH2", "pcol"], writes=["T2"])
        pr, prk = psb[7], "psb7"
        for j in range(128):
            o = j % 2
            P.ts("dve", OHlo[o][:], iota[:, 0:128], lo_s[:, j:j + 1], None, ALU.is_equal, reads=["iota", "lo"], writes=["OHlo%d" % o])
            P.mm(pr[0:32, 0:128], T2[:, j, :], OHlo[o][:], start=(j == 0), stop=(j == 127), reads=["T2", "OHlo%d" % o], writes=[prk])
        P.copy("dve", R2[:], pr[0:32, 0:128], reads=[prk], writes=["R2"])
        P.mm(pr[:, 128:160], R2[:], ident[0:32, 0:32], reads=["R2", "ident"], writes=[prk])
        P.copy("dve", RT2[:], pr[:, 128:160], reads=[prk], writes=["RT2"])
        P.stt("dve", tokf[:], RT2[:, 16:32], 128.0, RT2[:, 0:16], ALU.mult, ALU.add, reads=["RT2"], writes=["tokf"])
        P.copy("dve", idx[:], tokf[:], reads=["tokf"], writes=["idx"])
        P.ts("dve", tokg[:], tokf[:], float(S_E * e), None, ALU.add, reads=["tokf"], writes=["tokg"])
        P.copy("dve", idxg[:], tokg[:], reads=["tokg"], writes=["idxg"])
        for blk in range(16):
            P.dma_fn(lambda g, o=gate[:, blk:blk + 1], off=idxg[:, blk:blk + 1], src=affall: g.indirect_dma_start(
                out=o, out_offset=None, in_=src, in_offset=bass.IndirectOffsetOnAxis(ap=off, axis=0)),
                reads=["idxg"], writes=["gate"])
        outs.append(P.dma(dbg[e, :, 0:16], tokf[:], reads=["tokf"], writes=["dbg%d" % e]))
        outs.append(P.dma(dbg[e, :, 16:32], gate[:], reads=["gate"], writes=["dbg%db" % e]))
        if stop_after == "idx":
            continue
        if e == 1:
            cast_mode[0] = "both"
            load_weights(1)
        for blk in range(16):
            j = blk % 2
            J = str(j)
            P.dma_fn(lambda g, o=xg[j][:], off=idx[:, blk:blk + 1]: g.indirect_dma_start(
                out=o, out_offset=None, in_=x1_d, in_offset=bass.IndirectOffsetOnAxis(ap=off, axis=0)),
                reads=["idx"], writes=["xg" + J])
            P.copy("act", xgb[:, 0:512], xg[j][:, 0:512], reads=["xg" + J], writes=["xgb"])
            P.copy("dve", xgb[:, 512:1024], xg[j][:, 512:1024], reads=["xg" + J], writes=["xgb"])
            for hf in range(2):
                pb, pk = nb()
                for kq in range(4):
                    kc = 4 * hf + kq
                    P.mm(pb[:, 128 * kq:128 * kq + 128], xgb[:, 128 * kc:128 * kc + 128], identb[:], reads=["xgb", "identb"], writes=[pk])
                for kq in range(4):
                    P.copy("act" if kq % 2 else "dve", xsT[:, 4 * hf + kq, 128 * blk:128 * blk + 128],
                           pb[:, 128 * kq:128 * kq + 128], reads=[pk], writes=["xsT"])
        for jb in range(4):
            js = slice(512 * jb, 512 * jb + 512)
            for fb in range(16):
                gi = fb % 2
                pg, kg = nb()
                pu, ku = nb()
                for kc in range(8):
                    P.mm(pg[:, :], Wg[:, kc, 128 * fb:128 * fb + 128], xsT[:, kc, js], start=(kc == 0), stop=(kc == 7),
                         reads=["Wg", "xsT"], writes=[kg])
                for kc in range(8):
                    P.mm(pu[:, :], Wu[:, kc, 128 * fb:128 * fb + 128], xsT[:, kc, js], start=(kc == 0), stop=(kc == 7),
                         reads=["Wu", "xsT"], writes=[ku])
                P.act(sg[gi][:], pg[:, :], AF.Silu, reads=[kg], writes=["sg%d" % gi])
                P.tt("dve", hT[:, fb, :], sg[gi][:], pu[:, :], ALU.mult, reads=["sg%d" % gi, ku], writes=["hT"])
            for sub in range(4):
                blk = 4 * jb + sub
                yi = blk % 2
                Y = str(yi)
                if e == 1:
                    P.dma_fn(lambda g, o=cur[yi][:], off=idx[:, blk:blk + 1]: g.indirect_dma_start(
                        out=o, out_offset=None, in_=dense, in_offset=bass.IndirectOffsetOnAxis(ap=off, axis=0)),
                        reads=["idx"] + sckeys + zkeys, writes=["xg" + Y])
                for hf in range(2):
                    pb, pk = nb()
                    for fb in range(16):
                        P.mm(pb[:, :], hT[:, fb, 128 * sub:128 * sub + 128], Wd[:, fb, 512 * hf:512 * hf + 512], start=(fb == 0), stop=(fb == 15),
                             reads=["hT", "Wd"], writes=[pk])
                    if e == 0:
                        P.ts("dve", ysb[yi][:, 512 * hf:512 * hf + 512], pb[:, :], gate[:, blk:blk + 1], None, ALU.mult,
                             reads=[pk, "gate"], writes=["ysb" + Y])
                    else:
                        P.stt("dve", ysb[yi][:, 512 * hf:512 * hf + 512], pb[:, :], gate[:, blk:blk + 1], cur[yi][:, 512 * hf:512 * hf + 512],
                              ALU.mult, ALU.add, reads=[pk, "gate", "xg" + Y], writes=["ysb" + Y])
                sk = "dsc%d_%d" % (e, blk)
                op = P.dma_fn(lambda g, i_=ysb[yi][:], off=idx[:, blk:blk + 1]: g.indirect_dma_start(
                    out=dense, out_offset=bass.IndirectOffsetOnAxis(ap=off, axis=0), in_=i_, in_offset=None),
                    reads=["idx", "ysb" + Y] + (zkeys if e == 0 else []), writes=[sk])
                outs.append(op)
                if e == 0:
                    sckeys.append(sk)
    P.emit(final_wait_ops=outs)
    return nc


def e_consts():
    p = np.arange(128)[:, None]
    j = np.arange(128)[None, :]
    return dict(iota=np.ascontiguousarray(np.broadcast_to(np.arange(512, dtype=np.float32)[None, :], (128, 512))),
                jrow=np.ascontiguousarray(np.broadcast_to(j, (128, 128)).astype(np.float32)), pcol=p.astype(np.float32),
                U=(p <= j).astype(np.float32), SU=(p < j).astype(np.float32), ident=np.eye(128, dtype=np.float32))


def e_inputs(core, aff_full, x1_full, w):
    es = slice(2 * core, 2 * core + 2)
    r = dict(affT=np.ascontiguousarray(aff_full[:, es].T.reshape(2, 128, 128)), x1=x1_full,
             Wg=np.ascontiguousarray(w["expert_w_gate"][es]), Wu=np.ascontiguousarray(w["expert_w_up"][es]),
             Wd=np.ascontiguousarray(w["expert_w_down"][es]))
    r.update(e_consts())
    return r


def build_C():
    nc = bass.Bass("TRN2", target_bir_lowering=False)
    din = lambda n, s, d=F32: nc.dram_tensor(n, s, d, kind="ExternalInput").ap()
    part = din("part", [8, NT, 1024])
    x1 = din("x1", [NT, 1024])
    lng_d = din("lng", [128, 1024])
    lnb_d = din("lnb", [128, 1024])
    out = nc.dram_tensor("x2", [NT, 1024], F32, kind="ExternalOutput").ap()
    P = Prog(nc)
    sb = lambda n, s, d: nc.alloc_sbuf_tensor(n, s, d)
    lng = sb("lng_s", [128, 1024], F32)
    lnb = sb("lnb_s", [128, 1024], F32)
    epsln = sb("epsln_s", [128, 1], F32)
    xt = [sb("xt%d" % i, [128, 1024], F32) for i in range(3)]
    pt = [[sb("pt%d_%d" % (i, k), [128, 1024], F32) for k in range(8)] for i in range(3)]
    bst = sb("bst", [128, 2, 6], F32)
    mv = sb("mv", [128, 2], F32)
    rstd = sb("rstd", [128, 1], F32)
    P.dma(lng[:], lng_d, writes=["lng"])
    P.dma(lnb[:], lnb_d, writes=["lnb"])
    P.memset("dve", epsln[:], 1e-5, writes=["epsln"])
    outs = []
    for t in range(NT // 128):
        i = t % 3
        I = str(i)
        rows = slice(128 * t, 128 * t + 128)
        P.dma(xt[i][:], x1[rows, :], writes=["xt" + I], q="act")
        for k in range(8):
            P.dma(pt[i][k][:], part[k, rows, :], writes=["pt%s_%d" % (I, k)], q=("sp" if k % 2 else "act"))
        P.tt("pool", pt[i][0][:], pt[i][0][:], pt[i][1][:], ALU.add, reads=["pt%s_0" % I, "pt%s_1" % I], writes=["pt%s_0" % I])
        P.tt("dve", pt[i][2][:], pt[i][2][:], pt[i][3][:], ALU.add, reads=["pt%s_2" % I, "pt%s_3" % I], writes=["pt%s_2" % I])
        P.tt("pool", pt[i][4][:], pt[i][4][:], pt[i][5][:], ALU.add, reads=["pt%s_4" % I, "pt%s_5" % I], writes=["pt%s_4" % I])
        P.tt("dve", pt[i][6][:], pt[i][6][:], pt[i][7][:], ALU.add, reads=["pt%s_6" % I, "pt%s_7" % I], writes=["pt%s_6" % I])
        P.tt("pool", pt[i][0][:], pt[i][0][:], pt[i][2][:], ALU.add, reads=["pt%s_0" % I, "pt%s_2" % I], writes=["pt%s_0" % I])
        P.tt("dve", pt[i][4][:], pt[i][4][:], pt[i][6][:], ALU.add, reads=["pt%s_4" % I, "pt%s_6" % I], writes=["pt%s_4" % I])
        P.tt("dve", pt[i][0][:], pt[i][0][:], pt[i][4][:], ALU.add, reads=["pt%s_0" % I, "pt%s_4" % I], writes=["pt%s_0" % I])
        P.stt("dve", xt[i][:], xt[i][:], ALPHA, pt[i][0][:], ALU.mult, ALU.add, reads=["xt" + I, "pt%s_0" % I], writes=["xt" + I])
        layer_norm(P, xt[i], "xt" + I, bst, mv, rstd, lng, lnb, "lng", "lnb", epsln)
        outs.append(P.dma(out[rows, :], xt[i][:], reads=["xt" + I], writes=["x2"]))
    P.emit(final_wait_ops=outs)
    return nc


def c_inputs(core, dense_list, x1_full, w):
    s = slice(core * NT, core * NT + NT)
    return dict(part=np.ascontiguousarray(np.stack([d[s] for d in dense_list])), x1=np.ascontiguousarray(x1_full[s]),
                lng=np.ascontiguousarray(np.broadcast_to(w["ln_ffn_g"][None, :], (128, 1024))),
                lnb=np.ascontiguousarray(np.broadcast_to(w["ln_ffn_b"][None, :], (128, 1024))))


_NC = {}


def _get(name, fn):
    if name not in _NC:
        _NC[name] = fn()
    return _NC[name]


def _run(nc, maps):
    res = run_bass_kernel_spmd(nc, maps, core_ids=list(range(8)))
    return res.results


def kernel(**inputs):
    inputs = {k: np.asarray(v) for k, v in inputs.items()}
    x = np.ascontiguousarray(inputs["x"][0], dtype=np.float32)
    depth = inputs["w_in"].shape[0]
    for l in range(depth):
        w = {k: np.ascontiguousarray(v[l]) for k, v in inputs.items() if k != "x"}
        xT = np.ascontiguousarray(x.T)
        xTr = np.ascontiguousarray(xT[:, ::-1])
        maps = []
        for c in range(8):
            m = g_inputs(xT, w["w_in"], w["gla_gate_w2"], w["gla_gate_b"], c % 4, 0)
            if c // 4 == 1:
                m = g_inputs(xTr, w["w_in"], w["gla_gate_w2"], w["gla_gate_b"], c % 4, 1, flipped=True)
            maps.append(m)
        res = _run(_get("G", build_G), maps)
        oTf = np.concatenate([res[h]["oT"] for h in range(4)], axis=0)
        oTb = np.concatenate([res[4 + h]["oT"][:, ::-1] for h in range(4)], axis=0)
        del maps, res
        maps = [a_inputs(c, x, xT, oTf, oTb, w) for c in range(8)]
        res = _run(_get("A", build_A), maps)
        x1 = np.concatenate([res[c]["x1"] for c in range(8)], axis=0)
        aff = np.concatenate([res[c]["aff"] for c in range(8)], axis=0)
        del maps, res
        maps = [e_inputs(c, aff, x1, w) for c in range(8)]
        res = _run(_get("E", build_E), maps)
        dense = [res[c]["dense"] for c in range(8)]
        del maps, res
        maps = [c_inputs(c, dense, x1, w) for c in range(8)]
        res = _run(_get("C", build_C), maps)
        x = np.concatenate([res[c]["x2"] for c in range(8)], axis=0)
        del maps, res, dense
    return x[None].astype(np.float32)
```

```python
import numpy as np
import concourse.bass as bass
import concourse.mybir as mybir
from concourse.bass_utils import run_bass_kernel_spmd

F32 = mybir.dt.float32
BF16 = mybir.dt.bfloat16
I32 = mybir.dt.int32
ALU = mybir.AluOpType
AF = mybir.ActivationFunctionType
AX = mybir.AxisListType

ENGS = ("pe", "act", "dve", "pool", "sp")
import os
SKIP_SAME_ENGINE = bool(int(os.environ.get('SKIP_SAME', '0')))
NDSEM = 48
NHW = 32


class Prog:
    def __init__(self, nc):
        self.nc = nc
        self.ops = {e: [] for e in ENGS}
        self.sem = {e: nc.alloc_semaphore("prog_" + e) for e in ("pe", "act", "dve", "pool")}
        self.dsems = [nc.alloc_semaphore("dsem%d" % i) for i in range(NDSEM)]
        self.dcount = [0] * NDSEM
        self.dnext = {"hw": 0, "sw": 0}
        self.last_w = {}
        self.readers = {}
        self.allops = []
        self.out_dma_ops = []

    def _add(self, eng, fn, reads, writes, dma=False):
        op = dict(eng=eng, fn=fn, deps=[], signal=False, dma=dma, idx=len(self.allops))
        writes = list(writes) + [k for k in reads if k.startswith("ps") and k not in writes]
        reads = [k for k in reads if not k.startswith("ps")]
        deps = []
        for k in reads:
            w = self.last_w.get(k)
            if w is not None:
                deps.append(w)
        for k in writes:
            w = self.last_w.get(k)
            if w is not None:
                deps.append(w)
            for r in self.readers.get(k, ()):
                deps.append(r)
        if dma:
            if eng == "pool":
                i = NHW + self.dnext["sw"]
                self.dnext["sw"] = (self.dnext["sw"] + 1) % (NDSEM - NHW)
            else:
                i = self.dnext["hw"]
                self.dnext["hw"] = (self.dnext["hw"] + 1) % NHW
            prev = getattr(self, "_dprev", {}).get(i)
            if prev is not None:
                deps.append(prev)
            if not hasattr(self, "_dprev"):
                self._dprev = {}
            self._dprev[i] = op
            self.dcount[i] += 1
            op["dsem"] = i
            op["dval"] = 16 * self.dcount[i]
            op["signal"] = True
        seen = set()
        for d in deps:
            if d is op or id(d) in seen:
                continue
            seen.add(id(d))
            if d["eng"] == "pe" and eng == "pe" and not d["dma"] and not dma:
                continue
            if SKIP_SAME_ENGINE and d["eng"] == eng and not d["dma"] and not dma:
                continue
            op["deps"].append(d)
            d["signal"] = True
        for k in reads:
            self.readers.setdefault(k, []).append(op)
        for k in writes:
            self.last_w[k] = op
            self.readers[k] = []
        self.ops[eng].append(op)
        self.allops.append(op)
        return op

    def op(self, eng, fn, reads=(), writes=()):
        return self._add(eng, fn, list(reads), list(writes))

    def dma(self, out, in_, reads=(), writes=(), q="sp", **kw):
        return self._add(q, lambda e: e.dma_start(out=out, in_=in_, **kw), list(reads), list(writes), dma=True)

    def dma_fn(self, fn, reads=(), writes=(), q="pool"):
        return self._add(q, fn, list(reads), list(writes), dma=True)

    def mm(self, out, lhsT, rhs, start=True, stop=True, reads=(), writes=()):
        return self.op("pe", lambda t: t.matmul(out, lhsT, rhs, start=start, stop=stop), reads, writes)

    def tr(self, out, in_, ident, reads=(), writes=()):
        return self.op("pe", lambda t: t.transpose(out, in_, ident), reads, writes)

    def act(self, out, in_, func, reads=(), writes=(), **kw):
        return self.op("act", lambda a: a.activation(out=out, in_=in_, func=func, **kw), reads, writes)

    def copy(self, eng, out, in_, reads=(), writes=()):
        if eng == "act":
            return self.op("act", lambda a: a.copy(out, in_), reads, writes)
        return self.op(eng, lambda v: v.tensor_copy(out, in_), reads, writes)

    def tt(self, eng, out, in0, in1, op, reads=(), writes=()):
        return self.op(eng, lambda v: v.tensor_tensor(out, in0, in1, op), reads, writes)

    def stt(self, eng, out, in0, scalar, in1, op0, op1, reads=(), writes=()):
        return self.op(eng, lambda v: v.scalar_tensor_tensor(out, in0, scalar, in1, op0, op1), reads, writes)

    def ts(self, eng, out, in0, s1, s2, op0, op1=None, reads=(), writes=(), **kw):
        if op1 is None:
            return self.op(eng, lambda v: v.tensor_scalar(out, in0, s1, s2, op0, **kw), reads, writes)
        return self.op(eng, lambda v: v.tensor_scalar(out, in0, s1, s2, op0, op1, **kw), reads, writes)

    def memset(self, eng, ap, val, reads=(), writes=()):
        return self.op(eng, lambda v: v.memset(ap, val), reads, writes)

    def emit(self, final_wait_ops=()):
        cnt = {e: 0 for e in self.sem}
        for op in self.allops:
            if op["dma"]:
                op["ticket"] = (self.dsems[op["dsem"]], op["dval"])
            elif op["signal"]:
                cnt[op["eng"]] += 1
                op["ticket"] = (self.sem[op["eng"]], cnt[op["eng"]])
        nc = self.nc
        final_tickets = [op["ticket"] for op in final_wait_ops]

        def run(engname, engobj):
            waited = {}
            for op in self.ops[engname]:
                need = {}
                for d in op["deps"]:
                    s, v = d["ticket"]
                    if need.get(s, (None, 0))[1] < v:
                        need[s] = (s, v)
                for s, v in need.values():
                    if waited.get(s, 0) >= v:
                        continue
                    engobj.wait_ge(s, v)
                    waited[s] = v
                ins = op["fn"](engobj)
                if op["signal"]:
                    s, v = op["ticket"]
                    ins.then_inc(s, 16 if op["dma"] else 1)
            if engname == "sp":
                for s, v in final_tickets:
                    if waited.get(s, 0) < v:
                        engobj.wait_ge(s, v)
                        waited[s] = v

        with nc.Block() as block:
            @block.sync
            def _(e):
                run("sp", e)

            @block.tensor
            def _(e):
                run("pe", e)

            @block.scalar
            def _(e):
                run("act", e)

            @block.vector
            def _(e):
                run("dve", e)

            @block.gpsimd
            def _(e):
                run("pool", e)


S_ = 16384
BLK = 1024
NBLK = S_ // BLK


def build_G(nblk=NBLK, ssalloc=None):
    nc = bass.Bass("TRN2", target_bir_lowering=False)
    SS = ssalloc or nblk * BLK
    xT = nc.dram_tensor("xT", [1024, SS], F32, kind="ExternalInput").ap()
    W = nc.dram_tensor("W", [1024, 272], F32, kind="ExternalInput").ap()
    W2 = nc.dram_tensor("W2", [32, 64], F32, kind="ExternalInput").ap()
    U16d = nc.dram_tensor("U16", [128, 128], F32, kind="ExternalInput").ap()
    UL16d = nc.dram_tensor("UL16", [128, 128], F32, kind="ExternalInput").ap()
    UMd = nc.dram_tensor("UM", [128, 128], F32, kind="ExternalInput").ap()
    oT = nc.dram_tensor("oT", [128, SS], F32, kind="ExternalOutput").ap()
    P = Prog(nc)
    sb = lambda n, s, d: nc.alloc_sbuf_tensor(n, s, d)
    Wb = sb("Wb", [128, 8, 272], BF16)
    W2s = sb("W2s", [32, 64], F32)
    U16 = sb("U16s", [128, 128], F32)
    UL16 = sb("UL16s", [128, 128], F32)
    UM = sb("UMs", [128, 128], F32)
    xb = [sb("xb%d" % i, [128, 8, BLK], BF16) for i in range(2)]
    xf = [sb("xf%d" % i, [128, 8, BLK], F32) for i in range(2)]
    qT = [sb("qT%d" % i, [64, BLK], F32) for i in range(2)]
    kT = [sb("kT%d" % i, [64, BLK], F32) for i in range(2)]
    lrT = [sb("lrT%d" % i, [32, BLK], F32) for i in range(2)]
    ob = [sb("ob%d" % i, [128, BLK], F32) for i in range(2)]
    e1 = [sb("e1_%d" % i, [128, 64], F32) for i in range(2)]
    sp = [sb("sp_%d" % i, [128, 64], F32) for i in range(2)]
    eq = [sb("eq_%d" % i, [64, 128], F32) for i in range(2)]
    ek = [sb("ek_%d" % i, [64, 128], F32) for i in range(2)]
    ekh = [sb("ekh_%d" % i, [128, 64], F32) for i in range(2)]
    QT = [sb("QT_%d" % i, [64, 128], BF16) for i in range(2)]
    KT = [sb("KT_%d" % i, [64, 128], BF16) for i in range(2)]
    Kh = [sb("Kh_%d" % i, [128, 64], BF16) for i in range(2)]
    V = [sb("V_%d" % i, [128, 128], BF16) for i in range(2)]
    ATm = [sb("ATm_%d" % i, [128, 128], BF16) for i in range(2)]
    Sf = sb("Sf", [64, 128], F32)
    Sb = sb("Sb", [64, 128], BF16)
    ps = [nc.alloc_psum_tensor("ps%d" % i, [128, 512], F32) for i in range(8)]

    P.dma(Wb[:], W.rearrange("(kc p) n -> p kc n", p=128), writes=["Wb"], q="pool")
    P.dma(W2s[:], W2, writes=["W2s"])
    P.dma(U16[:], U16d, writes=["U16"])
    P.dma(UL16[:], UL16d, writes=["UL16"])
    P.dma(UM[:], UMd, writes=["UM"])
    P.memset("dve", Sf[:], 0.0, writes=["Sf"])
    P.memset("dve", Sb[:], 0.0, writes=["Sb"])
    for i in range(2):
        P.memset("pool", lrT[i][:], 1.0, writes=["lrT%d" % i])
    xTv = xT.rearrange("(kc p) t -> p kc t", p=128)
    outs = []

    def load_blk(b):
        i = b % 2
        P.dma(xf[i][:], xTv[:, :, b * BLK:(b + 1) * BLK], writes=["xf%d" % i])
        for kc in range(8):
            P.copy("pool" if kc % 2 else "act", xb[i][:, kc, :], xf[i][:, kc, :], reads=["xf%d" % i], writes=["xb%d" % i])

    load_blk(0)
    for b in range(nblk):
        i = b % 2
        if b + 1 < nblk:
            load_blk(b + 1)
        xk = "xb%d" % i
        for sbk in range(BLK // 512):
            cs = slice(sbk * 512, (sbk + 1) * 512)
            for (name, c0, c1, pst, dst, m) in (("q", 0, 64, ps[0], qT[i], 64), ("k", 64, 128, ps[1], kT[i], 64),
                                                 ("lr", 256, 272, ps[2], lrT[i], 16)):
                for kc in range(8):
                    P.mm(pst[0:m, :], Wb[:, kc, c0:c1], xb[i][:, kc, cs], start=(kc == 0), stop=(kc == 7),
                         reads=["Wb", xk], writes=["ps_" + name])
                P.copy("act", dst[0:m, cs], pst[0:m, :], reads=["ps_" + name], writes=["%sT%d" % (name, i)])
        def chunk_vars(b, c):
            n = b * (BLK // 128) + c
            j = n % 2
            return n, j, slice(c * 128, (c + 1) * 128), ps[3 + j], ps[5 + j], "psA%d" % j, "psB%d" % j, str(j)

        def stage1(b, c):
            i = b % 2
            xk = "xb%d" % i
            n, j, cols, pA, pC, kA, kB, J = chunk_vars(b, c)
            for kc in range(8):
                P.mm(pA[:, 0:192], xb[i][:, kc, cols], Wb[:, kc, 64:256], start=(kc == 0), stop=(kc == 7),
                     reads=["Wb", xk], writes=[kA])
            P.mm(pA[:, 192:256], lrT[i][:, cols], W2s[:], reads=["lrT%d" % i, "W2s"], writes=[kA])
            P.act(e1[j][:], pA[:, 192:256], AF.Exp, scale=-1.0, reads=[kA], writes=["e1" + J])
            P.act(sp[j][:], e1[j][:], AF.Ln, bias=1.0, reads=["e1" + J], writes=["sp" + J])

        def stage1b(b, c):
            i = b % 2
            xk = "xb%d" % i
            n, j, cols, pA, pC, kA, kB, J = chunk_vars(b, c)
            P.mm(pA[:, 320:384], UL16[:], sp[j][:], reads=["UL16", "sp" + J], writes=[kA])
            P.mm(pA[0:64, 384:512], sp[j][:], U16[:], reads=["U16", "sp" + J], writes=[kA])
            P.act(eq[j][:], pA[0:64, 384:512], AF.Exp, scale=-1.0, reads=[kA], writes=["eq" + J])
            P.act(ek[j][:], pA[0:64, 384:512], AF.Exp, scale=1.0, reads=[kA], writes=["ek" + J])
            P.act(ekh[j][:], pA[:, 320:384], AF.Exp, scale=1.0, reads=[kA], writes=["ekh" + J])
            P.stt("dve", QT[j][:], qT[i][:, cols], 0.125, eq[j][:], ALU.mult, ALU.mult,
                  reads=["qT%d" % i, "eq" + J], writes=["QT" + J])
            P.tt("dve", KT[j][:], kT[i][:, cols], ek[j][:], ALU.mult, reads=["kT%d" % i, "ek" + J], writes=["KT" + J])
            P.tt("dve", Kh[j][:], pA[:, 0:64], ekh[j][:], ALU.mult, reads=[kA, "ekh" + J], writes=["Kh" + J])
            P.copy("dve", V[j][:], pA[:, 64:192], reads=[kA], writes=["V" + J])

        def stage2(b, c):
            i = b % 2
            n, j, cols, pA, pC, kA, kB, J = chunk_vars(b, c)
            P.mm(pC[:, 0:128], KT[j][:], QT[j][:], reads=["KT" + J, "QT" + J], writes=[kB])
            P.tt("dve", ATm[j][:], pC[:, 0:128], UM[:], ALU.mult, reads=[kB, "UM"], writes=["ATm" + J])
            P.mm(pC[0:64, 128:256], Kh[j][:], V[j][:], reads=["Kh" + J, "V" + J], writes=[kB])
            P.mm(pC[:, 256:384], V[j][:], ATm[j][:], start=True, stop=False, reads=["V" + J, "ATm" + J], writes=[kB])
            P.mm(pC[:, 256:384], Sb[:], QT[j][:], start=False, stop=True, reads=["Sb", "QT" + J], writes=[kB])
            P.copy("act", ob[i][:, cols], pC[:, 256:384], reads=[kB], writes=["ob%d" % i])
            P.stt("dve", Sf[:], Sf[:], eq[j][:, 127:128], pC[0:64, 128:256], ALU.mult, ALU.add,
                  reads=["Sf", "eq" + J, kB], writes=["Sf"])
            P.copy("dve", Sb[:], Sf[:], reads=["Sf"], writes=["Sb"])

        NCH = BLK // 128
        stage1(b, 0)
        stage1b(b, 0)
        for c in range(NCH):
            if c + 1 < NCH:
                stage1(b, c + 1)
            stage2(b, c)
            if c + 1 < NCH:
                stage1b(b, c + 1)
        outs.append(P.dma(oT[:, b * BLK:(b + 1) * BLK], ob[i][:], reads=["ob%d" % i], writes=["oT"]))
    P.emit(final_wait_ops=outs)
    return nc


def g_consts():
    s = np.arange(128)[:, None]
    t = np.arange(128)[None, :]
    U = (s <= t).astype(np.float32)
    return dict(U16=U / 16.0, UL16=-(s > t).astype(np.float32) / 16.0, UM=U)


def g_inputs(xT_full, w_in_l, w2_l, b_l, h, d, flipped=False):
    xT = xT_full[:, ::-1] if (d == 1 and not flipped) else xT_full
    cq = 2304 + 64 * h
    ck = 2560 + 64 * h
    cv = 2816 + 128 * h
    clr = 3840 + 16 * d
    W = np.concatenate([w_in_l[:, cq:cq + 64], w_in_l[:, ck:ck + 64], w_in_l[:, cv:cv + 128], w_in_l[:, clr:clr + 16]], axis=1)
    W2 = np.zeros((32, 64), np.float32)
    W2[0:16] = w2_l[d][:, 64 * h:64 * h + 64]
    W2[16] = b_l[d][64 * h:64 * h + 64]
    r = dict(xT=np.ascontiguousarray(xT), W=np.ascontiguousarray(W), W2=W2)
    r.update(g_consts())
    return r


NT = 2048
TB = 512
XW = TB + 256
TPB = TB // 128
NB_A = NT // TB
ALPHA = 4.0 ** 0.25
SLOPES = [2.0 ** (-(h + 1)) for h in range(8)]
C_AQ, C_AK, C_AV, C_CH, C_CB, C_CC, C_GR = 0, 512, 640, 768, 1280, 1792, 2304
NWA = 2816


def layer_norm(P, t, tk, bst, mv, rstd, lng, lnb, gk, bk, epsln):
    for hf in range(2):
        P.op("dve", lambda v, o=bst[:, hf, :], a=t[:, 512 * hf:512 * hf + 512]: v.bn_stats(o, a), reads=[tk], writes=["bst"])
    P.op("dve", lambda v, o=mv[:], a=bst[:]: v.bn_aggr(o, a), reads=["bst"], writes=["mv"])
    P.act(rstd[:], mv[:, 1:2], AF.Sqrt, bias=epsln[:, 0:1], reads=["mv", "epsln"], writes=["rstd"])
    P.op("dve", lambda v, o=rstd[:]: v.reciprocal(o, o), reads=["rstd"], writes=["rstd"])
    P.ts("dve", t[:], t[:], mv[:, 0:1], rstd[:, 0:1], ALU.subtract, ALU.mult, reads=[tk, "mv", "rstd"], writes=[tk])
    P.tt("dve", t[:], t[:], lng[:], ALU.mult, reads=[tk, gk], writes=[tk])
    P.tt("pool", t[:], t[:], lnb[:], ALU.add, reads=[tk, bk], writes=[tk])


def build_A(nblocks=NB_A):
    nc = bass.Bass("TRN2", target_bir_lowering=False)
    din = lambda n, s, d=F32: nc.dram_tensor(n, s, d, kind="ExternalInput").ap()
    xTh = din("xTh", [1024, NT + 256])
    xtok = din("xtok", [NT, 1024])
    oTf = din("oTf", [512, NT])
    oTb = din("oTb", [512, NT])
    Wa_d = din("Wa", [1024, NWA])
    Wmg_d = din("Wmg", [8, 1024, 384])
    Wba_d = din("Wba", [512, 1024])
    Wbc_d = din("Wbc", [512, 1024])
    Wbg_d = din("Wbg", [512, 1024])
    Wo_d = din("Wo", [1024, 1024])
    Wr_d = din("Wr", [1024, 16])
    maskb_d = din("maskb", [3, 128, 384])
    dist_d = din("dist", [128, 384])
    sink_d = din("sink", [128, 8])
    convw_d = din("convw", [128, 4, 3])
    normg_d = din("normg", [128, 4])
    lng_d = din("lng", [128, 1024])
    lnb_d = din("lnb", [128, 1024])
    ident_d = din("ident", [128, 128])
    x1_d = nc.dram_tensor("x1", [NT, 1024], F32, kind="ExternalOutput").ap()
    aff_d = nc.dram_tensor("aff", [NT, 16], F32, kind="ExternalOutput").ap()

    P = Prog(nc)
    sb = lambda n, s, d: nc.alloc_sbuf_tensor(n, s, d)
    stage2 = sb("stage", [128, 4096], F32)
    stage = stage2[:].rearrange("p (a b) -> p a b", b=512)
    Wa = sb("Wa_s", [128, 8, NWA], BF16)
    Wmg = [sb("Wmg%d" % i, [128, 8, 384], BF16) for i in range(2)]
    Wba = sb("Wba_s", [128, 4, 1024], BF16)
    Wbc = sb("Wbc_s", [128, 4, 1024], BF16)
    Wbg = sb("Wbg_s", [128, 4, 1024], BF16)
    Wo = sb("Wo_s", [128, 8, 1024], BF16)
    Wr = sb("Wr_s", [128, 8, 16], F32)
    maskb = sb("maskb_s", [128, 3, 384], F32)
    dist = sb("dist_s", [128, 384], F32)
    sink = sb("sink_s", [128, 8], F32)
    convw = sb("convw_s", [128, 4, 3], F32)
    normg = sb("normg_s", [128, 4], F32)
    lng = sb("lng_s", [128, 1024], F32)
    lnb = sb("lnb_s", [128, 1024], F32)
    ident = sb("ident_s", [128, 128], F32)
    ones = sb("ones_s", [128, 128], F32)
    epst = sb("eps_s", [128, 1], F32)
    xb = sb("xb_s", [128, 8, XW], BF16)
    kT = sb("kT_s", [64, 2, XW], BF16)
    V = sb("V_s", [128, (TPB + 2) * 128], BF16)
    qT = [sb("qT%d" % i, [64, TB], BF16) for i in range(2)]
    s1 = [sb("s1_%d" % i, [128, 384], F32) for i in range(2)]
    Pn = [sb("Pn_%d" % i, [128, 384], BF16) for i in range(2)]
    Dg = [sb("Dg_%d" % i, [128, 128], BF16) for i in range(2)]
    PT = [sb("PT_%d" % i, [128, 384], BF16) for i in range(2)]
    st = [sb("st_%d" % i, [128, 8], F32) for i in range(2)]
    yaT = sb("yaT", [128, 4, TB], BF16)
    ycT = sb("ycT", [128, 4, TB], BF16)
    ygT = sb("ygT", [128, 4, TB], BF16)
    mT = sb("mT", [128, 8, TB], BF16)
    cc_s = sb("cc_s", [128, TB + 2], F32)
    u_s = sb("u_s", [128, TB + 2], F32)
    t_s = sb("t_s", [128, TB], F32)
    of_s = [sb("of_0", [128, TB], F32)] * 2
    ob_s = [sb("ob_0", [128, TB], F32)] * 2
    sq_s, rinv, silr = cc_s, u_s, t_s
    g_s = [sb("g_%d" % i, [128, TB], F32) for i in range(2)]
    accm = [sb("accm0", [128, TB], F32)] * 4
    tmp = sb("tmp_s", [128, TB], F32)
    xt = [sb("xt_%d" % i, [128, 1024], F32) for i in range(2)]
    bst = sb("bst", [128, 2, 6], F32)
    mv = sb("mv", [128, 2], F32)
    rstd = sb("rstd", [128, 1], F32)
    x1T = sb("x1T", [128, 1024], F32)
    lg = sb("lg", [128, 16], F32)
    lst = sb("lst", [128, 4], F32)
    affs = [sb("affs%d" % i, [128, 16], F32) for i in range(2)]
    psb = [nc.alloc_psum_tensor("psb%d" % i, [128, 512], F32) for i in range(8)]
    bank = [0]
    ceng = [0]

    def nb():
        i = bank[0]
        bank[0] = (i + 1) % 8
        return psb[i], "psb%d" % i

    def cast_eng():
        ceng[0] += 1
        return "pool" if ceng[0] % 2 else "act"

    deferred = []

    def load_w(dst, dkey, src, nk, ncols, c_from=0, defer=False):
        sv = src.rearrange("(kc p) n -> p kc n", p=128)
        for c0 in range(c_from, ncols, 512):
            w = min(512, ncols - c0)

            def piece(c0=c0, w=w, pool_only=defer):
                P.dma(stage[:, 0:nk, 0:w], sv[:, :, c0:c0 + w], writes=["stage"])
                for kc in range(nk):
                    P.copy("pool" if pool_only else cast_eng(), dst[:, kc, c0:c0 + w], stage[:, kc, 0:w], reads=["stage"], writes=[dkey])
            if defer:
                deferred.append(piece)
            else:
                piece()

    P.dma(Wr[:], Wr_d.rearrange("(kc p) n -> p kc n", p=128), writes=["Wr"])
    P.dma(maskb[:], maskb_d.rearrange("m p k -> p m k"), writes=["maskb"])
    for (t, d, k) in ((dist, dist_d, "dist"), (sink, sink_d, "sink"), (convw, convw_d, "convw"), (normg, normg_d, "normg"),
                      (lng, lng_d, "lng"), (lnb, lnb_d, "lnb"), (ident, ident_d, "ident")):
        P.dma(t[:], d, writes=[k])
    P.memset("dve", ones[:], 1.0, writes=["ones"])
    P.memset("dve", epst[:], 1e-6, writes=["eps"])
    epsln = sb("epsln_s", [128, 1], F32)
    P.memset("dve", epsln[:], 1e-5, writes=["epsln"])
    load_w(Wa, "Wa", Wa_d, 8, 1024)
    load_w(Wa, "Wa2", Wa_d, 8, NWA, c_from=1024, defer=True)
    load_w(Wba, "Wba", Wba_d, 4, 1024, defer=True)
    load_w(Wbc, "Wbc", Wbc_d, 4, 1024, defer=True)
    load_w(Wbg, "Wbg", Wbg_d, 4, 1024, defer=True)
    load_w(Wo, "Wo", Wo_d, 8, 1024, defer=True)
    outs = []
    xTv = xTh.rearrange("(kc p) t -> p kc t", p=128)
    mgcnt = [0]
    acnt = [0]

    for b in range(nblocks):
        for ps_ in range(2):
            for kk in range(4):
                P.dma(stage2[:, XW * kk:XW * kk + XW], xTv[:, 4 * ps_ + kk, TB * b:TB * b + XW], writes=["stage"])
            for kk in range(4):
                P.copy(cast_eng(), xb[:, 4 * ps_ + kk, :], stage2[:, XW * kk:XW * kk + XW], reads=["stage"], writes=["xb"])
        XM = lambda kc: xb[:, kc, 128:128 + TB]
        for g in range(2):
            for (c0, n) in ((0, 512), (512, XW - 512)):
                pb, pk = nb()
                for kc in range(8):
                    P.mm(pb[0:64, 0:n], Wa[:, kc, C_AK + 64 * g:C_AK + 64 * g + 64], xb[:, kc, c0:c0 + n],
                         start=(kc == 0), stop=(kc == 7), reads=["Wa", "xb"], writes=[pk])
                P.copy("act", kT[:, g, c0:c0 + n], pb[0:64, 0:n], reads=[pk], writes=["kT"])
        for half in range(2):
            pb, pk = nb()
            for tl in range(3):
                ti = 3 * half + tl
                for kc in range(8):
                    P.mm(pb[:, 128 * tl:128 * tl + 128], xb[:, kc, 128 * ti:128 * ti + 128], Wa[:, kc, C_AV:C_AV + 128],
                         start=(kc == 0), stop=(kc == 7), reads=["Wa", "xb"], writes=[pk])
            P.copy("act", V[:, 384 * half:384 * half + 384], pb[:, 0:384], reads=[pk], writes=["V"])
        def qproj(h):
            qi = h % 2
            pb, pk = nb()
            for kc in range(8):
                P.mm(pb[0:64, 0:TB], Wa[:, kc, C_AQ + 64 * h:C_AQ + 64 * h + 64], XM(kc), start=(kc == 0), stop=(kc == 7),
                     reads=["Wa", "xb"], writes=[pk])
            P.copy("act", qT[qi][:], pb[0:64, 0:TB], reads=[pk], writes=["qT%d" % qi])

        def stage_a(h, i, j):
            g = h // 4
            qi = h % 2
            J = str(j)
            gt = TPB * b + i
            mi = 0 if gt == 0 else (2 if gt == NT // 128 - 1 else 1)
            pb, pk = nb()
            P.mm(pb[:, 0:384], qT[qi][:, 128 * i:128 * i + 128], kT[:, g, 128 * i:128 * i + 384], reads=["qT%d" % qi, "kT"], writes=[pk])
            P.stt("dve", s1[j][:], pb[:, 0:384], 0.125, maskb[:, mi, :], ALU.mult, ALU.add, reads=[pk, "maskb"], writes=["s1" + J])
            P.stt("dve", s1[j][:], dist[:], -SLOPES[h], s1[j][:], ALU.mult, ALU.add, reads=["dist", "s1" + J], writes=["s1" + J])
            P.op("dve", lambda v, o=st[j][:, 0:1], a=s1[j][:]: v.reduce_max(o, a, AX.X), reads=["s1" + J], writes=["st" + J])
            P.ts("dve", st[j][:, 1:2], st[j][:, 0:1], sink[:, h:h + 1], -1.0, ALU.max, ALU.mult, reads=["st" + J, "sink"], writes=["st" + J])
            P.act(Pn[j][:], s1[j][:], AF.Exp, bias=st[j][:, 1:2], accum_out=st[j][:, 2:3], reads=["s1" + J, "st" + J], writes=["Pn" + J, "st" + J])
            P.act(st[j][:, 3:4], sink[:, h:h + 1], AF.Exp, bias=st[j][:, 1:2], reads=["sink", "st" + J], writes=["st" + J])
            P.tt("dve", st[j][:, 4:5], st[j][:, 2:3], st[j][:, 3:4], ALU.add, reads=["st" + J], writes=["st" + J])
            P.op("dve", lambda v, o=st[j][:, 5:6], a=st[j][:, 4:5]: v.reciprocal(o, a), reads=["st" + J], writes=["st" + J])
            P.ts("dve", Dg[j][:], ident[:], st[j][:, 5:6], None, ALU.mult, reads=["ident", "st" + J], writes=["Dg" + J])

        def stage_b(h, i, j):
            g = h // 4
            J = str(j)
            pt_, ptk = nb()
            for k in range(3):
                P.mm(pt_[:, 128 * k:128 * k + 128], Pn[j][:, 128 * k:128 * k + 128], Dg[j][:], reads=["Pn" + J, "Dg" + J], writes=[ptk])
            P.copy("act", PT[j][:], pt_[:, 0:384], reads=[ptk], writes=["PT" + J])
            pb2, pk2 = nb()
            po = 64 * (h % 2)
            for k in range(3):
                P.mm(pb2[po:po + 64, 0:128], V[:, 128 * (i + k) + 64 * g:128 * (i + k) + 64 * g + 64], PT[j][:, 128 * k:128 * k + 128],
                     start=(k == 0), stop=(k == 2), reads=["V", "PT" + J], writes=[pk2])
            P.copy("act", yaT[po:po + 64, h // 2, 128 * i:128 * i + 128], pb2[po:po + 64, 0:128], reads=[pk2], writes=["yaT"])

        qproj(0)
        prev = None
        for h in range(8):
            for i in range(TPB):
                j = acnt[0] % 2
                acnt[0] += 1
                if i == 0 and h + 1 < 8:
                    qproj(h + 1)
                stage_a(h, i, j)
                if prev is not None:
                    stage_b(*prev)
                prev = (h, i, j)
                if i in (0, 2):
                    if deferred:
                        deferred.pop(0)()
        stage_b(*prev)
        while deferred:
            deferred.pop(0)()
        for fc in range(4):
            pch, kch = nb()
            pcc, kcc = nb()
            pcb, kcb = nb()
            ph, kph = nb()
            for (pbk, kk, c0) in ((pch, kch, C_CH), (pcc, kcc, C_CC)):
                for kc in range(8):
                    P.mm(pbk[:, 0:TB], Wa[:, kc, c0 + 128 * fc:c0 + 128 * fc + 128], xb[:, kc, 127:127 + TB],
                         start=(kc == 0), stop=(kc == 7), reads=["Wa", "Wa2", "xb"], writes=[kk])
            for (c0, o0) in ((C_CH, 0), (C_CC, 2)):
                for kc in range(8):
                    P.mm(ph[:, o0:o0 + 2], Wa[:, kc, c0 + 128 * fc:c0 + 128 * fc + 128], xb[:, kc, 127 + TB:129 + TB],
                         start=(kc == 0), stop=(kc == 7), reads=["Wa", "Wa2", "xb"], writes=[kph])
            for kc in range(8):
                P.mm(pcb[:, 0:TB], Wa[:, kc, C_CB + 128 * fc:C_CB + 128 * fc + 128], XM(kc),
                     start=(kc == 0), stop=(kc == 7), reads=["Wa", "Wa2", "xb"], writes=[kcb])
            P.copy("act", cc_s[:, 0:TB], pcc[:, 0:TB], reads=[kcc], writes=["cc_s"])
            P.copy("act", cc_s[:, TB:TB + 2], ph[:, 2:4], reads=[kph], writes=["cc_s"])
            P.tt("dve", u_s[:, 0:TB], pch[:, 0:TB], cc_s[:, 0:TB], ALU.mult, reads=[kch, "cc_s"], writes=["u_s"])
            P.tt("dve", u_s[:, TB:TB + 2], ph[:, 0:2], cc_s[:, TB:TB + 2], ALU.mult, reads=[kph, "cc_s"], writes=["u_s"])
            P.ts("dve", t_s[:], u_s[:, 0:TB], convw[:, fc, 0:1], None, ALU.mult, reads=["u_s", "convw"], writes=["t_s"])
            P.stt("dve", t_s[:], u_s[:, 1:TB + 1], convw[:, fc, 1:2], t_s[:], ALU.mult, ALU.add, reads=["u_s", "convw", "t_s"], writes=["t_s"])
            P.stt("dve", t_s[:], u_s[:, 2:TB + 2], convw[:, fc, 2:3], t_s[:], ALU.mult, ALU.add, reads=["u_s", "convw", "t_s"], writes=["t_s"])
            P.tt("dve", ycT[:, fc, :], pcb[:, 0:TB], t_s[:], ALU.mult, reads=[kcb, "t_s"], writes=["ycT"])
        for h in range(4):
            P.dma(of_s[0][:], oTf[128 * h:128 * h + 128, TB * b:TB * b + TB], writes=["of"])
            P.dma(ob_s[0][:], oTb[128 * h:128 * h + 128, TB * b:TB * b + TB], writes=["ob"])
            P.tt("pool", of_s[0][:], of_s[0][:], ob_s[0][:], ALU.add, reads=["of", "ob"], writes=["of"])
            P.act(sq_s[:, 0:TB], of_s[0][:], AF.Square, reads=["of"], writes=["cc_s"])
            pb, pk = nb()
            P.mm(pb[:, 0:TB], ones[:], sq_s[:, 0:TB], reads=["ones", "cc_s"], writes=[pk])
            P.act(rinv[:, 0:TB], pb[:, 0:TB], AF.Sqrt, scale=1.0 / 128.0, bias=epst[:, 0:1], reads=[pk, "eps"], writes=["u_s"])
            P.op("dve", lambda v, o=rinv[:, 0:TB]: v.reciprocal(o, o), reads=["u_s"], writes=["u_s"])
            pr, pkr = nb()
            for kc in range(8):
                P.mm(pr[:, 0:TB], Wa[:, kc, C_GR + 128 * h:C_GR + 128 * h + 128], XM(kc), start=(kc == 0), stop=(kc == 7),
                     reads=["Wa", "Wa2", "xb"], writes=[pkr])
            P.act(silr[:], pr[:, 0:TB], AF.Silu, reads=[pkr], writes=["t_s"])
            P.stt("dve", rinv[:, 0:TB], of_s[0][:], normg[:, h:h + 1], rinv[:, 0:TB], ALU.mult, ALU.mult, reads=["of", "normg", "u_s"], writes=["u_s"])
            P.tt("dve", ygT[:, h, :], rinv[:, 0:TB], silr[:], ALU.mult, reads=["u_s", "t_s"], writes=["ygT"])
        for c in range(8):
            wi = mgcnt[0] % 2
            mgcnt[0] += 1
            P.dma(stage[:, :, 0:384], Wmg_d[c].rearrange("(kc p) n -> p kc n", p=128), writes=["stage"])
            for kc in range(8):
                P.copy(cast_eng(), Wmg[wi][:, kc, :], stage[:, kc, 0:384], reads=["stage"], writes=["Wmg%d" % wi])
            for br in range(3):
                gi = br % 2
                pg, kg = nb()
                for kc in range(8):
                    P.mm(pg[:, 0:TB], Wmg[wi][:, kc, 128 * br:128 * br + 128], XM(kc), start=(kc == 0), stop=(kc == 7),
                         reads=["Wmg%d" % wi, "xb"], writes=[kg])
                P.act(g_s[gi][:], pg[:, 0:TB], AF.Sigmoid, reads=[kg], writes=["g_s%d" % gi])
                pbr, kbr = nb()
                Wb_, yT_, wk, yk = ((Wba, yaT, "Wba", "yaT"), (Wbc, ycT, "Wbc", "ycT"), (Wbg, ygT, "Wbg", "ygT"))[br]
                for fk in range(4):
                    P.mm(pbr[:, 0:TB], Wb_[:, fk, 128 * c:128 * c + 128], yT_[:, fk, :], start=(fk == 0), stop=(fk == 3),
                         reads=[wk, yk], writes=[kbr])
                if br == 0:
                    P.tt("dve", accm[0][:], g_s[gi][:], pbr[:, 0:TB], ALU.mult, reads=["g_s%d" % gi, kbr], writes=["accm"])
                else:
                    P.tt("dve", tmp[:], g_s[gi][:], pbr[:, 0:TB], ALU.mult, reads=["g_s%d" % gi, kbr], writes=["tmp"])
                    if br == 1:
                        P.tt("pool", accm[0][:], accm[0][:], tmp[:], ALU.add, reads=["accm", "tmp"], writes=["accm"])
                    else:
                        P.tt("pool", mT[:, c, :], accm[0][:], tmp[:], ALU.add, reads=["accm", "tmp"], writes=["mT"])
        for i in range(TPB):
            gt = TPB * b + i
            j = gt % 2
            J = str(j)
            P.dma(xt[j][:], xtok[128 * gt:128 * gt + 128, :], writes=["xt" + J])
            for hf in range(2):
                pb, pk = nb()
                for c in range(8):
                    P.mm(pb[:, :], mT[:, c, 128 * i:128 * i + 128], Wo[:, c, 512 * hf:512 * hf + 512], start=(c == 0), stop=(c == 7),
                         reads=["mT", "Wo"], writes=[pk])
                P.stt("dve", xt[j][:, 512 * hf:512 * hf + 512], xt[j][:, 512 * hf:512 * hf + 512], ALPHA, pb[:, :], ALU.mult, ALU.add,
                      reads=["xt" + J, pk], writes=["xt" + J])
            layer_norm(P, xt[j], "xt" + J, bst, mv, rstd, lng, lnb, "lng", "lnb", epsln)
            outs.append(P.dma(x1_d[128 * gt:128 * gt + 128, :], xt[j][:], reads=["xt" + J], writes=["x1_d"]))
            for hf in range(2):
                pb, pk = nb()
                for kq in range(4):
                    kc = 4 * hf + kq
                    P.mm(pb[:, 128 * kq:128 * kq + 128], xt[j][:, 128 * kc:128 * kc + 128], ident[:], reads=["xt" + J, "ident"], writes=[pk])
                P.copy("act", x1T[:, 512 * hf:512 * hf + 512], pb[:, :], reads=[pk], writes=["x1T"])
            pb, pk = nb()
            for kc in range(8):
                P.mm(pb[:, 0:16], x1T[:, 128 * kc:128 * kc + 128], Wr[:, kc, :], start=(kc == 0), stop=(kc == 7), reads=["x1T", "Wr"], writes=[pk])
            P.copy("act", lg[:], pb[:, 0:16], reads=[pk], writes=["lg"])
            P.op("dve", lambda v, o=lst[:, 0:1], a=lg[:]: v.reduce_max(o, a, AX.X), reads=["lg"], writes=["lst"])
            P.ts("dve", lst[:, 1:2], lst[:, 0:1], -1.0, None, ALU.mult, reads=["lst"], writes=["lst"])
            P.act(lg[:], lg[:], AF.Exp, bias=lst[:, 1:2], accum_out=lst[:, 2:3], reads=["lg", "lst"], writes=["lg", "lst"])
            P.op("dve", lambda v, o=lst[:, 3:4], a=lst[:, 2:3]: v.reciprocal(o, a), reads=["lst"], writes=["lst"])
            P.ts("dve", affs[j][:], lg[:], lst[:, 3:4], None, ALU.mult, reads=["lg", "lst"], writes=["affs" + J])
            outs.append(P.dma(aff_d[128 * gt:128 * gt + 128, :], affs[j][:], reads=["affs" + J], writes=["aff_d"]))
    P.emit(final_wait_ops=outs)
    return nc


def a_consts(core):
    q = np.arange(128)[:, None]
    ko = np.arange(384)[None, :] - 128
    dist = np.abs(q - ko).astype(np.float32)
    base = np.where(dist <= 128, 0.0, -30000.0).astype(np.float32)
    mk = np.stack([base, base, base]).copy()
    if core == 0:
        mk[0][:, 0:128] = -30000.0
    if core == 7:
        mk[2][:, 256:384] = -30000.0
    return dict(maskb=mk, dist=dist, ident=np.eye(128, dtype=np.float32))


def a_inputs(core, x_l, xT_l, oTf_full, oTb_full, w):
    S = x_l.shape[0]
    t0 = core * NT
    xTh = np.zeros((1024, NT + 256), np.float32)
    lo, hi = max(t0 - 128, 0), min(t0 + NT + 128, S)
    xTh[:, lo - (t0 - 128):hi - (t0 - 128)] = xT_l[:, lo:hi]
    w_in = w["w_in"]
    Wa = np.concatenate([w_in[:, 0:2304], w_in[:, 3328:3840]], axis=1)
    r = dict(xTh=xTh, xtok=np.ascontiguousarray(x_l[t0:t0 + NT]),
             oTf=np.ascontiguousarray(oTf_full[:, t0:t0 + NT]), oTb=np.ascontiguousarray(oTb_full[:, t0:t0 + NT]),
             Wa=np.ascontiguousarray(Wa), Wmg=np.ascontiguousarray(w_in[:, 3872:6944].reshape(1024, 3, 8, 128).transpose(2, 0, 1, 3).reshape(8, 1024, 384)),
             Wba=w["w_branch_attn"], Wbc=w["w_branch_conv"], Wbg=w["w_branch_gla"], Wo=w["w_out"], Wr=w["router_w"],
             sink=np.ascontiguousarray(np.broadcast_to(w["attn_sink"][None, :], (128, 8))),
             convw=np.ascontiguousarray(w["conv_w"].reshape(3, 4, 128).transpose(2, 1, 0)),
             normg=np.ascontiguousarray(w["gla_norm_g"].reshape(4, 128).T),
             lng=np.ascontiguousarray(np.broadcast_to(w["ln_mix_g"][None, :], (128, 1024))),
             lnb=np.ascontiguousarray(np.broadcast_to(w["ln_mix_b"][None, :], (128, 1024))))
    r.update(a_consts(core))
    return r


S_E = 16384
CAP = 2048
NITER = 36


def build_E(stop_after=None):
    nc = bass.Bass("TRN2", target_bir_lowering=False)
    din = lambda n, s, d=F32: nc.dram_tensor(n, s, d, kind="ExternalInput").ap()
    aff_d = din("affT", [2, 128, 128])
    x1_d = din("x1", [S_E, 1024])
    Wg_d = din("Wg", [2, 1024, 2048])
    Wu_d = din("Wu", [2, 1024, 2048])
    Wd_d = din("Wd", [2, 2048, 1024])
    iota_d = din("iota", [128, 512])
    jrow_d = din("jrow", [128, 128])
    pcol_d = din("pcol", [128, 1])
    U_d = din("U", [128, 128])
    SU_d = din("SU", [128, 128])
    ident_d = din("ident", [128, 128])
    dense = nc.dram_tensor("dense", [S_E, 1024], F32, kind="ExternalOutput").ap()
    dbg = nc.dram_tensor("dbg", [2, 128, 32], F32, kind="ExternalOutput").ap()

    affall = aff_d.rearrange("e p (j o) -> (e p j) o", o=1)
    P = Prog(nc)
    sb = lambda n, s, d: nc.alloc_sbuf_tensor(n, s, d)
    stage = sb("stage", [128, 8, 512], F32)
    Wg = sb("Wg_s", [128, 8, 2048], BF16)
    Wu = sb("Wu_s", [128, 8, 2048], BF16)
    Wd = sb("Wd_s", [128, 16, 1024], BF16)
    xsT = sb("xsT", [128, 8, CAP], BF16)
    hT = sb("hT", [128, 16, 512], BF16)
    iota = sb("iota_s", [128, 512], F32)
    U = sb("U_s", [128, 128], BF16)
    SU = sb("SU_s", [128, 128], F32)
    ident = sb("ident_s", [128, 128], F32)
    identb = sb("identb_s", [128, 128], BF16)
    ones = sb("ones_s", [128, 128], F32)
    A2 = [sb("A_s%d" % i, [128, 128], F32) for i in range(2)]
    sm = sb("sm_s", [128, 16], F32)
    mk = sb("mk_s", [128, 128], F32)
    mkb = sb("mkb_s", [128, 128], BF16)
    mTb = sb("mTb_s", [128, 128], BF16)
    posm = sb("posm_s", [128, 128], F32)
    GE2 = sb("GE2", [128, 128, 17], BF16)
    OHH2 = sb("OHH2", [128, 128, 16], BF16)
    T2 = sb("T2", [128, 128, 32], BF16)
    jrow = sb("jrow_s", [128, 128], F32)
    pcol = sb("pcol_s", [128, 1], F32)
    lo_s = sb("lo_s", [128, 128], F32)
    OHlo = [sb("OHlo%d" % i, [128, 128], BF16) for i in range(2)]
    R2 = sb("R2_s", [32, 128], F32)
    RT2 = sb("RT2_s", [128, 32], F32)
    tokf = sb("tokf", [128, 16], F32)
    tokg = sb("tokg", [128, 16], F32)
    idxg = sb("idxg_s", [128, 16], I32)
    idx = sb("idx_s", [128, 16], I32)
    gate = sb("gate_s", [128, 16], F32)
    xg = [sb("xg%d" % i, [128, 1024], F32) for i in range(2)]
    xgb = sb("xgb", [128, 1024], BF16)
    sg = [sb("sg%d" % i, [128, 512], F32) for i in range(2)]
    ysb = [sb("ysb%d" % i, [128, 1024], F32) for i in range(2)]
    cur = xg
    psb = [nc.alloc_psum_tensor("psb%d" % i, [128, 512], F32) for i in range(8)]
    bank = [0]
    ceng = [0]

    def nb():
        i = bank[0]
        bank[0] = (i + 1) % 7
        return psb[i], "psb%d" % i

    cast_mode = ["act"]

    def cast_eng():
        if cast_mode[0] == "act":
            return "act"
        ceng[0] += 1
        return "dve" if ceng[0] % 2 else "act"

    def load_w(dst, dkey, sv, nk, kc0, ncols):
        for c0 in range(0, ncols, 512):
            P.dma(stage[:, 0:nk, :], sv[:, kc0:kc0 + nk, c0:c0 + 512], writes=["stage"])
            for kc in range(nk):
                P.copy(cast_eng(), dst[:, kc0 + kc, c0:c0 + 512], stage[:, kc, :], reads=["stage"], writes=[dkey])

    P.dma(iota[:], iota_d, writes=["iota"])
    P.dma(jrow[:], jrow_d, writes=["jrow"])
    P.dma(pcol[:], pcol_d, writes=["pcol"])
    P.dma(SU[:], SU_d, writes=["SU"])
    P.dma(ident[:], ident_d, writes=["ident"])
    P.dma(mk[:], U_d, writes=["mk"])
    P.copy("dve", U[:], mk[:], reads=["mk"], writes=["U"])
    P.copy("dve", identb[:], ident[:], reads=["ident"], writes=["identb"])
    P.memset("dve", ones[:], 1.0, writes=["ones"])
    for e in range(2):
        P.dma(A2[e][:], aff_d[e], writes=["A%d" % e])
    for kc in range(8):
        P.memset("dve", stage[:, kc, :], 0.0, writes=["stage"])
    dv = dense.rearrange("(n p) d -> n p d", p=128)
    zkeys = []
    zops = []
    for z in range(32):
        zk = "dz%d" % z
        zops.append(P.dma(dense[512 * z:512 * z + 512, :].rearrange("(p r) d -> p r d", p=128), stage[:, :, :], reads=["stage"], writes=[zk]))
        zkeys.append(zk)
    outs = list(zops)

    def load_weights(e):
        load_w(Wg, "Wg", Wg_d[e].rearrange("(kc p) n -> p kc n", p=128), 8, 0, 2048)
        load_w(Wu, "Wu", Wu_d[e].rearrange("(kc p) n -> p kc n", p=128), 8, 0, 2048)
        WdV = Wd_d[e].rearrange("(kc p) n -> p kc n", p=128)
        load_w(Wd, "Wd", WdV, 8, 0, 1024)
        load_w(Wd, "Wd", WdV, 8, 8, 1024)

    load_weights(0)
    sckeys = []
    cmp2 = sb("cmp2_s", [128, 2, 128], BF16)
    th = sb("th_s", [128, 8], F32)
    P.memset("dve", th[:], 0.0, writes=["th"])
    for it in range(NITER):
        step = 2.0 ** (-(it + 1))
        P.ts("dve", th[:, 2:4], th[:, 0:2], step, None, ALU.add, reads=["th"], writes=["th"])
        for e in range(2):
            P.ts("dve", cmp2[:, e, :], A2[e][:], th[:, 2 + e:3 + e], None, ALU.is_ge, reads=["A%d" % e, "th"], writes=["cmp2"])
        P.op("dve", lambda v, o=th[:, 4:6], a=cmp2[:]: v.reduce_sum(o, a, AX.X), reads=["cmp2"], writes=["th"])
        pb, pk = nb()
        P.mm(pb[:, 0:2], ones[:], th[:, 4:6], reads=["ones", "th"], writes=[pk])
        P.ts("dve", th[:, 6:8], pb[:, 0:2], float(CAP), None, ALU.is_ge, reads=[pk], writes=["th"])
        P.stt("dve", th[:, 0:2], th[:, 6:8], step, th[:, 0:2], ALU.mult, ALU.add, reads=["th"], writes=["th"])
    for e in range(2):
        E = str(e)
        A = A2[e]
        P.ts("dve", mk[:], A[:], th[:, e:e + 1], None, ALU.is_ge, reads=["A" + E, "th"], writes=["mk"])
        P.copy("dve", mkb[:], mk[:], reads=["mk"], writes=["mkb"])
        pb, pk = nb()
        P.mm(pb[:, 0:128], mkb[:], identb[:], reads=["mkb", "identb"], writes=[pk])
        P.copy("dve", mTb[:], pb[:, 0:128], reads=[pk], writes=["mTb"])
        pc, pck = nb()
        P.mm(pc[:, 0:128], mTb[:], U[:], reads=["mTb", "U"], writes=[pck])
        P.copy("dve", posm[:], pc[:, 0:128], reads=[pck], writes=["posm"])
        pb, pk = nb()
        P.mm(pb[:, 0:1], SU[:], posm[:, 127:128], reads=["SU", "posm"], writes=[pk])
        P.copy("dve", sm[:, 7:8], pb[:, 0:1], reads=[pk], writes=["sm"])
        P.ts("dve", posm[:], posm[:], sm[:, 7:8], None, ALU.add, reads=["posm", "sm"], writes=["posm"])
        P.tt("dve", posm[:], posm[:], mk[:], ALU.mult, reads=["posm", "mk"], writes=["posm"])
        P.ts("dve", posm[:], posm[:], -1.0, None, ALU.add, reads=["posm"], writes=["posm"])
        P.stt("dve", posm[:], mk[:], -4097.0, posm[:], ALU.mult, ALU.add, reads=["mk", "posm"], writes=["posm"])
        P.ts("dve", posm[:], posm[:], 4097.0, None, ALU.add, reads=["posm"], writes=["posm"])
        for a in range(17):
            P.ts("dve", GE2[:, :, a], posm[:], 128.0 * a, None, ALU.is_ge, reads=["posm"], writes=["GE2"])
        P.tt("dve", OHH2[:], GE2[:, :, 0:16], GE2[:, :, 1:17], ALU.subtract, reads=["GE2"], writes=["OHH2"])
        P.copy("dve", lo_s[:], posm[:], reads=["posm"], writes=["lo"])
        for a in range(1, 17):
            P.stt("dve", lo_s[:], GE2[:, :, a], -128.0, lo_s[:], ALU.mult, ALU.add, reads=["GE2", "lo"], writes=["lo"])
        for a in range(16):
            P.tt("dve", T2[:, :, a], OHH2[:, :, a], jrow[:], ALU.mult, reads=["OHH2", "jrow"], writes=["T2"])
        P.ts("dve", T2[:, :, 16:32], OHH2[:], pcol[:, 0:1], None, ALU.mult, reads=["OHH2", "pcol"], writes=["T2"])
        pr, prk = psb[7], "psb7"
        for j in range(128):
            o = j % 2
            P.ts("dve", OHlo[o][:], iota[:, 0:128], lo_s[:, j:j + 1], None, ALU.is_equal, reads=["iota", "lo"], writes=["OHlo%d" % o])
            P.mm(pr[0:32, 0:128], T2[:, j, :], OHlo[o][:], start=(j == 0), stop=(j == 127), reads=["T2", "OHlo%d" % o], writes=[prk])
        P.copy("dve", R2[:], pr[0:32, 0:128], reads=[prk], writes=["R2"])
        P.mm(pr[:, 128:160], R2[:], ident[0:32, 0:32], reads=["R2", "ident"], writes=[prk])
        P.copy("dve", RT2[:], pr[:, 128:160], reads=[prk], writes=["RT2"])
        P.stt("dve", tokf[:], RT2[:, 16:32], 128.0, RT2[:, 0:16], ALU.mult, ALU.add, reads=["RT2"], writes=["tokf"])
        P.copy("dve", idx[:], tokf[:], reads=["tokf"], writes=["idx"])
        P.ts("dve", tokg[:], tokf[:], float(S_E * e), None, ALU.add, reads=["tokf"], writes=["tokg"])
        P.copy("dve", idxg[:], tokg[:], reads=["tokg"], writes=["idxg"])
        for blk in range(16):
            P.dma_fn(lambda g, o=gate[:, blk:blk + 1], off=idxg[:, blk:blk + 1], src=affall: g.indirect_dma_start(
                out=o, out_offset=None, in_=src, in_offset=bass.IndirectOffsetOnAxis(ap=off, axis=0)),
                reads=["idxg"], writes=["gate"])
        outs.append(P.dma(dbg[e, :, 0:16], tokf[:], reads=["tokf"], writes=["dbg%d" % e]))
        outs.append(P.dma(dbg[e, :, 16:32], gate[:], reads=["gate"], writes=["dbg%db" % e]))
        if stop_after == "idx":
            continue
        if e == 1:
            cast_mode[0] = "both"
            load_weights(1)
        for blk in range(16):
            j = blk % 2
            J = str(j)
            P.dma_fn(lambda g, o=xg[j][:], off=idx[:, blk:blk + 1]: g.indirect_dma_start(
                out=o, out_offset=None, in_=x1_d, in_offset=bass.IndirectOffsetOnAxis(ap=off, axis=0)),
                reads=["idx"], writes=["xg" + J])
            P.copy("act", xgb[:, 0:512], xg[j][:, 0:512], reads=["xg" + J], writes=["xgb"])
            P.copy("dve", xgb[:, 512:1024], xg[j][:, 512:1024], reads=["xg" + J], writes=["xgb"])
            for hf in range(2):
                pb, pk = nb()
                for kq in range(4):
                    kc = 4 * hf + kq
                    P.mm(pb[:, 128 * kq:128 * kq + 128], xgb[:, 128 * kc:128 * kc + 128], identb[:], reads=["xgb", "identb"], writes=[pk])
                for kq in range(4):
                    P.copy("act" if kq % 2 else "dve", xsT[:, 4 * hf + kq, 128 * blk:128 * blk + 128],
                           pb[:, 128 * kq:128 * kq + 128], reads=[pk], writes=["xsT"])
        for jb in range(4):
            js = slice(512 * jb, 512 * jb + 512)
            for fb in range(16):
                gi = fb % 2
                pg, kg = nb()
                pu, ku = nb()
                for kc in range(8):
                    P.mm(pg[:, :], Wg[:, kc, 128 * fb:128 * fb + 128], xsT[:, kc, js], start=(kc == 0), stop=(kc == 7),
                         reads=["Wg", "xsT"], writes=[kg])
                for kc in range(8):
                    P.mm(pu[:, :], Wu[:, kc, 128 * fb:128 * fb + 128], xsT[:, kc, js], start=(kc == 0), stop=(kc == 7),
                         reads=["Wu", "xsT"], writes=[ku])
                P.act(sg[gi][:], pg[:, :], AF.Silu, reads=[kg], writes=["sg%d" % gi])
                P.tt("dve", hT[:, fb, :], sg[gi][:], pu[:, :], ALU.mult, reads=["sg%d" % gi, ku], writes=["hT"])
            for sub in range(4):
                blk = 4 * jb + sub
                yi = blk % 2
                Y = str(yi)
                if e == 1:
                    P.dma_fn(lambda g, o=cur[yi][:], off=idx[:, blk:blk + 1]: g.indirect_dma_start(
                        out=o, out_offset=None, in_=dense, in_offset=bass.IndirectOffsetOnAxis(ap=off, axis=0)),
                        reads=["idx"] + sckeys + zkeys, writes=["xg" + Y])
                for hf in range(2):
                    pb, pk = nb()
                    for fb in range(16):
                        P.mm(pb[:, :], hT[:, fb, 128 * sub:128 * sub + 128], Wd[:, fb, 512 * hf:512 * hf + 512], start=(fb == 0), stop=(fb == 15),
                             reads=["hT", "Wd"], writes=[pk])
                    if e == 0:
                        P.ts("dve", ysb[yi][:, 512 * hf:512 * hf + 512], pb[:, :], gate[:, blk:blk + 1], None, ALU.mult,
                             reads=[pk, "gate"], writes=["ysb" + Y])
                    else:
                        P.stt("dve", ysb[yi][:, 512 * hf:512 * hf + 512], pb[:, :], gate[:, blk:blk + 1], cur[yi][:, 512 * hf:512 * hf + 512],
                              ALU.mult, ALU.add, reads=[pk, "gate", "xg" + Y], writes=["ysb" + Y])
                sk = "dsc%d_%d" % (e, blk)
                op = P.dma_fn(lambda g, i_=ysb[yi][:], off=idx[:, blk:blk + 1]: g.indirect_dma_start(
                    out=dense, out_offset=bass.IndirectOffsetOnAxis(ap=off, axis=0), in_=i_, in_offset=None),
                    reads=["idx", "ysb" + Y] + (zkeys if e == 0 else []), writes=[sk])
                outs.append(op)
                if e == 0:
                    sckeys.append(sk)
    P.emit(final_wait_ops=outs)
    return nc


def e_consts():
    p = np.arange(128)[:, None]
    j = np.arange(128)[None, :]
    return dict(iota=np.ascontiguousarray(np.broadcast_to(np.arange(512, dtype=np.float32)[None, :], (128, 512))),
                jrow=np.ascontiguousarray(np.broadcast_to(j, (128, 128)).astype(np.float32)), pcol=p.astype(np.float32),
                U=(p <= j).astype(np.float32), SU=(p < j).astype(np.float32), ident=np.eye(128, dtype=np.float32))


def e_inputs(core, aff_full, x1_full, w):
    es = slice(2 * core, 2 * core + 2)
    r = dict(affT=np.ascontiguousarray(aff_full[:, es].T.reshape(2, 128, 128)), x1=x1_full,
             Wg=np.ascontiguousarray(w["expert_w_gate"][es]), Wu=np.ascontiguousarray(w["expert_w_up"][es]),
             Wd=np.ascontiguousarray(w["expert_w_down"][es]))
    r.update(e_consts())
    return r


def build_C():
    nc = bass.Bass("TRN2", target_bir_lowering=False)
    din = lambda n, s, d=F32: nc.dram_tensor(n, s, d, kind="ExternalInput").ap()
    part = din("part", [8, NT, 1024])
    x1 = din("x1", [NT, 1024])
    lng_d = din("lng", [128, 1024])
    lnb_d = din("lnb", [128, 1024])
    out = nc.dram_tensor("x2", [NT, 1024], F32, kind="ExternalOutput").ap()
    P = Prog(nc)
    sb = lambda n, s, d: nc.alloc_sbuf_tensor(n, s, d)
    lng = sb("lng_s", [128, 1024], F32)
    lnb = sb("lnb_s", [128, 1024], F32)
    epsln = sb("epsln_s", [128, 1], F32)
    xt = [sb("xt%d" % i, [128, 1024], F32) for i in range(3)]
    pt = [[sb("pt%d_%d" % (i, k), [128, 1024], F32) for k in range(8)] for i in range(3)]
    bst = sb("bst", [128, 2, 6], F32)
    mv = sb("mv", [128, 2], F32)
    rstd = sb("rstd", [128, 1], F32)
    P.dma(lng[:], lng_d, writes=["lng"])
    P.dma(lnb[:], lnb_d, writes=["lnb"])
    P.memset("dve", epsln[:], 1e-5, writes=["epsln"])
    outs = []
    for t in range(NT // 128):
        i = t % 3
        I = str(i)
        rows = slice(128 * t, 128 * t + 128)
        P.dma(xt[i][:], x1[rows, :], writes=["xt" + I], q="act")
        for k in range(8):
            P.dma(pt[i][k][:], part[k, rows, :], writes=["pt%s_%d" % (I, k)], q=("sp" if k % 2 else "act"))
        P.tt("pool", pt[i][0][:], pt[i][0][:], pt[i][1][:], ALU.add, reads=["pt%s_0" % I, "pt%s_1" % I], writes=["pt%s_0" % I])
        P.tt("dve", pt[i][2][:], pt[i][2][:], pt[i][3][:], ALU.add, reads=["pt%s_2" % I, "pt%s_3" % I], writes=["pt%s_2" % I])
        P.tt("pool", pt[i][4][:], pt[i][4][:], pt[i][5][:], ALU.add, reads=["pt%s_4" % I, "pt%s_5" % I], writes=["pt%s_4" % I])
        P.tt("dve", pt[i][6][:], pt[i][6][:], pt[i][7][:], ALU.add, reads=["pt%s_6" % I, "pt%s_7" % I], writes=["pt%s_6" % I])
        P.tt("pool", pt[i][0][:], pt[i][0][:], pt[i][2][:], ALU.add, reads=["pt%s_0" % I, "pt%s_2" % I], writes=["pt%s_0" % I])
        P.tt("dve", pt[i][4][:], pt[i][4][:], pt[i][6][:], ALU.add, reads=["pt%s_4" % I, "pt%s_6" % I], writes=["pt%s_4" % I])
        P.tt("dve", pt[i][0][:], pt[i][0][:], pt[i][4][:], ALU.add, reads=["pt%s_0" % I, "pt%s_4" % I], writes=["pt%s_0" % I])
        P.stt("dve", xt[i][:], xt[i][:], ALPHA, pt[i][0][:], ALU.mult, ALU.add, reads=["xt" + I, "pt%s_0" % I], writes=["xt" + I])
        layer_norm(P, xt[i], "xt" + I, bst, mv, rstd, lng, lnb, "lng", "lnb", epsln)
        outs.append(P.dma(out[rows, :], xt[i][:], reads=["xt" + I], writes=["x2"]))
    P.emit(final_wait_ops=outs)
    return nc


def c_inputs(core, dense_list, x1_full, w):
    s = slice(core * NT, core * NT + NT)
    return dict(part=np.ascontiguousarray(np.stack([d[s] for d in dense_list])), x1=np.ascontiguousarray(x1_full[s]),
                lng=np.ascontiguousarray(np.broadcast_to(w["ln_ffn_g"][None, :], (128, 1024))),
                lnb=np.ascontiguousarray(np.broadcast_to(w["ln_ffn_b"][None, :], (128, 1024))))


_NC = {}


def _get(name, fn):
    if name not in _NC:
        _NC[name] = fn()
    return _NC[name]


def _run(nc, maps):
    res = run_bass_kernel_spmd(nc, maps, core_ids=list(range(8)))
    return res.results


def kernel(**inputs):
    inputs = {k: np.asarray(v) for k, v in inputs.items()}
    x = np.ascontiguousarray(inputs["x"][0], dtype=np.float32)
    depth = inputs["w_in"].shape[0]
    for l in range(depth):
        w = {k: np.ascontiguousarray(v[l]) for k, v in inputs.items() if k != "x"}
        xT = np.ascontiguousarray(x.T)
        xTr = np.ascontiguousarray(xT[:, ::-1])
        maps = []
        for c in range(8):
            m = g_inputs(xT, w["w_in"], w["gla_gate_w2"], w["gla_gate_b"], c % 4, 0)
            if c // 4 == 1:
                m = g_inputs(xTr, w["w_in"], w["gla_gate_w2"], w["gla_gate_b"], c % 4, 1, flipped=True)
            maps.append(m)
        res = _run(_get("G", build_G), maps)
        oTf = np.concatenate([res[h]["oT"] for h in range(4)], axis=0)
        oTb = np.concatenate([res[4 + h]["oT"][:, ::-1] for h in range(4)], axis=0)
        del maps, res
        maps = [a_inputs(c, x, xT, oTf, oTb, w) for c in range(8)]
        res = _run(_get("A", build_A), maps)
        x1 = np.concatenate([res[c]["x1"] for c in range(8)], axis=0)
        aff = np.concatenate([res[c]["aff"] for c in range(8)], axis=0)
        del maps, res
        maps = [e_inputs(c, aff, x1, w) for c in range(8)]
        res = _run(_get("E", build_E), maps)
        dense = [res[c]["dense"] for c in range(8)]
        del maps, res
        maps = [c_inputs(c, dense, x1, w) for c in range(8)]
        res = _run(_get("C", build_C), maps)
        x = np.concatenate([res[c]["x2"] for c in range(8)], axis=0)
        del maps, res, dense
    return x[None].astype(np.float32)
```
